# Optimizing a Trainium2 kernel written in Bass

```python
import jax, jax.numpy as jnp
from jax import lax
import numpy as np

D_MODEL = 2048
BATCH = 4
SEQ = 4096
DEPTH = 2

HEAD_DIM = 128
N_HEADS_SB = 8
N_HEADS_FOX = 8
WIDTH_SB = N_HEADS_SB * HEAD_DIM
WIDTH_FOX = N_HEADS_FOX * HEAD_DIM
Q_BLOCK = 128
PEER_HEADS = 8
PEER_KEYS = 128
PEER_TOPK = 16
PEER_HALF = 128
PEER_QDIM = 2 * PEER_HALF
N_EXPERTS = PEER_KEYS * PEER_KEYS
TOKEN_CHUNK = 128
PLE_DIM = 256
EPS = 1e-6

IN_SIZES = [WIDTH_SB, WIDTH_SB, WIDTH_SB,
            WIDTH_FOX, WIDTH_FOX, WIDTH_FOX,
            N_HEADS_FOX,
            D_MODEL, D_MODEL]
IN_COLS = sum(IN_SIZES)
IN_SPLITS = [int(c) for c in np.cumsum(IN_SIZES)[:-1]]

kernel_name = "hybrid_sb_fox_peer_ple"


def _rmsnorm(t, g):
    t32 = t.astype(jnp.float32)
    y = t32 * lax.rsqrt(jnp.mean(t32 * t32, axis=-1, keepdims=True) + EPS)
    return (y * g.astype(jnp.float32)).astype(t.dtype)


def _heads(t, n_heads):
    b, s, _ = t.shape
    return t.reshape(b, s, n_heads, HEAD_DIM).transpose(0, 2, 1, 3)


def _blocks(t):
    b, h, s = t.shape[:3]
    t = t.reshape(b, h, s // Q_BLOCK, Q_BLOCK, *t.shape[3:])
    return jnp.moveaxis(t, 2, 0)


def _unblocks(o):
    nb, b, h, q, hd = o.shape
    return o.transpose(1, 0, 3, 2, 4).reshape(b, nb * q, h * hd)


def stick_breaking_attention(q, k, v):
    s_len = q.shape[2]
    scale = HEAD_DIM ** -0.5
    k32 = k.astype(jnp.float32)
    v32 = v.astype(jnp.float32)
    spos = jnp.arange(s_len)

    def block(args):
        qb, start = args
        z = jnp.einsum('bhqd,bhkd->bhqk', qb.astype(jnp.float32), k32) * scale
        tpos = start + jnp.arange(Q_BLOCK)
        strict = spos[None, :] < tpos[:, None]
        log_1m = jnp.where(strict, jax.nn.log_sigmoid(-z), 0.0)
        after = lax.cumsum(log_1m, axis=3, reverse=True) - log_1m
        a = jnp.where(strict, jnp.exp(jax.nn.log_sigmoid(z) + after), 0.0)
        return jnp.einsum('bhqk,bhkd->bhqd', a, v32)

    starts = jnp.arange(s_len // Q_BLOCK, dtype=jnp.int32) * Q_BLOCK
    o = lax.map(block, (_blocks(q), starts))
    return _unblocks(o).astype(v.dtype)


def forgetting_attention(q, k, v, log_f):
    s_len = q.shape[2]
    scale = HEAD_DIM ** -0.5
    k32 = k.astype(jnp.float32)
    v32 = v.astype(jnp.float32)
    f_cum = jnp.cumsum(log_f, axis=2)
    spos = jnp.arange(s_len)

    def block(args):
        qb, fq, start = args
        z = (jnp.einsum('bhqd,bhkd->bhqk', qb.astype(jnp.float32), k32) * scale
             + fq[..., :, None] - f_cum[:, :, None, :])
        tpos = start + jnp.arange(Q_BLOCK)
        causal = spos[None, :] <= tpos[:, None]
        w = jax.nn.softmax(jnp.where(causal, z, -jnp.inf), axis=-1)
        return jnp.einsum('bhqk,bhkd->bhqd', w, v32)

    starts = jnp.arange(s_len // Q_BLOCK, dtype=jnp.int32) * Q_BLOCK
    o = lax.map(block, (_blocks(q), _blocks(f_cum), starts))
    return _unblocks(o).astype(v.dtype)


def peer_ffn(xn, w_query, sub_keys, expert_u, expert_v):
    b, s, d = xn.shape
    q = (xn @ w_query).astype(jnp.float32).reshape(b, s, PEER_HEADS, 2, PEER_HALF)
    scores = jnp.einsum('bshpc,hpkc->bshpk', q, sub_keys.astype(jnp.float32))
    s1, i1 = lax.top_k(scores[..., 0, :], PEER_TOPK)
    s2, i2 = lax.top_k(scores[..., 1, :], PEER_TOPK)
    cand_s = (s1[..., :, None] + s2[..., None, :]).reshape(b, s, PEER_HEADS, PEER_TOPK * PEER_TOPK)
    cand_i = (i1[..., :, None] * PEER_KEYS + i2[..., None, :]).reshape(b, s, PEER_HEADS, PEER_TOPK * PEER_TOPK)
    top_s, pos = lax.top_k(cand_s, PEER_TOPK)
    idx = jnp.take_along_axis(cand_i, pos, axis=-1)
    gate = jax.nn.softmax(top_s, axis=-1).astype(xn.dtype)

    n_chunks = (b * s) // TOKEN_CHUNK
    xc = xn.reshape(n_chunks, TOKEN_CHUNK, d)
    ic = idx.reshape(n_chunks, TOKEN_CHUNK, PEER_HEADS, PEER_TOPK)
    gc = gate.reshape(n_chunks, TOKEN_CHUNK, PEER_HEADS, PEER_TOPK)

    def chunk(args):
        xb, ib, gb = args
        hidden = jax.nn.gelu(jnp.einsum('chkd,cd->chk', expert_u[ib], xb), approximate=False)
        return jnp.einsum('chk,chkd->cd', gb * hidden, expert_v[ib])

    out = lax.map(chunk, (xc, ic, gc))
    return out.reshape(b, s, d)


def setup_inputs(seed: int = 0) -> dict:
    key = jax.random.key(seed)
    ks = jax.random.split(key, 20)

    def nrm(k, shape, scale):
        return jax.random.normal(k, shape, jnp.float32) * scale

    return {
        "x": nrm(ks[0], (BATCH, SEQ, D_MODEL), 1.0),
        "p": nrm(ks[1], (DEPTH, BATCH, SEQ, PLE_DIM), 1.0),
        "norm_mix_g": 1.0 + nrm(ks[2], (DEPTH, D_MODEL), 0.02),
        "w_in": nrm(ks[3], (DEPTH, D_MODEL, IN_COLS), D_MODEL ** -0.5),
        "b_forget": jnp.linspace(1.0, 5.0, N_HEADS_FOX, dtype=jnp.float32)[None, :]
                    + nrm(ks[4], (DEPTH, N_HEADS_FOX), 0.1),
        "w_branch_sb": nrm(ks[5], (DEPTH, WIDTH_SB, D_MODEL), WIDTH_SB ** -0.5),
        "w_branch_fox": nrm(ks[6], (DEPTH, WIDTH_FOX, D_MODEL), WIDTH_FOX ** -0.5),
        "w_out": nrm(ks[7], (DEPTH, D_MODEL, D_MODEL), D_MODEL ** -0.5),
        "norm_ffn_g": 1.0 + nrm(ks[8], (DEPTH, D_MODEL), 0.02),
        "w_query": nrm(ks[9], (DEPTH, D_MODEL, PEER_HEADS * PEER_QDIM), D_MODEL ** -0.5),
        "sub_keys": nrm(ks[10], (DEPTH, PEER_HEADS, 2, PEER_KEYS, PEER_HALF), PEER_HALF ** -0.5),
        "expert_u": nrm(ks[11], (DEPTH, N_EXPERTS, D_MODEL), D_MODEL ** -0.5),
        "expert_v": nrm(ks[12], (DEPTH, N_EXPERTS, D_MODEL), (PEER_HEADS * PEER_TOPK) ** -0.5),
        "norm_ple_g": 1.0 + nrm(ks[13], (DEPTH, D_MODEL), 0.02),
        "w_ple": nrm(ks[14], (DEPTH, PLE_DIM, D_MODEL), PLE_DIM ** -0.5),
        "w_ple_gate": nrm(ks[15], (DEPTH, D_MODEL, D_MODEL), D_MODEL ** -0.5),
        "final_norm_g": 1.0 + nrm(ks[16], (D_MODEL,), 0.02),
    }


def reference(x, p, norm_mix_g, w_in, b_forget, w_branch_sb, w_branch_fox, w_out,
              norm_ffn_g, w_query, sub_keys, expert_u, expert_v,
              norm_ple_g, w_ple, w_ple_gate, final_norm_g):
    h = x
    for i in range(DEPTH):
        xn = _rmsnorm(h, norm_mix_g[i])
        proj = xn @ w_in[i]
        q_sb, k_sb, v_sb, q_fx, k_fx, v_fx, f_logit, g_sb, g_fx = jnp.split(proj, IN_SPLITS, axis=-1)

        o_sb = stick_breaking_attention(_heads(q_sb, N_HEADS_SB), _heads(k_sb, N_HEADS_SB),
                                        _heads(v_sb, N_HEADS_SB))
        log_f = jax.nn.log_sigmoid(f_logit.astype(jnp.float32)
                                   + b_forget[i].astype(jnp.float32)).transpose(0, 2, 1)
        o_fx = forgetting_attention(_heads(q_fx, N_HEADS_FOX), _heads(k_fx, N_HEADS_FOX),
                                    _heads(v_fx, N_HEADS_FOX), log_f)

        merged = (jax.nn.sigmoid(g_sb) * (o_sb @ w_branch_sb[i])
                  + jax.nn.sigmoid(g_fx) * (o_fx @ w_branch_fox[i]))
        h = h + merged @ w_out[i]

        h = h + peer_ffn(_rmsnorm(h, norm_ffn_g[i]), w_query[i], sub_keys[i], expert_u[i], expert_v[i])

        ple_gate = jax.nn.sigmoid(_rmsnorm(h, norm_ple_g[i]) @ w_ple_gate[i])
        h = h + ple_gate * (p[i] @ w_ple[i])
    return _rmsnorm(h, final_norm_g)
```

```python
from contextlib import ExitStack
import numpy as np
import ml_dtypes
import concourse.bass as bass
import concourse.mybir as mybir
from concourse.bass_utils import run_bass_kernel_spmd

F32 = mybir.dt.float32
BF16 = mybir.dt.bfloat16
AF = mybir.ActivationFunctionType
ALU = mybir.AluOpType

D = 2048
NCH = 16
SEQ = 4096
BATCH = 4
HD = 128
NCORES = 8
NT = 2048
IN_COLS = 10248
SCALE = HD ** -0.5
EPS = 1e-6
NEXP = 16384


class Sync:
    def __init__(self, nc, stack):
        self.nc = nc
        self.stack = stack
        self.eng = {'pe': nc.tensor, 'dve': nc.vector, 'act': nc.scalar, 'pool': nc.gpsimd, 'sp': nc.sync}
        self.prod = {}
        self.waited = {e: {} for e in self.eng}
        self.lastw = {}
        self.readers = {}
        self.nsem = 0
        self.sems = {}

    def _newsem(self):
        self.nsem += 1
        s = self.stack.enter_context(self.nc.semaphore(f"sy{self.nsem}"))
        self.sems[self.nsem] = s
        return self.nsem

    def _inc(self, pname, n):
        p = self.prod.get(pname)
        if p is None or p[1] + n > 30000:
            p = [self._newsem(), 0]
            self.prod[pname] = p
        p[1] += n
        return p[0], p[1]

    def _deps(self, r, w):
        deps = []
        for k in r:
            if k in self.lastw:
                deps.append(self.lastw[k])
        for k in w:
            if k in self.lastw:
                deps.append(self.lastw[k])
            rd = self.readers.get(k)
            if rd:
                deps.extend(rd.values())
        return deps

    def _wait(self, e, deps):
        best = {}
        for (sid, val, pn) in deps:
            if pn == 'pe' and e == 'pe':
                continue
            if self.waited[e].get(sid, 0) >= val:
                continue
            if sid not in best or best[sid] < val:
                best[sid] = val
        for sid, val in best.items():
            self.eng[e].wait_ge(self.sems[sid], val)
            self.waited[e][sid] = val

    def _commit(self, tok, r, w):
        for k in w:
            self.lastw[k] = tok
            self.readers[k] = {}
        for k in r:
            d = self.readers.setdefault(k, {})
            d[(tok[0], tok[2])] = tok

    def op(self, e, fn, r=(), w=()):
        self._wait(e, self._deps(r, w))
        ins = fn(self.eng[e])
        sid, val = self._inc(e, 1)
        ins.then_inc(self.sems[sid], 1)
        self._commit((sid, val, e), r, w)

    def dma(self, q, out, in_, r=(), w=(), stream=None, **kw):
        pname = 'dma:' + stream
        deps = self._deps(r, w)
        p = self.prod.get(pname)
        if p is not None:
            deps.append((p[0], p[1], pname))
        self._wait(q, deps)
        ins = self.eng[q].dma_start(out=out, in_=in_, **kw)
        sid, val = self._inc(pname, 16)
        ins.then_inc(self.sems[sid], 16)
        self._commit((sid, val, pname), r, w)

    def barrier(self):
        for e in ('sp', 'pool', 'act', 'dve', 'pe'):
            deps = [(p[0], p[1], pn) for pn, p in self.prod.items() if pn != e]
            self._wait(e, deps)

    def finish(self):
        for e in ('sp', 'pool', 'act', 'dve', 'pe'):
            deps = [(p[0], p[1], pn) for pn, p in self.prod.items() if pn != e]
            self._wait(e, deps)


class Ctx:
    def __init__(self, nc, stack):
        self.nc = nc
        self.S = Sync(nc, stack)
        self.stack = stack
        self.banks = [stack.enter_context(nc.psum_tensor(f"bank{i}", [128, 512], F32)) for i in range(8)]
        self.bi = 0
        self.uid = 0

    def bank(self):
        b = self.bi
        self.bi = (self.bi + 1) % 8
        return b

    def sb(self, stack, name, shape, dt):
        self.uid += 1
        return stack.enter_context(self.nc.sbuf_tensor(f"{name}_{self.uid}", shape, dt))


def load_consts(cx, stack, cst):
    S = cx.S
    c32 = cx.sb(stack, "c32", [128, 6 * 128], F32)
    c16 = cx.sb(stack, "c16", [128, 6 * 128], BF16)
    S.dma('sp', c32[:], cst, w=['c32'], stream='c32')
    S.op('dve', lambda e: e.tensor_copy(out=c16[:], in_=c32[:]), r=['c32'], w=['c16'])
    return c32, c16


def make_consts():
    p = np.arange(128)[:, None]
    i = np.arange(128)[None, :]
    c = np.concatenate([
        (p == i), (p < i), (p <= i), -1.0 * (p > i), -1.0 * (p <= i), np.ones((128, 128))
    ], axis=1).astype(np.float32)
    return np.ascontiguousarray(c)


def rmsnorm_tile(cx, hT_t, hkey, gcol, gi, out_ap_fn, okey, n, c32, tmp):
    S = cx.S
    sq, rstd, epsc = tmp
    b = cx.bank()
    bk = cx.banks[b]
    for c in range(NCH):
        s = sq[c % 2]
        S.op('act', lambda e, s=s, c=c: e.activation(out=s[:, :n], in_=hT_t[:, c, :n], func=AF.Square),
             r=[hkey], w=[f'sq{c % 2}'])
        S.op('pe', lambda e, s=s, c=c: e.matmul(bk[:, :n], lhsT=c32[:, 640:768], rhs=s[:, :n],
                                                 start=(c == 0), stop=(c == NCH - 1)),
             r=[f'sq{c % 2}'], w=[f'bank{b}'])
    S.op('act', lambda e: e.activation(out=rstd[:, :n], in_=bk[:, :n], func=AF.Ln, bias=epsc[:, 0:1], scale=1.0 / D),
         r=[f'bank{b}'], w=['rstd'])
    S.op('act', lambda e: e.activation(out=rstd[:, :n], in_=rstd[:, :n], func=AF.Exp, scale=-0.5),
         r=['rstd'], w=['rstd'])
    for c in range(NCH):
        S.op('dve', lambda e, c=c: e.scalar_tensor_tensor(out=out_ap_fn(c), in0=hT_t[:, c, :n],
                                                          scalar=gcol[:, gi * NCH + c:gi * NCH + c + 1],
                                                          in1=rstd[:, :n], op0=ALU.mult, op1=ALU.mult),
             r=[hkey, 'rstd', 'gcol'], w=[okey])


def norm_tmp(cx, stack):
    sq = [cx.sb(stack, "sq", [128, 512], F32) for _ in range(2)]
    rstd = cx.sb(stack, "rstd", [128, 512], F32)
    epsc = cx.sb(stack, "epsc", [128, 1], F32)
    cx.S.op('dve', lambda e: e.memset(epsc[:], EPS), w=['epsc'])
    return sq, rstd, epsc


def phase_proj(cx, hT, gpc, w_in, bfg, cst, qk_out, v_out, logf_out, sg_out, nt=NT):
    S = cx.S
    nc = cx.nc
    ntile = nt // 512
    with ExitStack() as st:
        c32, c16 = load_consts(cx, st, cst)
        xnT = cx.sb(st, "xnT", [128, NCH, nt], BF16)
        gcol = cx.sb(st, "gcol", [128, NCH], F32)
        bcol = cx.sb(st, "bcol", [8, 1], F32)
        S.dma('sp', gcol[:], gpc, w=['gcol'], stream='gcol')
        S.dma('sp', bcol[:], bfg, w=['bcol'], stream='bcol')
        tmp = norm_tmp(cx, st)
        hview = hT.rearrange("(c p) t -> p c t", p=128)
        with ExitStack() as st2:
            hts = [cx.sb(st2, "ht", [128, NCH, 512], F32) for _ in range(2)]
            for tt in range(ntile):
                ht = hts[tt % 2]
                hk = f'ht{tt % 2}'
                S.dma('sp', ht[:], hview[:, :, tt * 512:(tt + 1) * 512], w=[hk], stream=hk)
                rmsnorm_tile(cx, ht, hk, gcol, 0, lambda c, tt=tt: xnT[:, c, tt * 512:(tt + 1) * 512],
                             'xnT', 512, c32, tmp)
        S.barrier()
        wview = w_in.rearrange("(c p) n -> p c n", p=128)
        wsl = [cx.sb(st, "wsl", [128, NCH, 512], BF16) for _ in range(2)]
        stg = [cx.sb(st, "stg", [128, 512], BF16) for _ in range(4)]
        stgf = cx.sb(st, "stgf", [8, 512], F32)
        wf = cx.sb(st, "wf", [128, NCH, 8], BF16)
        slabs = []
        for kind, base in enumerate([0, 1024, 3072, 4096]):
            for s2 in range(2):
                slabs.append((base + 512 * s2, 'qk', kind * 8 + 4 * s2))
        for kind, base in enumerate([2048, 5120]):
            for s2 in range(2):
                slabs.append((base + 512 * s2, 'v', (kind, s2)))
        for kind, base in enumerate([6152, 8200]):
            for s4 in range(4):
                slabs.append((base + 512 * s4, 'g', (kind, s4)))
        si = 0
        ev = 0

        def load_slab(i):
            col0 = slabs[i][0]
            S.dma('pool', wsl[i % 2][:], wview[:, :, col0:col0 + 512], w=[f'wsl{i % 2}'], stream=f'wsl{i % 2}')

        load_slab(0)
        S.dma('pool', wf[:], wview[:, :, 6144:6152], w=['wf'], stream='wf')
        for i, (col0, mode, meta) in enumerate(slabs):
            if i + 1 < len(slabs):
                load_slab(i + 1)
            w = wsl[i % 2]
            wk = f'wsl{i % 2}'
            if mode in ('qk', 'g'):
                for tt in range(ntile):
                    for cc in range(4):
                        b = cx.bank()
                        bk = cx.banks[b]
                        for dc in range(NCH):
                            S.op('pe', lambda e, dc=dc, cc=cc, tt=tt, bk=bk: e.matmul(
                                bk[:, :], lhsT=w[:, dc, cc * 128:(cc + 1) * 128],
                                rhs=xnT[:, dc, tt * 512:(tt + 1) * 512], start=(dc == 0), stop=(dc == NCH - 1)),
                                r=[wk, 'xnT'], w=[f'bank{b}'])
                        sg = stg[ev % 4]
                        sk = f'stg{ev % 4}'
                        if mode == 'qk':
                            if ev % 2 == 0:
                                S.op('act', lambda e, sg=sg, bk=bk: e.activation(out=sg[:], in_=bk[:], func=AF.Copy),
                                     r=[f'bank{b}'], w=[sk])
                            else:
                                S.op('dve', lambda e, sg=sg, bk=bk: e.tensor_copy(out=sg[:], in_=bk[:]),
                                     r=[f'bank{b}'], w=[sk])
                            dst = qk_out[meta + cc, :, tt * 512:(tt + 1) * 512]
                        else:
                            S.op('act', lambda e, sg=sg, bk=bk: e.activation(out=sg[:], in_=bk[:], func=AF.Sigmoid),
                                 r=[f'bank{b}'], w=[sk])
                            kind, s4 = meta
                            r0 = (s4 * 4 + cc) * 128
                            dst = sg_out[kind, r0:r0 + 128, tt * 512:(tt + 1) * 512]
                        S.dma('sp', dst, sg[:], r=[sk], w=[], stream=sk + 'o')
                        ev += 1
            else:
                kind, s2 = meta
                for stt in range(nt // 128):
                    b = cx.bank()
                    bk = cx.banks[b]
                    for dc in range(NCH):
                        S.op('pe', lambda e, dc=dc, stt=stt, bk=bk: e.matmul(
                            bk[:, :], lhsT=xnT[:, dc, stt * 128:(stt + 1) * 128], rhs=w[:, dc, :],
                            start=(dc == 0), stop=(dc == NCH - 1)), r=[wk, 'xnT'], w=[f'bank{b}'])
                    sg = stg[ev % 4]
                    sk = f'stg{ev % 4}'
                    if ev % 2 == 0:
                        S.op('act', lambda e, sg=sg, bk=bk: e.activation(out=sg[:], in_=bk[:], func=AF.Copy),
                             r=[f'bank{b}'], w=[sk])
                    else:
                        S.op('dve', lambda e, sg=sg, bk=bk: e.tensor_copy(out=sg[:], in_=bk[:]),
                             r=[f'bank{b}'], w=[sk])
                    S.dma('sp', v_out[kind, stt * 128:(stt + 1) * 128, s2 * 512:(s2 + 1) * 512], sg[:],
                          r=[sk], w=[], stream=sk + 'o')
                    ev += 1
        for tt in range(ntile):
            b = cx.bank()
            bk = cx.banks[b]
            for dc in range(NCH):
                S.op('pe', lambda e, dc=dc, tt=tt, bk=bk: e.matmul(
                    bk[0:8, :], lhsT=wf[:, dc, :], rhs=xnT[:, dc, tt * 512:(tt + 1) * 512],
                    start=(dc == 0), stop=(dc == NCH - 1)), r=['wf', 'xnT'], w=[f'bank{b}'])
            S.op('act', lambda e, bk=bk: e.activation(out=stgf[:], in_=bk[0:8, :], func=AF.Sigmoid,
                                                      bias=bcol[:, 0:1], scale=1.0),
                 r=[f'bank{b}', 'bcol'], w=['stgf'])
            S.op('act', lambda e: e.activation(out=stgf[:], in_=stgf[:], func=AF.Ln), r=['stgf'], w=['stgf'])
            S.dma('sp', logf_out[:, tt * 512:(tt + 1) * 512], stgf[:], r=['stgf'], w=[], stream='stgfo')
        S.barrier()


def build_proj(nt=NT):
    nc = bass.Bass("TRN2", target_bir_lowering=False)
    hT = nc.dram_tensor("hT", [D, nt], F32, kind="ExternalInput").ap()
    gpc = nc.dram_tensor("gpc", [128, NCH], F32, kind="ExternalInput").ap()
    w_in = nc.dram_tensor("w_in", [D, IN_COLS], F32, kind="ExternalInput").ap()
    bfg = nc.dram_tensor("bfg", [8, 1], F32, kind="ExternalInput").ap()
    cst = nc.dram_tensor("cst", [128, 768], F32, kind="ExternalInput").ap()
    qk = nc.dram_tensor("qk", [32, 128, nt], BF16, kind="ExternalOutput").ap()
    v = nc.dram_tensor("v", [2, nt, 1024], BF16, kind="ExternalOutput").ap()
    logf = nc.dram_tensor("logf", [8, nt], F32, kind="ExternalOutput").ap()
    sg = nc.dram_tensor("sg", [2, D, nt], BF16, kind="ExternalOutput").ap()
    with ExitStack() as stack:
        cx = Ctx(nc, stack)
        phase_proj(cx, hT, gpc, w_in, bfg, cst, qk, v, logf, sg, nt)
        cx.S.finish()
    return nc


def make_consts2():
    c = make_consts()
    sel = np.zeros((128, 8 * 128), np.float32)
    for h in range(8):
        sel[h, h * 128:(h + 1) * 128] = 1.0
    return np.ascontiguousarray(np.concatenate([c, sel], axis=1))


def load_consts2(cx, stack, cst):
    S = cx.S
    c32 = cx.sb(stack, "c32", [128, 1792], F32)
    c16 = cx.sb(stack, "c16", [128, 768], BF16)
    S.dma('sp', c32[:], cst, w=['c32'], stream='c32')
    S.op('dve', lambda e: e.tensor_copy(out=c16[:], in_=c32[:, 0:768]), r=['c32'], w=['c16'])
    return c32, c16


def attn_load_head(cx, st, ci, qT, kT, v, h, seq):
    S = cx.S
    nblk = seq // 128
    d = {}
    d['k'] = cx.sb(st, 'k', [128, seq], BF16)
    d['q'] = cx.sb(st, 'q', [128, seq], BF16)
    d['v'] = cx.sb(st, 'v', [128, nblk, 128], BF16)
    return d


def attn_issue_loads(cx, d, ci, qT, kT, v, h):
    S = cx.S
    S.dma('sp', d['k'][:], kT[h], w=[f'k{ci}'], stream=f'k{ci}')
    S.dma('sp', d['q'][:], qT[h], w=[f'q{ci}'], stream=f'q{ci}')
    S.dma('sp', d['v'][:], v[:, h * 128:(h + 1) * 128].rearrange("(b p) d -> p b d", p=128), w=[f'v{ci}'],
          stream=f'v{ci}')


def phase_attn(cx, qT_sb, kT_sb, v_sb, qT_fx, kT_fx, v_fx, logf, cst, oT, nh, seq):
    S = cx.S
    nblk = seq // 128
    nqc = seq // 512
    nchain = 2 if nh >= 2 else 1
    with ExitStack() as st:
        c32, c16 = load_consts2(cx, st, cst)
        ident32 = c32[:, 0:128]
        strict32 = c32[:, 128:256]
        strict16 = c16[:, 128:256]
        incl16 = c16[:, 256:384]
        negTri = c32[:, 384:512]
        negTriC = c32[:, 512:640]
        ones16 = c16[:, 640:768]
        onec = cx.sb(st, "onec", [128, 1], F32)
        S.op('dve', lambda e: e.memset(onec[:], 1.0), w=['onec'])
        chains = []
        for ci in range(nchain):
            d = attn_load_head(cx, st, ci, None, None, None, 0, seq)
            for nm in ('e', 'sp', 'tmp', 'ea'):
                d[nm] = [cx.sb(st, nm, [128, 512], F32) for _ in range(2)]
            d['P'] = [cx.sb(st, 'P', [128, 512], BF16) for _ in range(2)]
            d['bias'] = [cx.sb(st, 'bias', [128, 4], F32) for _ in range(2)]
            d['og'] = cx.sb(st, 'og', [128, 512], BF16)
            d['rd'] = cx.sb(st, 'rd', [128, 512], F32)
            d['Zs'], d['A'], d['O'] = (3 * ci, 6 + ci), 3 * ci + 1, 3 * ci + 2
            chains.append(d)

        Fsb = cx.sb(st, "Fsb", [nh, seq], F32)
        negF = cx.sb(st, "negF", [128, nblk * nh], F32)
        Fref = cx.sb(st, "Fref", [128, nh * nblk], F32)
        spl = [cx.sb(st, "spl", [nh, seq], BF16) for _ in range(3)]
        kaug = [cx.sb(st, "kaug", [6, seq], BF16) for _ in range(nchain)]
        qaug = [cx.sb(st, "qaug", [6, seq], BF16) for _ in range(nchain)]
        with ExitStack() as st2:
            lf = cx.sb(st2, "lf", [nh, seq], F32)
            onesr = cx.sb(st2, "onesr", [nh, seq], F32)
            S.dma('sp', lf[:], logf, w=['lf'], stream='lf')
            S.op('dve', lambda e: e.memset(onesr[:], 1.0), w=['onesr'])
            S.op('dve', lambda e: e.tensor_tensor_scan(out=Fsb[:], data0=onesr[:], data1=lf[:], initial=0.0,
                                                       op0=ALU.mult, op1=ALU.add), r=['lf', 'onesr'], w=['Fsb'])
            b6, b7 = cx.banks[6], cx.banks[7]
            for blk in range(nblk):
                S.op('pe', lambda e, blk=blk: e.matmul(b6[:, blk * nh:(blk + 1) * nh],
                                                       lhsT=Fsb[0:nh, blk * 128:(blk + 1) * 128],
                                                       rhs=c32[0:nh, 0:nh], start=True, stop=True),
                     r=['Fsb', 'c32'], w=['bank6'])
            S.op('dve', lambda e: e.tensor_scalar(out=negF[:], in0=b6[:, 0:nblk * nh], scalar1=-1.0, scalar2=None,
                                                  op0=ALU.mult), r=['bank6'], w=['negF'])
            for h in range(nh):
                S.op('pe', lambda e, h=h: e.matmul(b7[:, h * nblk:(h + 1) * nblk],
                                                   lhsT=c32[0:nh, 768 + h * 128:768 + (h + 1) * 128],
                                                   rhs=Fsb[0:nh, 64:seq:128], start=True, stop=True),
                     r=['Fsb', 'c32'], w=['bank7'])
            S.op('dve', lambda e: e.tensor_copy(out=Fref[:], in_=b7[:, 0:nh * nblk]), r=['bank7'], w=['Fref'])
            S.op('dve', lambda e: e.tensor_scalar(out=lf[:], in0=Fsb[:], scalar1=float(HD ** 0.5), scalar2=None,
                                                  op0=ALU.mult), r=['Fsb', 'lf'], w=['lf'])
            S.op('dve', lambda e: e.tensor_copy(out=spl[0][:], in_=lf[:]), r=['lf'], w=['spl0'])
            S.op('dve', lambda e: e.tensor_tensor(out=onesr[:], in0=lf[:], in1=spl[0][:], op=ALU.subtract),
                 r=['lf', 'spl0'], w=['onesr'])
            S.op('dve', lambda e: e.tensor_copy(out=spl[1][:], in_=onesr[:]), r=['onesr'], w=['spl1'])
            S.op('dve', lambda e: e.tensor_tensor(out=onesr[:], in0=onesr[:], in1=spl[1][:], op=ALU.subtract),
                 r=['onesr', 'spl1'], w=['onesr'])
            S.op('dve', lambda e: e.tensor_copy(out=spl[2][:], in_=onesr[:]), r=['onesr'], w=['spl2'])
            S.barrier()

        def run_group(kind, heads):
            qT, kT, v = (qT_sb, kT_sb, v_sb) if kind == 0 else (qT_fx, kT_fx, v_fx)
            act = list(enumerate(heads))
            for ci, h in act:
                attn_issue_loads(cx, chains[ci], ci, qT, kT, v, h)
            if kind == 1:
                for ci, h in act:
                    S.op('dve', lambda e, ci=ci: e.memset(kaug[ci][:], 1.0), w=[f'kaug{ci}'])
                    S.op('dve', lambda e, ci=ci: e.memset(qaug[ci][:], -1.0), w=[f'qaug{ci}'])
                    for r3 in range(3):
                        S.dma('sp', kaug[ci][3 + r3:4 + r3, :], spl[r3][h:h + 1, :], r=[f'spl{r3}'], w=[f'kaug{ci}'],
                              stream=f'ka{ci}_{r3}')
                        S.dma('sp', qaug[ci][r3:r3 + 1, :], spl[r3][h:h + 1, :], r=[f'spl{r3}'], w=[f'qaug{ci}'],
                              stream=f'qa{ci}_{r3}')
            tiles = [(qc, ti, kb) for qc in range(nqc) for ti, kb in enumerate(range(4 * qc + 3, -1, -1))]

            def emit_qk(idx):
                qc_, ti_, kb_ = tiles[idx]
                c0_ = 128 * (kb_ - 4 * qc_) if kb_ >= 4 * qc_ else 0
                t0_ = qc_ * 512
                for ci, h in act:
                    d = chains[ci]
                    zb = d['Zs'][idx % 2]
                    Z = cx.banks[zb]
                    S.op('pe', lambda e, d=d, Z=Z: e.matmul(Z[:, c0_:512], lhsT=d['k'][:, kb_ * 128:(kb_ + 1) * 128],
                                                             rhs=d['q'][:, t0_ + c0_:t0_ + 512], start=True,
                                                             stop=(kind == 0)),
                         r=[f'k{ci}', f'q{ci}'], w=[f"bank{zb}"])
                    if kind == 1:
                        S.op('pe', lambda e, ci=ci, Z=Z: e.matmul(Z[:, c0_:512],
                                                                   lhsT=kaug[ci][:, kb_ * 128:(kb_ + 1) * 128],
                                                                   rhs=qaug[ci][:, t0_ + c0_:t0_ + 512], start=False,
                                                                   stop=True),
                             r=[f'kaug{ci}', f'qaug{ci}'], w=[f"bank{zb}"])

            emit_qk(0)
            tix = 0
            for qc in range(nqc):
                blocks = list(range(4 * qc + 3, -1, -1))
                for ti, kb in enumerate(blocks):
                    par = tix % 2
                    tix += 1
                    for d_ in chains:
                        d_['Z'] = d_['Zs'][par]
                    diag = kb >= 4 * qc
                    j = kb - 4 * qc if diag else 0
                    c0 = 128 * j
                    first = ti == 0
                    last = kb == 0
                    t0 = qc * 512
                    if tix < len(tiles):
                        emit_qk(tix)
                    if kind == 0:
                        for ci, h in act:
                            d = chains[ci]
                            Z = cx.banks[d['Z']]
                            e_, sp_ = d['e'][par], d['sp'][par]
                            S.op('act', lambda e, Z=Z, e_=e_: e.activation(out=e_[:, c0:512], in_=Z[:, c0:512],
                                                                           func=AF.Exp, scale=SCALE),
                                 r=[f"bank{d['Z']}"], w=[f'e{ci}_{par}'])
                            S.op('act', lambda e, e_=e_, sp_=sp_: e.activation(out=sp_[:, c0:512], in_=e_[:, c0:512],
                                                                               func=AF.Ln, bias=onec[:, 0:1], scale=1.0),
                                 r=[f'e{ci}_{par}', 'onec'], w=[f'sp{ci}_{par}'])
                            if diag:
                                S.op('pool', lambda e, sp_=sp_: e.tensor_tensor(out=sp_[:, c0:c0 + 128],
                                                                                in0=sp_[:, c0:c0 + 128], in1=strict32,
                                                                                op=ALU.mult),
                                     r=[f'sp{ci}_{par}', 'c32'], w=[f'sp{ci}_{par}'])
                        for ci, h in act:
                            d = chains[ci]
                            A = cx.banks[d['A']]
                            sp_ = d['sp'][par]
                            S.op('pe', lambda e, A=A, sp_=sp_: e.matmul(A[:, c0:512], lhsT=negTri, rhs=sp_[:, c0:512],
                                                                         start=first, stop=True),
                                 r=[f'sp{ci}_{par}', 'c32'], w=[f"bank{d['A']}"])
                        for ci, h in act:
                            d = chains[ci]
                            A = cx.banks[d['A']]
                            sp_, tmp_ = d['sp'][par], d['tmp'][par]
                            S.op('dve', lambda e, A=A, sp_=sp_, tmp_=tmp_: e.tensor_tensor(
                                out=tmp_[:, c0:512], in0=A[:, c0:512], in1=sp_[:, c0:512], op=ALU.subtract),
                                r=[f"bank{d['A']}", f'sp{ci}_{par}'], w=[f'tmp{ci}_{par}'])
                        for ci, h in act:
                            d = chains[ci]
                            tmp_, ea_ = d['tmp'][par], d['ea'][par]
                            S.op('act', lambda e, tmp_=tmp_, ea_=ea_: e.activation(out=ea_[:, c0:512],
                                                                                   in_=tmp_[:, c0:512], func=AF.Exp),
                                 r=[f'tmp{ci}_{par}'], w=[f'ea{ci}_{par}'])
                        for ci, h in act:
                            d = chains[ci]
                            e_, ea_, P_ = d['e'][par], d['ea'][par], d['P'][par]
                            S.op('dve', lambda e, e_=e_, ea_=ea_, P_=P_: e.tensor_tensor(
                                out=P_[:, c0:512], in0=e_[:, c0:512], in1=ea_[:, c0:512], op=ALU.mult),
                                r=[f'e{ci}_{par}', f'ea{ci}_{par}'], w=[f'P{ci}_{par}'])
                            if diag:
                                S.op('pool', lambda e, P_=P_: e.tensor_tensor(out=P_[:, c0:c0 + 128],
                                                                              in0=P_[:, c0:c0 + 128], in1=strict16,
                                                                              op=ALU.mult),
                                     r=[f'P{ci}_{par}', 'c16'], w=[f'P{ci}_{par}'])
                        for ci, h in act:
                            d = chains[ci]
                            O = cx.banks[d['O']]
                            A = cx.banks[d['A']]
                            P_, sp_ = d['P'][par], d['sp'][par]
                            S.op('pe', lambda e, O=O, P_=P_, d=d: e.matmul(O[:, c0:512], lhsT=d['v'][:, kb, :],
                                                                            rhs=P_[:, c0:512], start=first, stop=last),
                                 r=[f'P{ci}_{par}', f'v{ci}'], w=[f"bank{d['O']}"])
                            if not last:
                                S.op('pe', lambda e, A=A, sp_=sp_: e.matmul(A[:, c0:512], lhsT=negTriC,
                                                                             rhs=sp_[:, c0:512], start=False, stop=True),
                                     r=[f'sp{ci}_{par}', 'c32', f'tmp{ci}_{par}'], w=[f"bank{d['A']}"])
                    else:
                        for ci, h in act:
                            d = chains[ci]
                            Z = cx.banks[d['Z']]
                            bs, P_ = d['bias'][par], d['P'][par]
                            S.op('act', lambda e, Z=Z, P_=P_: e.activation(
                                out=P_[:, c0:512], in_=Z[:, c0:512], func=AF.Exp, scale=SCALE),
                                r=[f"bank{d['Z']}"], w=[f'P{ci}_{par}'])
                            if diag:
                                S.op('pool', lambda e, P_=P_: e.tensor_tensor(out=P_[:, c0:c0 + 128],
                                                                              in0=P_[:, c0:c0 + 128], in1=incl16,
                                                                              op=ALU.mult),
                                     r=[f'P{ci}_{par}', 'c16'], w=[f'P{ci}_{par}'])
                        for ci, h in act:
                            d = chains[ci]
                            O = cx.banks[d['O']]
                            A = cx.banks[d['A']]
                            P_ = d['P'][par]
                            S.op('pe', lambda e, O=O, P_=P_, d=d: e.matmul(O[:, c0:512], lhsT=d['v'][:, kb, :],
                                                                            rhs=P_[:, c0:512], start=first, stop=last),
                                 r=[f'P{ci}_{par}', f'v{ci}'], w=[f"bank{d['O']}"])
                            S.op('pe', lambda e, A=A, P_=P_: e.matmul(A[:, c0:512], lhsT=ones16, rhs=P_[:, c0:512],
                                                                       start=first, stop=last),
                                 r=[f'P{ci}_{par}', 'c16'], w=[f"bank{d['A']}"])
                for ci, h in act:
                    d = chains[ci]
                    O = cx.banks[d['O']]
                    A = cx.banks[d['A']]
                    if kind == 0:
                        S.op('act', lambda e, O=O, d=d: e.activation(out=d['og'][:], in_=O[:], func=AF.Copy),
                             r=[f"bank{d['O']}"], w=[f'og{ci}'])
                    else:
                        S.op('dve', lambda e, A=A, d=d: e.reciprocal(out=d['rd'][:], in_=A[:]),
                             r=[f"bank{d['A']}"], w=[f'rd{ci}'])
                        S.op('dve', lambda e, O=O, d=d: e.tensor_tensor(out=d['og'][:], in0=O[:], in1=d['rd'][:],
                                                                        op=ALU.mult),
                             r=[f"bank{d['O']}", f'rd{ci}'], w=[f'og{ci}'])
                    S.dma('sp', oT[kind, h * 128:(h + 1) * 128, qc * 512:(qc + 1) * 512], d['og'][:],
                          r=[f'og{ci}'], stream=f'og{ci}o')

        for kind in (0, 1):
            for h0 in range(0, nh, nchain):
                run_group(kind, list(range(h0, min(nh, h0 + nchain))))
        S.barrier()


def build_attn(nh=4, seq=SEQ):
    nc = bass.Bass("TRN2", target_bir_lowering=False)
    aps = {}
    for nm in ("qT_sb", "kT_sb", "qT_fx", "kT_fx"):
        aps[nm] = nc.dram_tensor(nm, [nh, 128, seq], BF16, kind="ExternalInput").ap()
    for nm in ("v_sb", "v_fx"):
        aps[nm] = nc.dram_tensor(nm, [seq, nh * 128], BF16, kind="ExternalInput").ap()
    logf = nc.dram_tensor("logf", [nh, seq], F32, kind="ExternalInput").ap()
    cst = nc.dram_tensor("cst", [128, 1792], F32, kind="ExternalInput").ap()
    oT = nc.dram_tensor("oT", [2, nh * 128, seq], BF16, kind="ExternalOutput").ap()
    with ExitStack() as stack:
        cx = Ctx(nc, stack)
        phase_attn(cx, aps["qT_sb"], aps["kT_sb"], aps["v_sb"], aps["qT_fx"], aps["kT_fx"], aps["v_fx"], logf, cst,
                   oT, nh, seq)
        cx.S.finish()
    return nc


def linear_T(cx, wview, kc, ncols, rhs_fn, rkeys, wsl, evac, n=512, pre=None):
    S = cx.S
    nsl = ncols // 512

    def load(i):
        S.dma('pool', wsl[i % 2][:, 0:kc, :], wview[:, :, i * 512:(i + 1) * 512],
              w=[f'wsl{i % 2}'], stream=f'wsl{i % 2}')

    load(0)
    for i in range(nsl):
        if i + 1 < nsl:
            load(i + 1)
        if pre is not None:
            pre(i)
        w = wsl[i % 2]
        for cc in range(4):
            b = cx.bank2()
            bk = cx.banks[b]
            for dc in range(kc):
                S.op('pe', lambda e, dc=dc, cc=cc, bk=bk: e.matmul(bk[:, :n], lhsT=w[:, dc, cc * 128:(cc + 1) * 128],
                                                                    rhs=rhs_fn(dc), start=(dc == 0), stop=(dc == kc - 1)),
                     r=[f'wsl{i % 2}'] + rkeys, w=[f'bank{b}'])
            evac(i * 4 + cc, b, bk)


def phase_post(cx, hT, oT, sg, w_bsb, w_bfx, w_out, w_query, skT, uT, ev, w_pg, w_ple, pT, gpc, cst, hT_out,
               fin_out, nt=NT, nexp=NEXP, sel=None):
    S = cx.S
    ntile = nt // 512
    nich = nexp // 128
    assert nich == 128
    GRP = 4
    ngrp = nich // GRP
    cx.bank2 = lambda: 4 + (cx.bank() % 4)
    with ExitStack() as st:
        c32, c16 = load_consts(cx, st, cst)
        ident16 = c16[:, 0:128]
        gcol = cx.sb(st, "gcol", [128, 48], F32)
        S.dma('sp', gcol[:], gpc, w=['gcol'], stream='gcol')
        tmpn = norm_tmp(cx, st)
        msel = None
        if sel is not None:
            msel = cx.sb(st, "msel", [128, 2], F32)
            S.dma('sp', msel[:], sel['m'], w=['msel'], stream='msel')
        hacc = cx.sb(st, "hacc", [128, NCH, 512], F32)
        xnb = cx.sb(st, "xnb", [128, NCH, 512], BF16)
        E = [cx.sb(st, "E", [128, 16, 128], F32) for _ in range(4)]
        thr = [cx.sb(st, "thr", [128, 8], F32) for _ in range(4)]
        Dg = [cx.sb(st, "Dg", [128, 8, 128], BF16) for _ in range(4)]
        hview = hT.rearrange("(c p) t -> p c t", p=128)
        oview = hT_out.rearrange("(c p) t -> p c t", p=128)
        fview = fin_out.rearrange("(c p) t -> p c t", p=128) if fin_out is not None else None
        wv = lambda w: w.rearrange("(c p) n -> p c n", p=128)
        for tt in range(ntile):
            tsl = slice(tt * 512, (tt + 1) * 512)
            S.dma('sp', hacc[:], hview[:, :, tsl], w=['hacc'], stream='hacc')
            if sel is not None:
                h1view = sel['hT1'].rearrange("(c p) t -> p c t", p=128)
                for k4 in range(4):
                    S.dma('sp', E[k4][:].rearrange("p a k -> p (a k)").rearrange("p (c t) -> p c t", c=4),
                          h1view[:, 4 * k4:4 * k4 + 4, tsl], w=[f'E{k4}'], stream=f'E{k4}')
                S.op('dve', lambda e: e.tensor_scalar(out=hacc[:].rearrange("p c t -> p (c t)"),
                                                      in0=hacc[:].rearrange("p c t -> p (c t)"),
                                                      scalar1=msel[:, 0:1], scalar2=None, op0=ALU.mult),
                     r=['hacc', 'msel'], w=['hacc'])
                for k4 in range(4):
                    hv = hacc[:, 4 * k4:4 * k4 + 4, :].rearrange("p c t -> p (c t)")
                    S.op('dve', lambda e, k4=k4, hv=hv: e.scalar_tensor_tensor(
                        out=hv, in0=E[k4][:].rearrange("p a k -> p (a k)"), scalar=msel[:, 1:2], in1=hv,
                        op0=ALU.mult, op1=ALU.add), r=[f'E{k4}', 'hacc', 'msel'], w=['hacc'])
            with ExitStack() as s2:
                wsl = [cx.sb(s2, "wsl", [128, NCH, 512], BF16) for _ in range(2)]
                mrg = cx.sb(s2, "mrg", [128, NCH, 512], BF16)
                ot = cx.sb(s2, "ot", [128, 16, 512], BF16)
                sgs = cx.sb(s2, "sgs", [128, 2, 4, 512], BF16)
                t2 = cx.sb(s2, "t2", [128, 512], F32)
                skb = cx.sb(s2, "skb", [128, 16, 128], BF16)
                top = cx.sb(s2, "top", [128, 16, 16], F32)
                negm = cx.sb(s2, "negm", [128, 16], F32)
                d16 = cx.sb(s2, "d16", [128, 16], F32)
                work = cx.sb(s2, "work", [128, 256], F32)
                cand = cx.sb(s2, "cand", [128, 8, 256], F32)
                ctop = cx.sb(s2, "ctop", [128, 8, 24], F32)
                sm = cx.sb(s2, "sm", [128, 8, 4], F32)
                ez = cx.sb(s2, "ez", [128, 8, 16], F32)
                S.dma('pool', skb[:], skT.rearrange("a c k -> c a k"), w=['skb'], stream='skb')
                S.dma('sp', ot[:], oT.rearrange("k (c p) t -> p (k c) t", p=128)[:, :, tsl], w=['ot'], stream='ot')
                sgv = sg.rearrange("k (c p) t -> p k c t", p=128)
                if sel is not None:
                    sgs2 = cx.sb(s2, "sgs2", [128, 2, 4, 512], BF16)
                    sgv1 = sel['sg1'].rearrange("k (c p) t -> p k c t", p=128)
                    S.dma('sp', mrg[:], sel['oT1'].rearrange("k (c p) t -> p (k c) t", p=128)[:, :, tsl], w=['mrg'],
                          stream='mrgl')
                    of, mf = ot[:].rearrange("p c t -> p (c t)"), mrg[:].rearrange("p c t -> p (c t)")
                    S.op('dve', lambda e: e.tensor_scalar(out=of, in0=of, scalar1=msel[:, 0:1], scalar2=None,
                                                          op0=ALU.mult), r=['ot', 'msel'], w=['ot'])
                    S.op('dve', lambda e: e.scalar_tensor_tensor(out=of, in0=mf, scalar=msel[:, 1:2], in1=of,
                                                                 op0=ALU.mult, op1=ALU.add),
                         r=['mrg', 'ot', 'msel'], w=['ot'])
                bsv, bfv = wv(w_bsb), wv(w_bfx)

                def load_b(i):
                    S.dma('pool', wsl[i % 2][:, 0:8, :], bsv[:, :, i * 512:(i + 1) * 512], w=[f'wsl{i % 2}'],
                          stream=f'wsl{i % 2}')
                    S.dma('pool', wsl[i % 2][:, 8:16, :], bfv[:, :, i * 512:(i + 1) * 512], w=[f'wsl{i % 2}'],
                          stream=f'wsl{i % 2}b')

                load_b(0)
                for i in range(4):
                    if i + 1 < 4:
                        load_b(i + 1)
                    S.dma('sp', sgs[:, 0], sgv[:, 0, 4 * i:4 * i + 4, tsl], w=['sgs'], stream='sgs')
                    S.dma('sp', sgs[:, 1], sgv[:, 1, 4 * i:4 * i + 4, tsl], w=['sgs'], stream='sgsb')
                    if sel is not None:
                        S.dma('sp', sgs2[:, 0], sgv1[:, 0, 4 * i:4 * i + 4, tsl], w=['sgs2'], stream='sgs2')
                        S.dma('sp', sgs2[:, 1], sgv1[:, 1, 4 * i:4 * i + 4, tsl], w=['sgs2'], stream='sgs2b')
                        sf = sgs[:].rearrange("p k c t -> p (k c t)")
                        sf2 = sgs2[:].rearrange("p k c t -> p (k c t)")
                        S.op('dve', lambda e, sf=sf: e.tensor_scalar(out=sf, in0=sf, scalar1=msel[:, 0:1], scalar2=None,
                                                                     op0=ALU.mult), r=['sgs', 'msel'], w=['sgs'])
                        S.op('dve', lambda e, sf=sf, sf2=sf2: e.scalar_tensor_tensor(
                            out=sf, in0=sf2, scalar=msel[:, 1:2], in1=sf, op0=ALU.mult, op1=ALU.add),
                            r=['sgs2', 'sgs', 'msel'], w=['sgs'])
                    w = wsl[i % 2]
                    for cc in range(4):
                        c = 4 * i + cc
                        b1, b2 = cx.bank2(), cx.bank2()
                        k1, k2 = cx.banks[b1], cx.banks[b2]
                        for dc in range(8):
                            S.op('pe', lambda e, dc=dc, cc=cc, k1=k1: e.matmul(
                                k1[:], lhsT=w[:, dc, cc * 128:(cc + 1) * 128], rhs=ot[:, dc, :],
                                start=(dc == 0), stop=(dc == 7)), r=[f'wsl{i % 2}', 'ot'], w=[f'bank{b1}'])
                        for dc in range(8):
                            S.op('pe', lambda e, dc=dc, cc=cc, k2=k2: e.matmul(
                                k2[:], lhsT=w[:, 8 + dc, cc * 128:(cc + 1) * 128], rhs=ot[:, 8 + dc, :],
                                start=(dc == 0), stop=(dc == 7)), r=[f'wsl{i % 2}', 'ot'], w=[f'bank{b2}'])
                        S.op('dve', lambda e, k1=k1, cc=cc: e.tensor_tensor(out=t2[:], in0=k1[:], in1=sgs[:, 0, cc, :],
                                                                            op=ALU.mult),
                             r=[f'bank{b1}', 'sgs'], w=['t2'])
                        S.op('dve', lambda e, k2=k2, cc=cc, c=c: e.tensor_tensor(out=mrg[:, c, :], in0=k2[:],
                                                                                 in1=sgs[:, 1, cc, :], op=ALU.mult),
                             r=[f'bank{b2}', 'sgs'], w=['mrg'])
                        S.op('dve', lambda e, c=c: e.tensor_tensor(out=mrg[:, c, :], in0=mrg[:, c, :], in1=t2[:],
                                                                   op=ALU.add), r=['mrg', 't2'], w=['mrg'])

                def ev_out(c, b, bk):
                    S.op('dve', lambda e: e.tensor_tensor(out=hacc[:, c, :], in0=bk[:], in1=hacc[:, c, :], op=ALU.add),
                         r=[f'bank{b}', 'hacc'], w=['hacc'])

                linear_T(cx, wv(w_out), NCH, D, lambda dc: mrg[:, dc, :], ['mrg'], wsl, ev_out)
                rmsnorm_tile(cx, hacc, 'hacc', gcol, 0, lambda c: xnb[:, c, :], 'xnb', 512, c32, tmpn)
                qpT = mrg

                def ev_q(c, b, bk):
                    if c % 2 == 0:
                        S.op('act', lambda e: e.activation(out=qpT[:, c, :], in_=bk[:], func=AF.Copy),
                             r=[f'bank{b}'], w=['mrg'])
                    else:
                        S.op('dve', lambda e: e.tensor_copy(out=qpT[:, c, :], in_=bk[:]), r=[f'bank{b}'], w=['mrg'])

                linear_T(cx, wv(w_query), NCH, D, lambda dc: xnb[:, dc, :], ['xnb'], wsl, ev_q)
                for stt in range(4):
                    sbanks = []
                    for g4 in range(4):
                        b = cx.bank2()
                        bk = cx.banks[b]
                        sbanks.append((b, bk))
                        for q4 in range(4):
                            hp = g4 * 4 + q4
                            S.op('pe', lambda e, hp=hp, q4=q4, bk=bk: e.matmul(
                                bk[:, q4 * 128:(q4 + 1) * 128], lhsT=qpT[:, hp, stt * 128:(stt + 1) * 128],
                                rhs=skb[:, hp, :], start=True, stop=True), r=['mrg', 'skb'], w=[f'bank{b}'])
                    for hp in range(16):
                        b, bk = sbanks[hp // 4]
                        sc = bk[:, (hp % 4) * 128:(hp % 4 + 1) * 128]
                        S.op('dve', lambda e, sc=sc, hp=hp: e.max(out=top[:, hp, 0:8], in_=sc), r=[f'bank{b}'], w=['top'])
                        S.op('dve', lambda e, sc=sc, hp=hp: e.match_replace(out=work[:, 0:128], in_to_replace=top[:, hp, 0:8],
                                                                            in_values=sc, imm_value=-1e30),
                             r=[f'bank{b}', 'top'], w=['work'])
                        S.op('dve', lambda e, hp=hp: e.max(out=top[:, hp, 8:16], in_=work[:, 0:128]), r=['work'], w=['top'])
                    S.op('dve', lambda e: e.tensor_scalar(out=negm[:], in0=top[:, :, 0], scalar1=-1.0, scalar2=None,
                                                          op0=ALU.mult), r=['top'], w=['negm'])
                    S.op('dve', lambda e: e.tensor_tensor(out=d16[:], in0=top[:, :, 15], in1=negm[:], op=ALU.add),
                         r=['top', 'negm'], w=['d16'])
                    S.op('dve', lambda e: e.tensor_scalar(out=d16[:], in0=d16[:], scalar1=-2e-6, scalar2=None,
                                                          op0=ALU.add), r=['d16'], w=['d16'])
                    S.op('act', lambda e: e.activation(out=d16[:], in_=d16[:], func=AF.Exp), r=['d16'], w=['d16'])
                    for hp in range(16):
                        b, bk = sbanks[hp // 4]
                        sc = bk[:, (hp % 4) * 128:(hp % 4 + 1) * 128]
                        S.op('act', lambda e, sc=sc, hp=hp: e.activation(out=E[stt][:, hp, :], in_=sc, func=AF.Exp,
                                                                         bias=negm[:, hp:hp + 1], scale=1.0),
                             r=[f'bank{b}', 'negm'], w=[f'E{stt}'])
                        S.op('dve', lambda e, hp=hp: e.scalar_tensor_tensor(
                            out=E[stt][:, hp, :], in0=E[stt][:, hp, :], scalar=d16[:, hp:hp + 1], in1=E[stt][:, hp, :],
                            op0=ALU.is_ge, op1=ALU.mult), r=[f'E{stt}', 'd16'], w=[f'E{stt}'])
                    tv = top[:].rearrange("p (h two) a -> p h two a", two=2)
                    S.op('dve', lambda e: e.tensor_tensor(
                        out=cand[:].rearrange("p h (a b) -> p h a b", a=16),
                        in0=tv[:, :, 0, :].unsqueeze(3).to_broadcast([128, 8, 16, 16]),
                        in1=tv[:, :, 1, :].unsqueeze(2).to_broadcast([128, 8, 16, 16]), op=ALU.add),
                        r=['top'], w=['cand'])
                    for h in range(8):
                        S.op('dve', lambda e, h=h: e.max(out=ctop[:, h, 0:8], in_=cand[:, h, :]), r=['cand'], w=['ctop'])
                        S.op('dve', lambda e, h=h: e.match_replace(out=work[:], in_to_replace=ctop[:, h, 0:8],
                                                                   in_values=cand[:, h, :], imm_value=-1e30),
                             r=['cand', 'ctop'], w=['work'])
                        S.op('dve', lambda e, h=h: e.max(out=ctop[:, h, 8:16], in_=work[:]), r=['work'], w=['ctop'])
                        S.op('dve', lambda e, h=h: e.match_replace(out=work[:], in_to_replace=ctop[:, h, 8:16],
                                                                   in_values=work[:], imm_value=-1e30),
                             r=['work', 'ctop'], w=['work'])
                        S.op('dve', lambda e, h=h: e.max(out=ctop[:, h, 16:24], in_=work[:]), r=['work'], w=['ctop'])
                    S.op('dve', lambda e: e.tensor_scalar(out=sm[:, :, 0], in0=ctop[:, :, 0], scalar1=-1.0, scalar2=None,
                                                          op0=ALU.mult), r=['ctop'], w=['sm'])
                    S.op('dve', lambda e: e.tensor_tensor(out=sm[:, :, 3], in0=ctop[:, :, 15], in1=ctop[:, :, 16],
                                                          op=ALU.add), r=['ctop'], w=['sm'])
                    S.op('dve', lambda e: e.scalar_tensor_tensor(out=sm[:, :, 3], in0=sm[:, :, 3], scalar=0.5,
                                                                 in1=sm[:, :, 0], op0=ALU.mult, op1=ALU.add),
                         r=['sm'], w=['sm'])
                    for h in range(8):
                        S.op('act', lambda e, h=h: e.activation(out=ez[:, h, :], in_=ctop[:, h, 0:16], func=AF.Exp,
                                                                bias=sm[:, h, 0:1], scale=1.0),
                             r=['ctop', 'sm'], w=['ez'])
                    S.op('dve', lambda e: e.tensor_reduce(out=sm[:, :, 1], in_=ez[:], axis=mybir.AxisListType.X,
                                                          op=ALU.add), r=['ez'], w=['sm'])
                    S.op('dve', lambda e: e.reciprocal(out=sm[:, :, 2], in_=sm[:, :, 1]), r=['sm'], w=['sm'])
                    S.op('act', lambda e: e.activation(out=thr[stt][:], in_=sm[:, :, 3], func=AF.Exp), r=['sm'],
                         w=[f'thr{stt}'])
                    S.op('dve', lambda e: e.tensor_tensor(out=thr[stt][:], in0=thr[stt][:], in1=sm[:, :, 2], op=ALU.mult),
                         r=[f'thr{stt}', 'sm'], w=[f'thr{stt}'])
                    S.op('act', lambda e: e.activation(out=sm[:, :, 1], in_=sm[:, :, 3], func=AF.Exp, scale=-1.0),
                         r=['sm'], w=['sm'])
                    Ev = E[stt][:].rearrange("p (h two) k -> p h two k", two=2)
                    S.op('dve', lambda e, Ev=Ev: e.tensor_tensor(out=Ev[:, :, 0, :], in0=Ev[:, :, 0, :],
                                                                 in1=sm[:, :, 1].unsqueeze(2).to_broadcast([128, 8, 128]),
                                                                 op=ALU.mult), r=[f'E{stt}', 'sm'], w=[f'E{stt}'])
                    for h in range(8):
                        S.op('dve', lambda e, h=h: e.tensor_scalar(out=Dg[stt][:, h, :], in0=c32[:, 0:128],
                                                                   scalar1=thr[stt][:, h:h + 1], scalar2=None,
                                                                   op0=ALU.mult), r=['c32', f'thr{stt}'], w=[f'Dg{stt}'])
                S.barrier()
            with ExitStack() as s3:
                usl = [cx.sb(s3, "usl", [128, NCH, GRP * 128], BF16) for _ in range(2)]
                vsl = [cx.sb(s3, "vsl", [128, GRP, D], BF16) for _ in range(2)]
                Gm = [[cx.sb(s3, "Gm", [128, 4 * GRP * 128], BF16) for _ in range(2)] for _ in range(2)]
                Pt = [cx.sb(s3, "Pt", [128, 4 * GRP * 128], F32) for _ in range(1)]
                oev = [cx.sb(s3, "oev", [128, 512], F32) for _ in range(2)]
                gl = [cx.sb(s3, "gl", [128, 512], F32) for _ in range(GRP)]
                WT = [cx.sb(s3, "WT", [128, 512], BF16) for _ in range(GRP)]
                uview = uT.rearrange("(c p) e -> p c e", p=128)
                vview = ev.rearrange("(g c p) d -> g p c d", p=128, c=GRP)
                st8 = {'pc': 0, 'hb': 0, 'ob': 0}
                obank = {}

                def load_u(g):
                    S.dma('pool', usl[g % 2][:], uview[:, :, g * GRP * 128:(g + 1) * GRP * 128], w=[f'usl{g % 2}'],
                          stream=f'usl{g % 2}')

                def load_v(g):
                    S.dma('pool', vsl[g % 2][:], vview[g], w=[f'vsl{g % 2}'], stream=f'vsl{g % 2}')

                def emit_hidden(g):
                    u = usl[g % 2]
                    for ic in range(GRP):
                        hb = 4 + st8['hb'] % 2
                        st8['hb'] += 1
                        hbk = cx.banks[hb]
                        for dc in range(NCH):
                            S.op('pe', lambda e, dc=dc, ic=ic, hbk=hbk: e.matmul(
                                hbk[:], lhsT=u[:, dc, ic * 128:(ic + 1) * 128], rhs=xnb[:, dc, :],
                                start=(dc == 0), stop=(dc == NCH - 1)), r=[f'usl{g % 2}', 'xnb'], w=[f'bank{hb}'])
                        S.op('act', lambda e, ic=ic, hbk=hbk: e.activation(out=gl[ic][:], in_=hbk[:], func=AF.Gelu),
                             r=[f'bank{hb}'], w=[f'gl{ic}'])

                def emit_gm(g, stt, half):
                    i0 = g * GRP
                    par = stt % 2
                    P = Pt[0]
                    pk = "Pt0"
                    st8['pc'] += 1
                    Ev = E[stt][:].rearrange("p (h two) k -> p h two k", two=2)
                    gk = f'Gm{par}_{half}'
                    S.op('dve', lambda e, P=P: e.tensor_tensor(
                        out=P[:].rearrange("p (h a k) -> p h a k", h=4, a=GRP),
                        in0=Ev[:, 4 * half:4 * half + 4, 0, i0:i0 + GRP].unsqueeze(3).to_broadcast([128, 4, GRP, 128]),
                        in1=Ev[:, 4 * half:4 * half + 4, 1, :].unsqueeze(2).to_broadcast([128, 4, GRP, 128]),
                        op=ALU.mult), r=[f'E{stt}'], w=[pk])
                    S.op('dve', lambda e, P=P: e.scalar_tensor_tensor(
                        out=Gm[par][half][:], in0=P[:], scalar=1.0, in1=P[:], op0=ALU.is_ge, op1=ALU.mult),
                        r=[pk], w=[gk])
                    if half == 1:
                        for ic in range(GRP):
                            bk = cx.banks[ic]
                            for h in range(8):
                                S.op('pe', lambda e, bk=bk, h=h, ic=ic: e.matmul(
                                    bk[:, stt * 128:(stt + 1) * 128],
                                    lhsT=Gm[par][h // 4][:, ((h % 4) * GRP + ic) * 128:((h % 4) * GRP + ic + 1) * 128],
                                    rhs=Dg[stt][:, h, :], start=(h == 0), stop=(h == 7)),
                                    r=[f'Gm{par}_{h // 4}', f'Dg{stt}'], w=[f'bank{ic}'])

                def emit_wt(g):
                    for ic in range(GRP):
                        S.op('dve', lambda e, ic=ic: e.tensor_tensor(out=WT[ic][:], in0=cx.banks[ic][:], in1=gl[ic][:],
                                                                     op=ALU.mult),
                             r=[f'bank{ic}', f'gl{ic}'], w=[f'WT{ic}'])

                def emit_out(g, dcgs):
                    v = vsl[g % 2]
                    for dcg in dcgs:
                        ob = 6 + st8['ob'] % 2
                        st8['ob'] += 1
                        obank[dcg] = ob
                        obk = cx.banks[ob]
                        for ic in range(GRP):
                            S.op('pe', lambda e, ic=ic, dcg=dcg, obk=obk: e.matmul(
                                obk[:], lhsT=v[:, ic, dcg * 128:(dcg + 1) * 128], rhs=WT[ic][:],
                                start=(ic == 0), stop=(ic == GRP - 1)),
                                r=[f'vsl{g % 2}', f'WT{ic}'], w=[f'bank{ob}'])

                def emit_add(g, dcgs):
                    for dcg in dcgs:
                        ob = obank[dcg]
                        obk = cx.banks[ob]
                        if dcg % 2 == 1:
                            S.op('dve', lambda e, dcg=dcg, obk=obk: e.tensor_tensor(out=hacc[:, dcg, :], in0=obk[:],
                                                                                    in1=hacc[:, dcg, :], op=ALU.add),
                                 r=[f'bank{ob}', f'hacc{dcg}'], w=[f'hacc{dcg}'])
                            continue
                        oe = oev[(dcg // 2) % 2]
                        ok = f'oev{(dcg // 2) % 2}'
                        S.op('act', lambda e, obk=obk, oe=oe: e.activation(out=oe[:], in_=obk[:], func=AF.Copy),
                             r=[f'bank{ob}'], w=[ok])
                        S.op('pool', lambda e, dcg=dcg, oe=oe: e.tensor_tensor(out=hacc[:, dcg, :], in0=oe[:],
                                                                               in1=hacc[:, dcg, :], op=ALU.add),
                             r=[ok, f'hacc{dcg}'], w=[f'hacc{dcg}'])

                load_u(0)
                load_v(0)
                load_u(1)
                load_v(1)
                emit_hidden(0)
                for stt in range(4):
                    emit_gm(0, stt, 0)
                    emit_gm(0, stt, 1)
                for g in range(ngrp):
                    emit_wt(g)
                    if g + 1 < ngrp:
                        emit_hidden(g + 1)
                    if g + 2 < ngrp:
                        load_u(g + 2)
                    for k in range(8):
                        dcgs = [2 * k, 2 * k + 1]
                        emit_out(g, dcgs)
                        if g + 1 < ngrp:
                            emit_gm(g + 1, k // 2, k % 2)
                        emit_add(g, dcgs)
                    if g + 2 < ngrp:
                        load_v(g + 2)
                S.barrier()
            with ExitStack() as s4:
                wsl = [cx.sb(s4, "wsl", [128, NCH, 512], BF16) for _ in range(2)]
                sgate = cx.sb(s4, "sgate", [128, NCH, 512], F32)
                pt = cx.sb(s4, "pt", [128, 2, 512], BF16)
                t2 = [cx.sb(s4, "t2", [128, 512], F32) for _ in range(2)]
                S.dma('pool', pt[:], pT.rearrange("(c p) t -> p c t", p=128)[:, :, tsl], w=['pt'], stream='pt')
                rmsnorm_tile(cx, hacc, 'hacc', gcol, 1, lambda c: xnb[:, c, :], 'xnb', 512, c32, tmpn)

                def ev_g(c, b, bk):
                    S.op('act', lambda e: e.activation(out=sgate[:, c, :], in_=bk[:], func=AF.Sigmoid),
                         r=[f'bank{b}'], w=['sgate'])

                linear_T(cx, wv(w_pg), NCH, D, lambda dc: xnb[:, dc, :], ['xnb'], wsl, ev_g)

                def ev_p(c, b, bk):
                    t = t2[c % 2]
                    S.op('dve', lambda e: e.tensor_tensor(out=t[:], in0=bk[:], in1=sgate[:, c, :], op=ALU.mult),
                         r=[f'bank{b}', 'sgate'], w=[f't2{c % 2}'])
                    S.op('dve', lambda e: e.tensor_tensor(out=hacc[:, c, :], in0=t[:], in1=hacc[:, c, :], op=ALU.add),
                         r=[f't2{c % 2}', 'hacc'], w=['hacc'])

                linear_T(cx, wv(w_ple), 2, D, lambda dc: pt[:, dc, :], ['pt'], wsl, ev_p)
                S.dma('sp', oview[:, :, tsl], hacc[:], r=['hacc'], stream='hout')
                if fin_out is not None:
                    rmsnorm_tile(cx, hacc, 'hacc', gcol, 2, lambda c: sgate[:, c, :], 'sgate', 512, c32, tmpn)
                    S.dma('sp', fview[:, :, tsl], sgate[:], r=['sgate'], stream='fout')
                S.barrier()


def build_post(nt=NT, nexp=NEXP):
    nc = bass.Bass("TRN2", target_bir_lowering=False)
    di = lambda nm, shp, dt=F32: nc.dram_tensor(nm, shp, dt, kind="ExternalInput").ap()
    hT = di("hT", [D, nt]); oT = di("oT", [2, 1024, nt], BF16); sg = di("sg", [2, D, nt], BF16)
    w_bsb = di("w_bsb", [1024, D]); w_bfx = di("w_bfx", [1024, D]); w_out = di("w_out", [D, D])
    w_query = di("w_query", [D, D]); skT = di("skT", [16, 128, 128]); uT = di("uT", [D, nexp]); ev = di("ev", [nexp, D])
    w_pg = di("w_pg", [D, D]); w_ple = di("w_ple", [256, D]); pT = di("pT", [256, nt]); gpc = di("gpc", [128, 48])
    cst = di("cst", [128, 768])
    hT_out = nc.dram_tensor("hT_out", [D, nt], F32, kind="ExternalOutput").ap()
    fin_out = nc.dram_tensor("fin_out", [D, nt], F32, kind="ExternalOutput").ap()
    with ExitStack() as stack:
        cx = Ctx(nc, stack)
        phase_post(cx, hT, oT, sg, w_bsb, w_bfx, w_out, w_query, skT, uT, ev, w_pg, w_ple, pT, gpc, cst, hT_out,
                   fin_out, nt, nexp)
        cx.S.finish()
    return nc


DEPTH = 2


def build_fused(depth=DEPTH):
    nc = bass.Bass("TRN2", target_bir_lowering=False)
    di = lambda nm, shp, dt=F32: nc.dram_tensor(nm, shp, dt, kind="ExternalInput").ap()
    sc = lambda nm, shp, dt: nc.dram_tensor(nm, shp, dt, kind="Internal").ap()
    xT = di("xT", [D, SEQ]); pT = di("pT", [depth - 1, 256, SEQ]); pTl = di("pTl", [256, NT])
    msel_in = di("msel", [128, 2])
    gmix = di("gmix", [depth, 128, NCH]); g3 = di("g3", [depth, 128, 48]); bfg = di("bfg", [depth, 8, 1])
    w_in = di("w_in", [depth, D, IN_COLS]); w_bsb = di("w_bsb", [depth, 1024, D]); w_bfx = di("w_bfx", [depth, 1024, D])
    w_out = di("w_out", [depth, D, D]); w_query = di("w_query", [depth, D, D]); skT = di("skT", [depth, 16, 128, 128])
    uT = di("uT", [depth, D, NEXP]); ev = di("ev", [depth, NEXP, D]); w_pg = di("w_pg", [depth, D, D])
    w_ple = di("w_ple", [depth, 256, D]); cst = di("cst", [128, 1792])
    outT = nc.dram_tensor("outT", [D, NT], F32, kind="ExternalOutput").ap()
    qk = sc("s_qk", [32, 128, SEQ], BF16); v = sc("s_v", [2, SEQ, 1024], BF16); logf = sc("s_logf", [8, SEQ], F32)
    sg = sc("s_sg", [2, D, SEQ], BF16); oT = sc("s_oT", [2, 1024, SEQ], BF16)
    hbuf = [sc("s_hA", [D, SEQ], F32), sc("s_hB", [D, SEQ], F32)]
    cst1 = cst[:, 0:768]
    with ExitStack() as stack:
        cx = Ctx(nc, stack)
        cur = xT
        for i in range(depth):
            nxt = hbuf[i % 2]
            last = i == depth - 1
            for hf in range(SEQ // NT):
                hs = slice(hf * NT, (hf + 1) * NT)
                phase_proj(cx, cur[:, hs], gmix[i], w_in[i], bfg[i], cst1, qk[:, :, hs], v[:, hs, :], logf[:, hs],
                           sg[:, :, hs], NT)
            phase_attn(cx, qk[0:8], qk[8:16], v[0], qk[16:24], qk[24:32], v[1], logf, cst, oT, 8, SEQ)
            if last:
                h0, h1 = slice(0, NT), slice(NT, 2 * NT)
                phase_post(cx, cur[:, h0], oT[:, :, h0], sg[:, :, h0], w_bsb[i], w_bfx[i], w_out[i], w_query[i], skT[i],
                           uT[i], ev[i], w_pg[i], w_ple[i], pTl, g3[i], cst1, nxt[:, h0], outT, NT, NEXP,
                           sel={'m': msel_in, 'hT1': cur[:, h1], 'oT1': oT[:, :, h1], 'sg1': sg[:, :, h1]})
            else:
                for hf in range(SEQ // NT):
                    hs = slice(hf * NT, (hf + 1) * NT)
                    phase_post(cx, cur[:, hs], oT[:, :, hs], sg[:, :, hs], w_bsb[i], w_bfx[i], w_out[i], w_query[i],
                               skT[i], uT[i], ev[i], w_pg[i], w_ple[i], pT[i][:, hs], g3[i], cst1, nxt[:, hs],
                               None, NT, NEXP)
            cur = nxt
        cx.S.finish()
    return nc


_PROGS = {}


def _pc(g):
    return np.ascontiguousarray(np.asarray(g, np.float32).reshape(NCH, 128).T)


def kernel(x, p, norm_mix_g, w_in, b_forget, w_branch_sb, w_branch_fox, w_out, norm_ffn_g, w_query, sub_keys,
           expert_u, expert_v, norm_ple_g, w_ple, w_ple_gate, final_norm_g):
    f32 = lambda a: np.ascontiguousarray(np.asarray(a, np.float32))
    x = f32(x); p = f32(p)
    depth = w_in.shape[0]
    if "fused" not in _PROGS:
        _PROGS["fused"] = build_fused(depth)
    nc = _PROGS["fused"]
    cores = list(range(NCORES))
    shared = {
        "gmix": np.stack([_pc(norm_mix_g[i]) for i in range(depth)]),
        "g3": np.stack([np.concatenate([_pc(norm_ffn_g[i]), _pc(norm_ple_g[i]), _pc(final_norm_g)], axis=1)
                        for i in range(depth)]),
        "bfg": f32(b_forget).reshape(depth, 8, 1),
        "w_in": f32(w_in), "w_bsb": f32(w_branch_sb), "w_bfx": f32(w_branch_fox), "w_out": f32(w_out),
        "w_query": f32(w_query),
        "skT": np.ascontiguousarray(f32(sub_keys).reshape(depth, 16, 128, 128).transpose(0, 1, 3, 2)),
        "uT": np.ascontiguousarray(f32(expert_u).transpose(0, 2, 1)), "ev": f32(expert_v),
        "w_pg": f32(w_ple_gate), "w_ple": f32(w_ple), "cst": make_consts2(),
    }
    maps = []
    for c in cores:
        b, g = c % BATCH, c // BATCH
        m = dict(shared)
        m["xT"] = np.ascontiguousarray(x[b].T)
        m["pT"] = np.ascontiguousarray(p[:depth - 1, b].transpose(0, 2, 1))
        m["pTl"] = np.ascontiguousarray(p[depth - 1, b, g * NT:(g + 1) * NT].T)
        ms = np.zeros((128, 2), np.float32)
        ms[:, g] = 1.0
        m["msel"] = ms
        maps.append(m)
    res = run_bass_kernel_spmd(nc, maps, core_ids=cores).results
    out = np.empty((BATCH, SEQ, D), np.float32)
    for c in cores:
        b, g = c % BATCH, c // BATCH
        out[b, g * NT:(g + 1) * NT] = res[c]["outT"].T
    return out
```

```python
from contextlib import ExitStack
import numpy as np
import ml_dtypes
import concourse.bass as bass
import concourse.mybir as mybir
from concourse.bass_utils import run_bass_kernel_spmd

F32 = mybir.dt.float32
BF16 = mybir.dt.bfloat16
AF = mybir.ActivationFunctionType
ALU = mybir.AluOpType

D = 2048
NCH = 16
SEQ = 4096
BATCH = 4
HD = 128
NCORES = 8
NT = 2048
IN_COLS = 10248
SCALE = HD ** -0.5
EPS = 1e-6
NEXP = 16384


class Sync:
    def __init__(self, nc, stack):
        self.nc = nc
        self.stack = stack
        self.eng = {'pe': nc.tensor, 'dve': nc.vector, 'act': nc.scalar, 'pool': nc.gpsimd, 'sp': nc.sync}
        self.prod = {}
        self.waited = {e: {} for e in self.eng}
        self.lastw = {}
        self.readers = {}
        self.nsem = 0
        self.sems = {}

    def _newsem(self):
        self.nsem += 1
        s = self.stack.enter_context(self.nc.semaphore(f"sy{self.nsem}"))
        self.sems[self.nsem] = s
        return self.nsem

    def _inc(self, pname, n):
        p = self.prod.get(pname)
        if p is None or p[1] + n > 30000:
            p = [self._newsem(), 0]
            self.prod[pname] = p
        p[1] += n
        return p[0], p[1]

    def _deps(self, r, w):
        deps = []
        for k in r:
            if k in self.lastw:
                deps.append(self.lastw[k])
        for k in w:
            if k in self.lastw:
                deps.append(self.lastw[k])
            rd = self.readers.get(k)
            if rd:
                deps.extend(rd.values())
        return deps

    def _wait(self, e, deps):
        best = {}
        for (sid, val, pn) in deps:
            if pn == 'pe' and e == 'pe':
                continue
            if self.waited[e].get(sid, 0) >= val:
                continue
            if sid not in best or best[sid] < val:
                best[sid] = val
        for sid, val in best.items():
            self.eng[e].wait_ge(self.sems[sid], val)
            self.waited[e][sid] = val

    def _commit(self, tok, r, w):
        for k in w:
            self.lastw[k] = tok
            self.readers[k] = {}
        for k in r:
            d = self.readers.setdefault(k, {})
            d[(tok[0], tok[2])] = tok

    def op(self, e, fn, r=(), w=()):
        self._wait(e, self._deps(r, w))
        ins = fn(self.eng[e])
        sid, val = self._inc(e, 1)
        ins.then_inc(self.sems[sid], 1)
        self._commit((sid, val, e), r, w)

    def dma(self, q, out, in_, r=(), w=(), stream=None, **kw):
        pname = 'dma:' + stream
        deps = self._deps(r, w)
        p = self.prod.get(pname)
        if p is not None:
            deps.append((p[0], p[1], pname))
        self._wait(q, deps)
        ins = self.eng[q].dma_start(out=out, in_=in_, **kw)
        sid, val = self._inc(pname, 16)
        ins.then_inc(self.sems[sid], 16)
        self._commit((sid, val, pname), r, w)

    def barrier(self):
        for e in ('sp', 'pool', 'act', 'dve', 'pe'):
            deps = [(p[0], p[1], pn) for pn, p in self.prod.items() if pn != e]
            self._wait(e, deps)

    def finish(self):
        for e in ('sp', 'pool', 'act', 'dve', 'pe'):
            deps = [(p[0], p[1], pn) for pn, p in self.prod.items() if pn != e]
            self._wait(e, deps)


class Ctx:
    def __init__(self, nc, stack):
        self.nc = nc
        self.S = Sync(nc, stack)
        self.stack = stack
        self.banks = [stack.enter_context(nc.psum_tensor(f"bank{i}", [128, 512], F32)) for i in range(8)]
        self.bi = 0
        self.uid = 0

    def bank(self):
        b = self.bi
        self.bi = (self.bi + 1) % 8
        return b

    def sb(self, stack, name, shape, dt):
        self.uid += 1
        return stack.enter_context(self.nc.sbuf_tensor(f"{name}_{self.uid}", shape, dt))


def load_consts(cx, stack, cst):
    S = cx.S
    c32 = cx.sb(stack, "c32", [128, 6 * 128], F32)
    c16 = cx.sb(stack, "c16", [128, 6 * 128], BF16)
    S.dma('sp', c32[:], cst, w=['c32'], stream='c32')
    S.op('dve', lambda e: e.tensor_copy(out=c16[:], in_=c32[:]), r=['c32'], w=['c16'])
    return c32, c16


def make_consts():
    p = np.arange(128)[:, None]
    i = np.arange(128)[None, :]
    c = np.concatenate([
        (p == i), (p < i), (p <= i), -1.0 * (p > i), -1.0 * (p <= i), np.ones((128, 128))
    ], axis=1).astype(np.float32)
    return np.ascontiguousarray(c)


def rmsnorm_tile(cx, hT_t, hkey, gcol, gi, out_ap_fn, okey, n, c32, tmp):
    S = cx.S
    sq, rstd, epsc = tmp
    b = cx.bank()
    bk = cx.banks[b]
    for c in range(NCH):
        s = sq[c % 2]
        S.op('act', lambda e, s=s, c=c: e.activation(out=s[:, :n], in_=hT_t[:, c, :n], func=AF.Square),
             r=[hkey], w=[f'sq{c % 2}'])
        S.op('pe', lambda e, s=s, c=c: e.matmul(bk[:, :n], lhsT=c32[:, 640:768], rhs=s[:, :n],
                                                 start=(c == 0), stop=(c == NCH - 1)),
             r=[f'sq{c % 2}'], w=[f'bank{b}'])
    S.op('act', lambda e: e.activation(out=rstd[:, :n], in_=bk[:, :n], func=AF.Ln, bias=epsc[:, 0:1], scale=1.0 / D),
         r=[f'bank{b}'], w=['rstd'])
    S.op('act', lambda e: e.activation(out=rstd[:, :n], in_=rstd[:, :n], func=AF.Exp, scale=-0.5),
         r=['rstd'], w=['rstd'])
    for c in range(NCH):
        S.op('dve', lambda e, c=c: e.scalar_tensor_tensor(out=out_ap_fn(c), in0=hT_t[:, c, :n],
                                                          scalar=gcol[:, gi * NCH + c:gi * NCH + c + 1],
                                                          in1=rstd[:, :n], op0=ALU.mult, op1=ALU.mult),
             r=[hkey, 'rstd', 'gcol'], w=[okey])


def norm_tmp(cx, stack):
    sq = [cx.sb(stack, "sq", [128, 512], F32) for _ in range(2)]
    rstd = cx.sb(stack, "rstd", [128, 512], F32)
    epsc = cx.sb(stack, "epsc", [128, 1], F32)
    cx.S.op('dve', lambda e: e.memset(epsc[:], EPS), w=['epsc'])
    return sq, rstd, epsc


def phase_proj(cx, hT, gpc, w_in, bfg, cst, qk_out, v_out, logf_out, sg_out, nt=NT):
    S = cx.S
    nc = cx.nc
    ntile = nt // 512
    with ExitStack() as st:
        c32, c16 = load_consts(cx, st, cst)
        xnT = cx.sb(st, "xnT", [128, NCH, nt], BF16)
        gcol = cx.sb(st, "gcol", [128, NCH], F32)
        bcol = cx.sb(st, "bcol", [8, 1], F32)
        S.dma('sp', gcol[:], gpc, w=['gcol'], stream='gcol')
        S.dma('sp', bcol[:], bfg, w=['bcol'], stream='bcol')
        tmp = norm_tmp(cx, st)
        hview = hT.rearrange("(c p) t -> p c t", p=128)
        with ExitStack() as st2:
            hts = [cx.sb(st2, "ht", [128, NCH, 512], F32) for _ in range(2)]
            for tt in range(ntile):
                ht = hts[tt % 2]
                hk = f'ht{tt % 2}'
                S.dma('sp', ht[:], hview[:, :, tt * 512:(tt + 1) * 512], w=[hk], stream=hk)
                rmsnorm_tile(cx, ht, hk, gcol, 0, lambda c, tt=tt: xnT[:, c, tt * 512:(tt + 1) * 512],
                             'xnT', 512, c32, tmp)
        S.barrier()
        wview = w_in.rearrange("(c p) n -> p c n", p=128)
        wsl = [cx.sb(st, "wsl", [128, NCH, 512], BF16) for _ in range(2)]
        stg = [cx.sb(st, "stg", [128, 512], BF16) for _ in range(4)]
        stgf = cx.sb(st, "stgf", [8, 512], F32)
        wf = cx.sb(st, "wf", [128, NCH, 8], BF16)
        slabs = []
        for kind, base in enumerate([0, 1024, 3072, 4096]):
            for s2 in range(2):
                slabs.append((base + 512 * s2, 'qk', kind * 8 + 4 * s2))
        for kind, base in enumerate([2048, 5120]):
            for s2 in range(2):
                slabs.append((base + 512 * s2, 'v', (kind, s2)))
        for kind, base in enumerate([6152, 8200]):
            for s4 in range(4):
                slabs.append((base + 512 * s4, 'g', (kind, s4)))
        si = 0
        ev = 0

        def load_slab(i):
            col0 = slabs[i][0]
            S.dma('pool', wsl[i % 2][:], wview[:, :, col0:col0 + 512], w=[f'wsl{i % 2}'], stream=f'wsl{i % 2}')

        load_slab(0)
        S.dma('pool', wf[:], wview[:, :, 6144:6152], w=['wf'], stream='wf')
        for i, (col0, mode, meta) in enumerate(slabs):
            if i + 1 < len(slabs):
                load_slab(i + 1)
            w = wsl[i % 2]
            wk = f'wsl{i % 2}'
            if mode in ('qk', 'g'):
                for tt in range(ntile):
                    for cc in range(4):
                        b = cx.bank()
                        bk = cx.banks[b]
                        for dc in range(NCH):
                            S.op('pe', lambda e, dc=dc, cc=cc, tt=tt, bk=bk: e.matmul(
                                bk[:, :], lhsT=w[:, dc, cc * 128:(cc + 1) * 128],
                                rhs=xnT[:, dc, tt * 512:(tt + 1) * 512], start=(dc == 0), stop=(dc == NCH - 1)),
                                r=[wk, 'xnT'], w=[f'bank{b}'])
                        sg = stg[ev % 4]
                        sk = f'stg{ev % 4}'
                        if mode == 'qk':
                            if ev % 2 == 0:
                                S.op('act', lambda e, sg=sg, bk=bk: e.activation(out=sg[:], in_=bk[:], func=AF.Copy),
                                     r=[f'bank{b}'], w=[sk])
                            else:
                                S.op('dve', lambda e, sg=sg, bk=bk: e.tensor_copy(out=sg[:], in_=bk[:]),
                                     r=[f'bank{b}'], w=[sk])
                            dst = qk_out[meta + cc, :, tt * 512:(tt + 1) * 512]
                        else:
                            S.op('act', lambda e, sg=sg, bk=bk: e.activation(out=sg[:], in_=bk[:], func=AF.Sigmoid),
                                 r=[f'bank{b}'], w=[sk])
                            kind, s4 = meta
                            r0 = (s4 * 4 + cc) * 128
                            dst = sg_out[kind, r0:r0 + 128, tt * 512:(tt + 1) * 512]
                        S.dma('sp', dst, sg[:], r=[sk], w=[], stream=sk + 'o')
                        ev += 1
            else:
                kind, s2 = meta
                for stt in range(nt // 128):
                    b = cx.bank()
                    bk = cx.banks[b]
                    for dc in range(NCH):
                        S.op('pe', lambda e, dc=dc, stt=stt, bk=bk: e.matmul(
                            bk[:, :], lhsT=xnT[:, dc, stt * 128:(stt + 1) * 128], rhs=w[:, dc, :],
                            start=(dc == 0), stop=(dc == NCH - 1)), r=[wk, 'xnT'], w=[f'bank{b}'])
                    sg = stg[ev % 4]
                    sk = f'stg{ev % 4}'
                    if ev % 2 == 0:
                        S.op('act', lambda e, sg=sg, bk=bk: e.activation(out=sg[:], in_=bk[:], func=AF.Copy),
                             r=[f'bank{b}'], w=[sk])
                    else:
                        S.op('dve', lambda e, sg=sg, bk=bk: e.tensor_copy(out=sg[:], in_=bk[:]),
                             r=[f'bank{b}'], w=[sk])
                    S.dma('sp', v_out[kind, stt * 128:(stt + 1) * 128, s2 * 512:(s2 + 1) * 512], sg[:],
                          r=[sk], w=[], stream=sk + 'o')
                    ev += 1
        for tt in range(ntile):
            b = cx.bank()
            bk = cx.banks[b]
            for dc in range(NCH):
                S.op('pe', lambda e, dc=dc, tt=tt, bk=bk: e.matmul(
                    bk[0:8, :], lhsT=wf[:, dc, :], rhs=xnT[:, dc, tt * 512:(tt + 1) * 512],
                    start=(dc == 0), stop=(dc == NCH - 1)), r=['wf', 'xnT'], w=[f'bank{b}'])
            S.op('act', lambda e, bk=bk: e.activation(out=stgf[:], in_=bk[0:8, :], func=AF.Sigmoid,
                                                      bias=bcol[:, 0:1], scale=1.0),
                 r=[f'bank{b}', 'bcol'], w=['stgf'])
            S.op('act', lambda e: e.activation(out=stgf[:], in_=stgf[:], func=AF.Ln), r=['stgf'], w=['stgf'])
            S.dma('sp', logf_out[:, tt * 512:(tt + 1) * 512], stgf[:], r=['stgf'], w=[], stream='stgfo')
        S.barrier()


def build_proj(nt=NT):
    nc = bass.Bass("TRN2", target_bir_lowering=False)
    hT = nc.dram_tensor("hT", [D, nt], F32, kind="ExternalInput").ap()
    gpc = nc.dram_tensor("gpc", [128, NCH], F32, kind="ExternalInput").ap()
    w_in = nc.dram_tensor("w_in", [D, IN_COLS], F32, kind="ExternalInput").ap()
    bfg = nc.dram_tensor("bfg", [8, 1], F32, kind="ExternalInput").ap()
    cst = nc.dram_tensor("cst", [128, 768], F32, kind="ExternalInput").ap()
    qk = nc.dram_tensor("qk", [32, 128, nt], BF16, kind="ExternalOutput").ap()
    v = nc.dram_tensor("v", [2, nt, 1024], BF16, kind="ExternalOutput").ap()
    logf = nc.dram_tensor("logf", [8, nt], F32, kind="ExternalOutput").ap()
    sg = nc.dram_tensor("sg", [2, D, nt], BF16, kind="ExternalOutput").ap()
    with ExitStack() as stack:
        cx = Ctx(nc, stack)
        phase_proj(cx, hT, gpc, w_in, bfg, cst, qk, v, logf, sg, nt)
        cx.S.finish()
    return nc


def make_consts2():
    c = make_consts()
    sel = np.zeros((128, 8 * 128), np.float32)
    for h in range(8):
        sel[h, h * 128:(h + 1) * 128] = 1.0
    return np.ascontiguousarray(np.concatenate([c, sel], axis=1))


def load_consts2(cx, stack, cst):
    S = cx.S
    c32 = cx.sb(stack, "c32", [128, 1792], F32)
    c16 = cx.sb(stack, "c16", [128, 768], BF16)
    S.dma('sp', c32[:], cst, w=['c32'], stream='c32')
    S.op('dve', lambda e: e.tensor_copy(out=c16[:], in_=c32[:, 0:768]), r=['c32'], w=['c16'])
    return c32, c16


def attn_load_head(cx, st, ci, qT, kT, v, h, seq):
    S = cx.S
    nblk = seq // 128
    d = {}
    d['k'] = cx.sb(st, 'k', [128, seq], BF16)
    d['q'] = cx.sb(st, 'q', [128, seq], BF16)
    d['v'] = cx.sb(st, 'v', [128, nblk, 128], BF16)
    return d


def attn_issue_loads(cx, d, ci, qT, kT, v, h):
    S = cx.S
    S.dma('sp', d['k'][:], kT[h], w=[f'k{ci}'], stream=f'k{ci}')
    S.dma('sp', d['q'][:], qT[h], w=[f'q{ci}'], stream=f'q{ci}')
    S.dma('sp', d['v'][:], v[:, h * 128:(h + 1) * 128].rearrange("(b p) d -> p b d", p=128), w=[f'v{ci}'],
          stream=f'v{ci}')


def phase_attn(cx, qT_sb, kT_sb, v_sb, qT_fx, kT_fx, v_fx, logf, cst, oT, nh, seq):
    S = cx.S
    nblk = seq // 128
    nqc = seq // 512
    nchain = 2 if nh >= 2 else 1
    with ExitStack() as st:
        c32, c16 = load_consts2(cx, st, cst)
        ident32 = c32[:, 0:128]
        strict32 = c32[:, 128:256]
        strict16 = c16[:, 128:256]
        incl16 = c16[:, 256:384]
        negTri = c32[:, 384:512]
        negTriC = c32[:, 512:640]
        ones16 = c16[:, 640:768]
        onec = cx.sb(st, "onec", [128, 1], F32)
        S.op('dve', lambda e: e.memset(onec[:], 1.0), w=['onec'])
        chains = []
        for ci in range(nchain):
            d = attn_load_head(cx, st, ci, None, None, None, 0, seq)
            for nm in ('e', 'sp', 'tmp', 'ea'):
                d[nm] = [cx.sb(st, nm, [128, 512], F32) for _ in range(2)]
            d['P'] = [cx.sb(st, 'P', [128, 512], BF16) for _ in range(2)]
            d['bias'] = [cx.sb(st, 'bias', [128, 4], F32) for _ in range(2)]
            d['og'] = cx.sb(st, 'og', [128, 512], BF16)
            d['rd'] = cx.sb(st, 'rd', [128, 512], F32)
            d['Zs'], d['A'], d['O'] = (3 * ci, 6 + ci), 3 * ci + 1, 3 * ci + 2
            chains.append(d)

        Fsb = cx.sb(st, "Fsb", [nh, seq], F32)
        negF = cx.sb(st, "negF", [128, nblk * nh], F32)
        Fref = cx.sb(st, "Fref", [128, nh * nblk], F32)
        spl = [cx.sb(st, "spl", [nh, seq], BF16) for _ in range(3)]
        kaug = [cx.sb(st, "kaug", [6, seq], BF16) for _ in range(nchain)]
        qaug = [cx.sb(st, "qaug", [6, seq], BF16) for _ in range(nchain)]
        with ExitStack() as st2:
            lf = cx.sb(st2, "lf", [nh, seq], F32)
            onesr = cx.sb(st2, "onesr", [nh, seq], F32)
            S.dma('sp', lf[:], logf, w=['lf'], stream='lf')
            S.op('dve', lambda e: e.memset(onesr[:], 1.0), w=['onesr'])
            S.op('dve', lambda e: e.tensor_tensor_scan(out=Fsb[:], data0=onesr[:], data1=lf[:], initial=0.0,
                                                       op0=ALU.mult, op1=ALU.add), r=['lf', 'onesr'], w=['Fsb'])
            b6, b7 = cx.banks[6], cx.banks[7]
            for blk in range(nblk):
                S.op('pe', lambda e, blk=blk: e.matmul(b6[:, blk * nh:(blk + 1) * nh],
                                                       lhsT=Fsb[0:nh, blk * 128:(blk + 1) * 128],
                                                       rhs=c32[0:nh, 0:nh], start=True, stop=True),
                     r=['Fsb', 'c32'], w=['bank6'])
            S.op('dve', lambda e: e.tensor_scalar(out=negF[:], in0=b6[:, 0:nblk * nh], scalar1=-1.0, scalar2=None,
                                                  op0=ALU.mult), r=['bank6'], w=['negF'])
            for h in range(nh):
                S.op('pe', lambda e, h=h: e.matmul(b7[:, h * nblk:(h + 1) * nblk],
                                                   lhsT=c32[0:nh, 768 + h * 128:768 + (h + 1) * 128],
                                                   rhs=Fsb[0:nh, 64:seq:128], start=True, stop=True),
                     r=['Fsb', 'c32'], w=['bank7'])
            S.op('dve', lambda e: e.tensor_copy(out=Fref[:], in_=b7[:, 0:nh * nblk]), r=['bank7'], w=['Fref'])
            S.op('dve', lambda e: e.tensor_scalar(out=lf[:], in0=Fsb[:], scalar1=float(HD ** 0.5), scalar2=None,
                                                  op0=ALU.mult), r=['Fsb', 'lf'], w=['lf'])
            S.op('dve', lambda e: e.tensor_copy(out=spl[0][:], in_=lf[:]), r=['lf'], w=['spl0'])
            S.op('dve', lambda e: e.tensor_tensor(out=onesr[:], in0=lf[:], in1=spl[0][:], op=ALU.subtract),
                 r=['lf', 'spl0'], w=['onesr'])
            S.op('dve', lambda e: e.tensor_copy(out=spl[1][:], in_=onesr[:]), r=['onesr'], w=['spl1'])
            S.op('dve', lambda e: e.tensor_tensor(out=onesr[:], in0=onesr[:], in1=spl[1][:], op=ALU.subtract),
                 r=['onesr', 'spl1'], w=['onesr'])
            S.op('dve', lambda e: e.tensor_copy(out=spl[2][:], in_=onesr[:]), r=['onesr'], w=['spl2'])
            S.barrier()

        def run_group(kind, heads):
            qT, kT, v = (qT_sb, kT_sb, v_sb) if kind == 0 else (qT_fx, kT_fx, v_fx)
            act = list(enumerate(heads))
            for ci, h in act:
                attn_issue_loads(cx, chains[ci], ci, qT, kT, v, h)
            if kind == 1:
                for ci, h in act:
                    S.op('dve', lambda e, ci=ci: e.memset(kaug[ci][:], 1.0), w=[f'kaug{ci}'])
                    S.op('dve', lambda e, ci=ci: e.memset(qaug[ci][:], -1.0), w=[f'qaug{ci}'])
                    for r3 in range(3):
                        S.dma('sp', kaug[ci][3 + r3:4 + r3, :], spl[r3][h:h + 1, :], r=[f'spl{r3}'], w=[f'kaug{ci}'],
                              stream=f'ka{ci}_{r3}')
                        S.dma('sp', qaug[ci][r3:r3 + 1, :], spl[r3][h:h + 1, :], r=[f'spl{r3}'], w=[f'qaug{ci}'],
                              stream=f'qa{ci}_{r3}')
            tiles = [(qc, ti, kb) for qc in range(nqc) for ti, kb in enumerate(range(4 * qc + 3, -1, -1))]

            def emit_qk(idx):
                qc_, ti_, kb_ = tiles[idx]
                c0_ = 128 * (kb_ - 4 * qc_) if kb_ >= 4 * qc_ else 0
                t0_ = qc_ * 512
                for ci, h in act:
                    d = chains[ci]
                    zb = d['Zs'][idx % 2]
                    Z = cx.banks[zb]
                    S.op('pe', lambda e, d=d, Z=Z: e.matmul(Z[:, c0_:512], lhsT=d['k'][:, kb_ * 128:(kb_ + 1) * 128],
                                                             rhs=d['q'][:, t0_ + c0_:t0_ + 512], start=True,
                                                             stop=(kind == 0)),
                         r=[f'k{ci}', f'q{ci}'], w=[f"bank{zb}"])
                    if kind == 1:
                        S.op('pe', lambda e, ci=ci, Z=Z: e.matmul(Z[:, c0_:512],
                                                                   lhsT=kaug[ci][:, kb_ * 128:(kb_ + 1) * 128],
                                                                   rhs=qaug[ci][:, t0_ + c0_:t0_ + 512], start=False,
                                                                   stop=True),
                             r=[f'kaug{ci}', f'qaug{ci}'], w=[f"bank{zb}"])

            emit_qk(0)
            tix = 0
            for qc in range(nqc):
                blocks = list(range(4 * qc + 3, -1, -1))
                for ti, kb in enumerate(blocks):
                    par = tix % 2
                    tix += 1
                    for d_ in chains:
                        d_['Z'] = d_['Zs'][par]
                    diag = kb >= 4 * qc
                    j = kb - 4 * qc if diag else 0
                    c0 = 128 * j
                    first = ti == 0
                    last = kb == 0
                    t0 = qc * 512
                    if tix < len(tiles):
                        emit_qk(tix)
                    if kind == 0:
                        for ci, h in act:
                            d = chains[ci]
                            Z = cx.banks[d['Z']]
                            e_, sp_ = d['e'][par], d['sp'][par]
                            S.op('act', lambda e, Z=Z, e_=e_: e.activation(out=e_[:, c0:512], in_=Z[:, c0:512],
                                                                           func=AF.Exp, scale=SCALE),
                                 r=[f"bank{d['Z']}"], w=[f'e{ci}_{par}'])
                            S.op('act', lambda e, e_=e_, sp_=sp_: e.activation(out=sp_[:, c0:512], in_=e_[:, c0:512],
                                                                               func=AF.Ln, bias=onec[:, 0:1], scale=1.0),
                                 r=[f'e{ci}_{par}', 'onec'], w=[f'sp{ci}_{par}'])
                            if diag:
                                S.op('pool', lambda e, sp_=sp_: e.tensor_tensor(out=sp_[:, c0:c0 + 128],
                                                                                in0=sp_[:, c0:c0 + 128], in1=strict32,
                                                                                op=ALU.mult),
                                     r=[f'sp{ci}_{par}', 'c32'], w=[f'sp{ci}_{par}'])
                        for ci, h in act:
                            d = chains[ci]
                            A = cx.banks[d['A']]
                            sp_ = d['sp'][par]
                            S.op('pe', lambda e, A=A, sp_=sp_: e.matmul(A[:, c0:512], lhsT=negTri, rhs=sp_[:, c0:512],
                                                                         start=first, stop=True),
                                 r=[f'sp{ci}_{par}', 'c32'], w=[f"bank{d['A']}"])
                        for ci, h in act:
                            d = chains[ci]
                            A = cx.banks[d['A']]
                            sp_, tmp_ = d['sp'][par], d['tmp'][par]
                            S.op('dve', lambda e, A=A, sp_=sp_, tmp_=tmp_: e.tensor_tensor(
                                out=tmp_[:, c0:512], in0=A[:, c0:512], in1=sp_[:, c0:512], op=ALU.subtract),
                                r=[f"bank{d['A']}", f'sp{ci}_{par}'], w=[f'tmp{ci}_{par}'])
                        for ci, h in act:
                            d = chains[ci]
                            tmp_, ea_ = d['tmp'][par], d['ea'][par]
                            S.op('act', lambda e, tmp_=tmp_, ea_=ea_: e.activation(out=ea_[:, c0:512],
                                                                                   in_=tmp_[:, c0:512], func=AF.Exp),
                                 r=[f'tmp{ci}_{par}'], w=[f'ea{ci}_{par}'])
                        for ci, h in act:
                            d = chains[ci]
                            e_, ea_, P_ = d['e'][par], d['ea'][par], d['P'][par]
                            S.op('dve', lambda e, e_=e_, ea_=ea_, P_=P_: e.tensor_tensor(
                                out=P_[:, c0:512], in0=e_[:, c0:512], in1=ea_[:, c0:512], op=ALU.mult),
                                r=[f'e{ci}_{par}', f'ea{ci}_{par}'], w=[f'P{ci}_{par}'])
                            if diag:
                                S.op('pool', lambda e, P_=P_: e.tensor_tensor(out=P_[:, c0:c0 + 128],
                                                                              in0=P_[:, c0:c0 + 128], in1=strict16,
                                                                              op=ALU.mult),
                                     r=[f'P{ci}_{par}', 'c16'], w=[f'P{ci}_{par}'])
                        for ci, h in act:
                            d = chains[ci]
                            O = cx.banks[d['O']]
                            A = cx.banks[d['A']]
                            P_, sp_ = d['P'][par], d['sp'][par]
                            S.op('pe', lambda e, O=O, P_=P_, d=d: e.matmul(O[:, c0:512], lhsT=d['v'][:, kb, :],
                                                                            rhs=P_[:, c0:512], start=first, stop=last),
                                 r=[f'P{ci}_{par}', f'v{ci}'], w=[f"bank{d['O']}"])
                            if not last:
                                S.op('pe', lambda e, A=A, sp_=sp_: e.matmul(A[:, c0:512], lhsT=negTriC,
                                                                             rhs=sp_[:, c0:512], start=False, stop=True),
                                     r=[f'sp{ci}_{par}', 'c32', f'tmp{ci}_{par}'], w=[f"bank{d['A']}"])
                    else:
                        for ci, h in act:
                            d = chains[ci]
                            Z = cx.banks[d['Z']]
                            bs, P_ = d['bias'][par], d['P'][par]
                            S.op('act', lambda e, Z=Z, P_=P_: e.activation(
                                out=P_[:, c0:512], in_=Z[:, c0:512], func=AF.Exp, scale=SCALE),
                                r=[f"bank{d['Z']}"], w=[f'P{ci}_{par}'])
                            if diag:
                                S.op('pool', lambda e, P_=P_: e.tensor_tensor(out=P_[:, c0:c0 + 128],
                                                                              in0=P_[:, c0:c0 + 128], in1=incl16,
                                                                              op=ALU.mult),
                                     r=[f'P{ci}_{par}', 'c16'], w=[f'P{ci}_{par}'])
                        for ci, h in act:
                            d = chains[ci]
                            O = cx.banks[d['O']]
                            A = cx.banks[d['A']]
                            P_ = d['P'][par]
                            S.op('pe', lambda e, O=O, P_=P_, d=d: e.matmul(O[:, c0:512], lhsT=d['v'][:, kb, :],
                                                                            rhs=P_[:, c0:512], start=first, stop=last),
                                 r=[f'P{ci}_{par}', f'v{ci}'], w=[f"bank{d['O']}"])
                            S.op('pe', lambda e, A=A, P_=P_: e.matmul(A[:, c0:512], lhsT=ones16, rhs=P_[:, c0:512],
                                                                       start=first, stop=last),
                                 r=[f'P{ci}_{par}', 'c16'], w=[f"bank{d['A']}"])
                for ci, h in act:
                    d = chains[ci]
                    O = cx.banks[d['O']]
                    A = cx.banks[d['A']]
                    if kind == 0:
                        S.op('act', lambda e, O=O, d=d: e.activation(out=d['og'][:], in_=O[:], func=AF.Copy),
                             r=[f"bank{d['O']}"], w=[f'og{ci}'])
                    else:
                        S.op('dve', lambda e, A=A, d=d: e.reciprocal(out=d['rd'][:], in_=A[:]),
                             r=[f"bank{d['A']}"], w=[f'rd{ci}'])
                        S.op('dve', lambda e, O=O, d=d: e.tensor_tensor(out=d['og'][:], in0=O[:], in1=d['rd'][:],
                                                                        op=ALU.mult),
                             r=[f"bank{d['O']}", f'rd{ci}'], w=[f'og{ci}'])
                    S.dma('sp', oT[kind, h * 128:(h + 1) * 128, qc * 512:(qc + 1) * 512], d['og'][:],
                          r=[f'og{ci}'], stream=f'og{ci}o')

        for kind in (0, 1):
            for h0 in range(0, nh, nchain):
                run_group(kind, list(range(h0, min(nh, h0 + nchain))))
        S.barrier()


def build_attn(nh=4, seq=SEQ):
    nc = bass.Bass("TRN2", target_bir_lowering=False)
    aps = {}
    for nm in ("qT_sb", "kT_sb", "qT_fx", "kT_fx"):
        aps[nm] = nc.dram_tensor(nm, [nh, 128, seq], BF16, kind="ExternalInput").ap()
    for nm in ("v_sb", "v_fx"):
        aps[nm] = nc.dram_tensor(nm, [seq, nh * 128], BF16, kind="ExternalInput").ap()
    logf = nc.dram_tensor("logf", [nh, seq], F32, kind="ExternalInput").ap()
    cst = nc.dram_tensor("cst", [128, 1792], F32, kind="ExternalInput").ap()
    oT = nc.dram_tensor("oT", [2, nh * 128, seq], BF16, kind="ExternalOutput").ap()
    with ExitStack() as stack:
        cx = Ctx(nc, stack)
        phase_attn(cx, aps["qT_sb"], aps["kT_sb"], aps["v_sb"], aps["qT_fx"], aps["kT_fx"], aps["v_fx"], logf, cst,
                   oT, nh, seq)
        cx.S.finish()
    return nc


def linear_T(cx, wview, kc, ncols, rhs_fn, rkeys, wsl, evac, n=512, pre=None):
    S = cx.S
    nsl = ncols // 512

    def load(i):
        S.dma('pool', wsl[i % 2][:, 0:kc, :], wview[:, :, i * 512:(i + 1) * 512],
              w=[f'wsl{i % 2}'], stream=f'wsl{i % 2}')

    load(0)
    for i in range(nsl):
        if i + 1 < nsl:
            load(i + 1)
        if pre is not None:
            pre(i)
        w = wsl[i % 2]
        for cc in range(4):
            b = cx.bank2()
            bk = cx.banks[b]
            for dc in range(kc):
                S.op('pe', lambda e, dc=dc, cc=cc, bk=bk: e.matmul(bk[:, :n], lhsT=w[:, dc, cc * 128:(cc + 1) * 128],
                                                                    rhs=rhs_fn(dc), start=(dc == 0), stop=(dc == kc - 1)),
                     r=[f'wsl{i % 2}'] + rkeys, w=[f'bank{b}'])
            evac(i * 4 + cc, b, bk)


def phase_post(cx, hT, oT, sg, w_bsb, w_bfx, w_out, w_query, skT, uT, ev, w_pg, w_ple, pT, gpc, cst, hT_out,
               fin_out, nt=NT, nexp=NEXP, sel=None):
    S = cx.S
    ntile = nt // 512
    nich = nexp // 128
    assert nich == 128
    GRP = 4
    ngrp = nich // GRP
    cx.bank2 = lambda: 4 + (cx.bank() % 4)
    with ExitStack() as st:
        c32, c16 = load_consts(cx, st, cst)
        ident16 = c16[:, 0:128]
        gcol = cx.sb(st, "gcol", [128, 48], F32)
        S.dma('sp', gcol[:], gpc, w=['gcol'], stream='gcol')
        tmpn = norm_tmp(cx, st)
        msel = None
        if sel is not None:
            msel = cx.sb(st, "msel", [128, 2], F32)
            S.dma('sp', msel[:], sel['m'], w=['msel'], stream='msel')
        hacc = cx.sb(st, "hacc", [128, NCH, 512], F32)
        xnb = cx.sb(st, "xnb", [128, NCH, 512], BF16)
        E = [cx.sb(st, "E", [128, 16, 128], F32) for _ in range(4)]
        thr = [cx.sb(st, "thr", [128, 8], F32) for _ in range(4)]
        Dg = [cx.sb(st, "Dg", [128, 8, 128], BF16) for _ in range(4)]
        hview = hT.rearrange("(c p) t -> p c t", p=128)
        oview = hT_out.rearrange("(c p) t -> p c t", p=128)
        fview = fin_out.rearrange("(c p) t -> p c t", p=128) if fin_out is not None else None
        wv = lambda w: w.rearrange("(c p) n -> p c n", p=128)
        for tt in range(ntile):
            tsl = slice(tt * 512, (tt + 1) * 512)
            S.dma('sp', hacc[:], hview[:, :, tsl], w=['hacc'], stream='hacc')
            if sel is not None:
                h1view = sel['hT1'].rearrange("(c p) t -> p c t", p=128)
                for k4 in range(4):
                    S.dma('sp', E[k4][:].rearrange("p a k -> p (a k)").rearrange("p (c t) -> p c t", c=4),
                          h1view[:, 4 * k4:4 * k4 + 4, tsl], w=[f'E{k4}'], stream=f'E{k4}')
                S.op('dve', lambda e: e.tensor_scalar(out=hacc[:].rearrange("p c t -> p (c t)"),
                                                      in0=hacc[:].rearrange("p c t -> p (c t)"),
                                                      scalar1=msel[:, 0:1], scalar2=None, op0=ALU.mult),
                     r=['hacc', 'msel'], w=['hacc'])
                for k4 in range(4):
                    hv = hacc[:, 4 * k4:4 * k4 + 4, :].rearrange("p c t -> p (c t)")
                    S.op('dve', lambda e, k4=k4, hv=hv: e.scalar_tensor_tensor(
                        out=hv, in0=E[k4][:].rearrange("p a k -> p (a k)"), scalar=msel[:, 1:2], in1=hv,
                        op0=ALU.mult, op1=ALU.add), r=[f'E{k4}', 'hacc', 'msel'], w=['hacc'])
            with ExitStack() as s2:
                wsl = [cx.sb(s2, "wsl", [128, NCH, 512], BF16) for _ in range(2)]
                mrg = cx.sb(s2, "mrg", [128, NCH, 512], BF16)
                ot = cx.sb(s2, "ot", [128, 16, 512], BF16)
                sgs = cx.sb(s2, "sgs", [128, 2, 4, 512], BF16)
                t2 = cx.sb(s2, "t2", [128, 512], F32)
                skb = cx.sb(s2, "skb", [128, 16, 128], BF16)
                top = cx.sb(s2, "top", [128, 16, 16], F32)
                negm = cx.sb(s2, "negm", [128, 16], F32)
                d16 = cx.sb(s2, "d16", [128, 16], F32)
                work = cx.sb(s2, "work", [128, 256], F32)
                cand = cx.sb(s2, "cand", [128, 8, 256], F32)
                ctop = cx.sb(s2, "ctop", [128, 8, 24], F32)
                sm = cx.sb(s2, "sm", [128, 8, 4], F32)
                ez = cx.sb(s2, "ez", [128, 8, 16], F32)
                S.dma('pool', skb[:], skT.rearrange("a c k -> c a k"), w=['skb'], stream='skb')
                S.dma('sp', ot[:], oT.rearrange("k (c p) t -> p (k c) t", p=128)[:, :, tsl], w=['ot'], stream='ot')
                sgv = sg.rearrange("k (c p) t -> p k c t", p=128)
                if sel is not None:
                    sgs2 = cx.sb(s2, "sgs2", [128, 2, 4, 512], BF16)
                    sgv1 = sel['sg1'].rearrange("k (c p) t -> p k c t", p=128)
                    S.dma('sp', mrg[:], sel['oT1'].rearrange("k (c p) t -> p (k c) t", p=128)[:, :, tsl], w=['mrg'],
                          stream='mrgl')
                    of, mf = ot[:].rearrange("p c t -> p (c t)"), mrg[:].rearrange("p c t -> p (c t)")
                    S.op('dve', lambda e: e.tensor_scalar(out=of, in0=of, scalar1=msel[:, 0:1], scalar2=None,
                                                          op0=ALU.mult), r=['ot', 'msel'], w=['ot'])
                    S.op('dve', lambda e: e.scalar_tensor_tensor(out=of, in0=mf, scalar=msel[:, 1:2], in1=of,
                                                                 op0=ALU.mult, op1=ALU.add),
                         r=['mrg', 'ot', 'msel'], w=['ot'])
                bsv, bfv = wv(w_bsb), wv(w_bfx)

                def load_b(i):
                    S.dma('pool', wsl[i % 2][:, 0:8, :], bsv[:, :, i * 512:(i + 1) * 512], w=[f'wsl{i % 2}'],
                          stream=f'wsl{i % 2}')
                    S.dma('pool', wsl[i % 2][:, 8:16, :], bfv[:, :, i * 512:(i + 1) * 512], w=[f'wsl{i % 2}'],
                          stream=f'wsl{i % 2}b')

                load_b(0)
                for i in range(4):
                    if i + 1 < 4:
                        load_b(i + 1)
                    S.dma('sp', sgs[:, 0], sgv[:, 0, 4 * i:4 * i + 4, tsl], w=['sgs'], stream='sgs')
                    S.dma('sp', sgs[:, 1], sgv[:, 1, 4 * i:4 * i + 4, tsl], w=['sgs'], stream='sgsb')
                    if sel is not None:
                        S.dma('sp', sgs2[:, 0], sgv1[:, 0, 4 * i:4 * i + 4, tsl], w=['sgs2'], stream='sgs2')
                        S.dma('sp', sgs2[:, 1], sgv1[:, 1, 4 * i:4 * i + 4, tsl], w=['sgs2'], stream='sgs2b')
                        sf = sgs[:].rearrange("p k c t -> p (k c t)")
                        sf2 = sgs2[:].rearrange("p k c t -> p (k c t)")
                        S.op('dve', lambda e, sf=sf: e.tensor_scalar(out=sf, in0=sf, scalar1=msel[:, 0:1], scalar2=None,
                                                                     op0=ALU.mult), r=['sgs', 'msel'], w=['sgs'])
                        S.op('dve', lambda e, sf=sf, sf2=sf2: e.scalar_tensor_tensor(
                            out=sf, in0=sf2, scalar=msel[:, 1:2], in1=sf, op0=ALU.mult, op1=ALU.add),
                            r=['sgs2', 'sgs', 'msel'], w=['sgs'])
                    w = wsl[i % 2]
                    for cc in range(4):
                        c = 4 * i + cc
                        b1, b2 = cx.bank2(), cx.bank2()
                        k1, k2 = cx.banks[b1], cx.banks[b2]
                        for dc in range(8):
                            S.op('pe', lambda e, dc=dc, cc=cc, k1=k1: e.matmul(
                                k1[:], lhsT=w[:, dc, cc * 128:(cc + 1) * 128], rhs=ot[:, dc, :],
                                start=(dc == 0), stop=(dc == 7)), r=[f'wsl{i % 2}', 'ot'], w=[f'bank{b1}'])
                        for dc in range(8):
                            S.op('pe', lambda e, dc=dc, cc=cc, k2=k2: e.matmul(
                                k2[:], lhsT=w[:, 8 + dc, cc * 128:(cc + 1) * 128], rhs=ot[:, 8 + dc, :],
                                start=(dc == 0), stop=(dc == 7)), r=[f'wsl{i % 2}', 'ot'], w=[f'bank{b2}'])
                        S.op('dve', lambda e, k1=k1, cc=cc: e.tensor_tensor(out=t2[:], in0=k1[:], in1=sgs[:, 0, cc, :],
                                                                            op=ALU.mult),
                             r=[f'bank{b1}', 'sgs'], w=['t2'])
                        S.op('dve', lambda e, k2=k2, cc=cc, c=c: e.tensor_tensor(out=mrg[:, c, :], in0=k2[:],
                                                                                 in1=sgs[:, 1, cc, :], op=ALU.mult),
                             r=[f'bank{b2}', 'sgs'], w=['mrg'])
                        S.op('dve', lambda e, c=c: e.tensor_tensor(out=mrg[:, c, :], in0=mrg[:, c, :], in1=t2[:],
                                                                   op=ALU.add), r=['mrg', 't2'], w=['mrg'])

                def ev_out(c, b, bk):
                    S.op('dve', lambda e: e.tensor_tensor(out=hacc[:, c, :], in0=bk[:], in1=hacc[:, c, :], op=ALU.add),
                         r=[f'bank{b}', 'hacc'], w=['hacc'])

                linear_T(cx, wv(w_out), NCH, D, lambda dc: mrg[:, dc, :], ['mrg'], wsl, ev_out)
                rmsnorm_tile(cx, hacc, 'hacc', gcol, 0, lambda c: xnb[:, c, :], 'xnb', 512, c32, tmpn)
                qpT = mrg

                def ev_q(c, b, bk):
                    if c % 2 == 0:
                        S.op('act', lambda e: e.activation(out=qpT[:, c, :], in_=bk[:], func=AF.Copy),
                             r=[f'bank{b}'], w=['mrg'])
                    else:
                        S.op('dve', lambda e: e.tensor_copy(out=qpT[:, c, :], in_=bk[:]), r=[f'bank{b}'], w=['mrg'])

                linear_T(cx, wv(w_query), NCH, D, lambda dc: xnb[:, dc, :], ['xnb'], wsl, ev_q)
                for stt in range(4):
                    sbanks = []
                    for g4 in range(4):
                        b = cx.bank2()
                        bk = cx.banks[b]
                        sbanks.append((b, bk))
                        for q4 in range(4):
                            hp = g4 * 4 + q4
                            S.op('pe', lambda e, hp=hp, q4=q4, bk=bk: e.matmul(
                                bk[:, q4 * 128:(q4 + 1) * 128], lhsT=qpT[:, hp, stt * 128:(stt + 1) * 128],
                                rhs=skb[:, hp, :], start=True, stop=True), r=['mrg', 'skb'], w=[f'bank{b}'])
                    for hp in range(16):
                        b, bk = sbanks[hp // 4]
                        sc = bk[:, (hp % 4) * 128:(hp % 4 + 1) * 128]
                        S.op('dve', lambda e, sc=sc, hp=hp: e.max(out=top[:, hp, 0:8], in_=sc), r=[f'bank{b}'], w=['top'])
                        S.op('dve', lambda e, sc=sc, hp=hp: e.match_replace(out=work[:, 0:128], in_to_replace=top[:, hp, 0:8],
                                                                            in_values=sc, imm_value=-1e30),
                             r=[f'bank{b}', 'top'], w=['work'])
                        S.op('dve', lambda e, hp=hp: e.max(out=top[:, hp, 8:16], in_=work[:, 0:128]), r=['work'], w=['top'])
                    S.op('dve', lambda e: e.tensor_scalar(out=negm[:], in0=top[:, :, 0], scalar1=-1.0, scalar2=None,
                                                          op0=ALU.mult), r=['top'], w=['negm'])
                    S.op('dve', lambda e: e.tensor_tensor(out=d16[:], in0=top[:, :, 15], in1=negm[:], op=ALU.add),
                         r=['top', 'negm'], w=['d16'])
                    S.op('dve', lambda e: e.tensor_scalar(out=d16[:], in0=d16[:], scalar1=-2e-6, scalar2=None,
                                                          op0=ALU.add), r=['d16'], w=['d16'])
                    S.op('act', lambda e: e.activation(out=d16[:], in_=d16[:], func=AF.Exp), r=['d16'], w=['d16'])
                    for hp in range(16):
                        b, bk = sbanks[hp // 4]
                        sc = bk[:, (hp % 4) * 128:(hp % 4 + 1) * 128]
                        S.op('act', lambda e, sc=sc, hp=hp: e.activation(out=E[stt][:, hp, :], in_=sc, func=AF.Exp,
                                                                         bias=negm[:, hp:hp + 1], scale=1.0),
                             r=[f'bank{b}', 'negm'], w=[f'E{stt}'])
                        S.op('dve', lambda e, hp=hp: e.scalar_tensor_tensor(
                            out=E[stt][:, hp, :], in0=E[stt][:, hp, :], scalar=d16[:, hp:hp + 1], in1=E[stt][:, hp, :],
                            op0=ALU.is_ge, op1=ALU.mult), r=[f'E{stt}', 'd16'], w=[f'E{stt}'])
                    tv = top[:].rearrange("p (h two) a -> p h two a", two=2)
                    S.op('dve', lambda e: e.tensor_tensor(
                        out=cand[:].rearrange("p h (a b) -> p h a b", a=16),
                        in0=tv[:, :, 0, :].unsqueeze(3).to_broadcast([128, 8, 16, 16]),
                        in1=tv[:, :, 1, :].unsqueeze(2).to_broadcast([128, 8, 16, 16]), op=ALU.add),
                        r=['top'], w=['cand'])
                    for h in range(8):
                        S.op('dve', lambda e, h=h: e.max(out=ctop[:, h, 0:8], in_=cand[:, h, :]), r=['cand'], w=['ctop'])
                        S.op('dve', lambda e, h=h: e.match_replace(out=work[:], in_to_replace=ctop[:, h, 0:8],
                                                                   in_values=cand[:, h, :], imm_value=-1e30),
                             r=['cand', 'ctop'], w=['work'])
                        S.op('dve', lambda e, h=h: e.max(out=ctop[:, h, 8:16], in_=work[:]), r=['work'], w=['ctop'])
                        S.op('dve', lambda e, h=h: e.match_replace(out=work[:], in_to_replace=ctop[:, h, 8:16],
                                                                   in_values=work[:], imm_value=-1e30),
                             r=['work', 'ctop'], w=['work'])
                        S.op('dve', lambda e, h=h: e.max(out=ctop[:, h, 16:24], in_=work[:]), r=['work'], w=['ctop'])
                    S.op('dve', lambda e: e.tensor_scalar(out=sm[:, :, 0], in0=ctop[:, :, 0], scalar1=-1.0, scalar2=None,
                                                          op0=ALU.mult), r=['ctop'], w=['sm'])
                    S.op('dve', lambda e: e.tensor_tensor(out=sm[:, :, 3], in0=ctop[:, :, 15], in1=ctop[:, :, 16],
                                                          op=ALU.add), r=['ctop'], w=['sm'])
                    S.op('dve', lambda e: e.scalar_tensor_tensor(out=sm[:, :, 3], in0=sm[:, :, 3], scalar=0.5,
                                                                 in1=sm[:, :, 0], op0=ALU.mult, op1=ALU.add),
                         r=['sm'], w=['sm'])
                    for h in range(8):
                        S.op('act', lambda e, h=h: e.activation(out=ez[:, h, :], in_=ctop[:, h, 0:16], func=AF.Exp,
                                                                bias=sm[:, h, 0:1], scale=1.0),
                             r=['ctop', 'sm'], w=['ez'])
                    S.op('dve', lambda e: e.tensor_reduce(out=sm[:, :, 1], in_=ez[:], axis=mybir.AxisListType.X,
                                                          op=ALU.add), r=['ez'], w=['sm'])
                    S.op('dve', lambda e: e.reciprocal(out=sm[:, :, 2], in_=sm[:, :, 1]), r=['sm'], w=['sm'])
                    S.op('act', lambda e: e.activation(out=thr[stt][:], in_=sm[:, :, 3], func=AF.Exp), r=['sm'],
                         w=[f'thr{stt}'])
                    S.op('dve', lambda e: e.tensor_tensor(out=thr[stt][:], in0=thr[stt][:], in1=sm[:, :, 2], op=ALU.mult),
                         r=[f'thr{stt}', 'sm'], w=[f'thr{stt}'])
                    S.op('act', lambda e: e.activation(out=sm[:, :, 1], in_=sm[:, :, 3], func=AF.Exp, scale=-1.0),
                         r=['sm'], w=['sm'])
                    Ev = E[stt][:].rearrange("p (h two) k -> p h two k", two=2)
                    S.op('dve', lambda e, Ev=Ev: e.tensor_tensor(out=Ev[:, :, 0, :], in0=Ev[:, :, 0, :],
                                                                 in1=sm[:, :, 1].unsqueeze(2).to_broadcast([128, 8, 128]),
                                                                 op=ALU.mult), r=[f'E{stt}', 'sm'], w=[f'E{stt}'])
                    for h in range(8):
                        S.op('dve', lambda e, h=h: e.tensor_scalar(out=Dg[stt][:, h, :], in0=c32[:, 0:128],
                                                                   scalar1=thr[stt][:, h:h + 1], scalar2=None,
                                                                   op0=ALU.mult), r=['c32', f'thr{stt}'], w=[f'Dg{stt}'])
                S.barrier()
            with ExitStack() as s3:
                usl = [cx.sb(s3, "usl", [128, NCH, GRP * 128], BF16) for _ in range(2)]
                vsl = [cx.sb(s3, "vsl", [128, GRP, D], BF16) for _ in range(2)]
                Gm = [[cx.sb(s3, "Gm", [128, 4 * GRP * 128], BF16) for _ in range(2)] for _ in range(2)]
                Pt = [cx.sb(s3, "Pt", [128, 4 * GRP * 128], F32) for _ in range(1)]
                oev = [cx.sb(s3, "oev", [128, 512], F32) for _ in range(2)]
                gl = [cx.sb(s3, "gl", [128, 512], F32) for _ in range(GRP)]
                WT = [cx.sb(s3, "WT", [128, 512], BF16) for _ in range(GRP)]
                uview = uT.rearrange("(c p) e -> p c e", p=128)
                vview = ev.rearrange("(g c p) d -> g p c d", p=128, c=GRP)
                st8 = {'pc': 0, 'hb': 0, 'ob': 0}
                obank = {}

                def load_u(g):
                    S.dma('pool', usl[g % 2][:], uview[:, :, g * GRP * 128:(g + 1) * GRP * 128], w=[f'usl{g % 2}'],
                          stream=f'usl{g % 2}')

                def load_v(g):
                    S.dma('pool', vsl[g % 2][:], vview[g], w=[f'vsl{g % 2}'], stream=f'vsl{g % 2}')

                def emit_hidden(g):
                    u = usl[g % 2]
                    for ic in range(GRP):
                        hb = 4 + st8['hb'] % 2
                        st8['hb'] += 1
                        hbk = cx.banks[hb]
                        for dc in range(NCH):
                            S.op('pe', lambda e, dc=dc, ic=ic, hbk=hbk: e.matmul(
                                hbk[:], lhsT=u[:, dc, ic * 128:(ic + 1) * 128], rhs=xnb[:, dc, :],
                                start=(dc == 0), stop=(dc == NCH - 1)), r=[f'usl{g % 2}', 'xnb'], w=[f'bank{hb}'])
                        S.op('act', lambda e, ic=ic, hbk=hbk: e.activation(out=gl[ic][:], in_=hbk[:], func=AF.Gelu),
                             r=[f'bank{hb}'], w=[f'gl{ic}'])

                def emit_gm(g, stt, half):
                    i0 = g * GRP
                    par = stt % 2
                    P = Pt[0]
                    pk = "Pt0"
                    st8['pc'] += 1
                    Ev = E[stt][:].rearrange("p (h two) k -> p h two k", two=2)
                    gk = f'Gm{par}_{half}'
                    S.op('dve', lambda e, P=P: e.tensor_tensor(
                        out=P[:].rearrange("p (h a k) -> p h a k", h=4, a=GRP),
                        in0=Ev[:, 4 * half:4 * half + 4, 0, i0:i0 + GRP].unsqueeze(3).to_broadcast([128, 4, GRP, 128]),
                        in1=Ev[:, 4 * half:4 * half + 4, 1, :].unsqueeze(2).to_broadcast([128, 4, GRP, 128]),
                        op=ALU.mult), r=[f'E{stt}'], w=[pk])
                    S.op('dve', lambda e, P=P: e.scalar_tensor_tensor(
                        out=Gm[par][half][:], in0=P[:], scalar=1.0, in1=P[:], op0=ALU.is_ge, op1=ALU.mult),
                        r=[pk], w=[gk])
                    if half == 1:
                        for ic in range(GRP):
                            bk = cx.banks[ic]
                            for h in range(8):
                                S.op('pe', lambda e, bk=bk, h=h, ic=ic: e.matmul(
                                    bk[:, stt * 128:(stt + 1) * 128],
                                    lhsT=Gm[par][h // 4][:, ((h % 4) * GRP + ic) * 128:((h % 4) * GRP + ic + 1) * 128],
                                    rhs=Dg[stt][:, h, :], start=(h == 0), stop=(h == 7)),
                                    r=[f'Gm{par}_{h // 4}', f'Dg{stt}'], w=[f'bank{ic}'])

                def emit_wt(g):
                    for ic in range(GRP):
                        S.op('dve', lambda e, ic=ic: e.tensor_tensor(out=WT[ic][:], in0=cx.banks[ic][:], in1=gl[ic][:],
                                                                     op=ALU.mult),
                             r=[f'bank{ic}', f'gl{ic}'], w=[f'WT{ic}'])

                def emit_out(g, dcgs):
                    v = vsl[g % 2]
                    for dcg in dcgs:
                        ob = 6 + st8['ob'] % 2
                        st8['ob'] += 1
                        obank[dcg] = ob
                        obk = cx.banks[ob]
                        for ic in range(GRP):
                            S.op('pe', lambda e, ic=ic, dcg=dcg, obk=obk: e.matmul(
                                obk[:], lhsT=v[:, ic, dcg * 128:(dcg + 1) * 128], rhs=WT[ic][:],
                                start=(ic == 0), stop=(ic == GRP - 1)),
                                r=[f'vsl{g % 2}', f'WT{ic}'], w=[f'bank{ob}'])

                def emit_add(g, dcgs):
                    for dcg in dcgs:
                        ob = obank[dcg]
                        obk = cx.banks[ob]
                        oe = oev[dcg % 2]
                        ok = f'oev{dcg % 2}'
                        S.op('act', lambda e, obk=obk, oe=oe: e.activation(out=oe[:], in_=obk[:], func=AF.Copy),
                             r=[f'bank{ob}'], w=[ok])
                        S.op('pool', lambda e, dcg=dcg, oe=oe: e.tensor_tensor(out=hacc[:, dcg, :], in0=oe[:],
                                                                               in1=hacc[:, dcg, :], op=ALU.add),
                             r=[ok, f'hacc{dcg}'], w=[f'hacc{dcg}'])

                load_u(0)
                load_v(0)
                load_u(1)
                load_v(1)
                emit_hidden(0)
                for stt in range(4):
                    emit_gm(0, stt, 0)
                    emit_gm(0, stt, 1)
                for g in range(ngrp):
                    emit_wt(g)
                    if g + 1 < ngrp:
                        emit_hidden(g + 1)
                    if g + 2 < ngrp:
                        load_u(g + 2)
                    for k in range(8):
                        dcgs = [2 * k, 2 * k + 1]
                        emit_out(g, dcgs)
                        if g + 1 < ngrp:
                            emit_gm(g + 1, k // 2, k % 2)
                        emit_add(g, dcgs)
                    if g + 2 < ngrp:
                        load_v(g + 2)
                S.barrier()
            with ExitStack() as s4:
                wsl = [cx.sb(s4, "wsl", [128, NCH, 512], BF16) for _ in range(2)]
                sgate = cx.sb(s4, "sgate", [128, NCH, 512], F32)
                pt = cx.sb(s4, "pt", [128, 2, 512], BF16)
                t2 = [cx.sb(s4, "t2", [128, 512], F32) for _ in range(2)]
                S.dma('pool', pt[:], pT.rearrange("(c p) t -> p c t", p=128)[:, :, tsl], w=['pt'], stream='pt')
                rmsnorm_tile(cx, hacc, 'hacc', gcol, 1, lambda c: xnb[:, c, :], 'xnb', 512, c32, tmpn)

                def ev_g(c, b, bk):
                    S.op('act', lambda e: e.activation(out=sgate[:, c, :], in_=bk[:], func=AF.Sigmoid),
                         r=[f'bank{b}'], w=['sgate'])

                linear_T(cx, wv(w_pg), NCH, D, lambda dc: xnb[:, dc, :], ['xnb'], wsl, ev_g)

                def ev_p(c, b, bk):
                    t = t2[c % 2]
                    S.op('dve', lambda e: e.tensor_tensor(out=t[:], in0=bk[:], in1=sgate[:, c, :], op=ALU.mult),
                         r=[f'bank{b}', 'sgate'], w=[f't2{c % 2}'])
                    S.op('dve', lambda e: e.tensor_tensor(out=hacc[:, c, :], in0=t[:], in1=hacc[:, c, :], op=ALU.add),
                         r=[f't2{c % 2}', 'hacc'], w=['hacc'])

                linear_T(cx, wv(w_ple), 2, D, lambda dc: pt[:, dc, :], ['pt'], wsl, ev_p)
                S.dma('sp', oview[:, :, tsl], hacc[:], r=['hacc'], stream='hout')
                if fin_out is not None:
                    rmsnorm_tile(cx, hacc, 'hacc', gcol, 2, lambda c: sgate[:, c, :], 'sgate', 512, c32, tmpn)
                    S.dma('sp', fview[:, :, tsl], sgate[:], r=['sgate'], stream='fout')
                S.barrier()


def build_post(nt=NT, nexp=NEXP):
    nc = bass.Bass("TRN2", target_bir_lowering=False)
    di = lambda nm, shp, dt=F32: nc.dram_tensor(nm, shp, dt, kind="ExternalInput").ap()
    hT = di("hT", [D, nt]); oT = di("oT", [2, 1024, nt], BF16); sg = di("sg", [2, D, nt], BF16)
    w_bsb = di("w_bsb", [1024, D]); w_bfx = di("w_bfx", [1024, D]); w_out = di("w_out", [D, D])
    w_query = di("w_query", [D, D]); skT = di("skT", [16, 128, 128]); uT = di("uT", [D, nexp]); ev = di("ev", [nexp, D])
    w_pg = di("w_pg", [D, D]); w_ple = di("w_ple", [256, D]); pT = di("pT", [256, nt]); gpc = di("gpc", [128, 48])
    cst = di("cst", [128, 768])
    hT_out = nc.dram_tensor("hT_out", [D, nt], F32, kind="ExternalOutput").ap()
    fin_out = nc.dram_tensor("fin_out", [D, nt], F32, kind="ExternalOutput").ap()
    with ExitStack() as stack:
        cx = Ctx(nc, stack)
        phase_post(cx, hT, oT, sg, w_bsb, w_bfx, w_out, w_query, skT, uT, ev, w_pg, w_ple, pT, gpc, cst, hT_out,
                   fin_out, nt, nexp)
        cx.S.finish()
    return nc


DEPTH = 2


def build_fused(depth=DEPTH):
    nc = bass.Bass("TRN2", target_bir_lowering=False)
    di = lambda nm, shp, dt=F32: nc.dram_tensor(nm, shp, dt, kind="ExternalInput").ap()
    sc = lambda nm, shp, dt: nc.dram_tensor(nm, shp, dt, kind="Internal").ap()
    xT = di("xT", [D, SEQ]); pT = di("pT", [depth - 1, 256, SEQ]); pTl = di("pTl", [256, NT])
    msel_in = di("msel", [128, 2])
    gmix = di("gmix", [depth, 128, NCH]); g3 = di("g3", [depth, 128, 48]); bfg = di("bfg", [depth, 8, 1])
    w_in = di("w_in", [depth, D, IN_COLS]); w_bsb = di("w_bsb", [depth, 1024, D]); w_bfx = di("w_bfx", [depth, 1024, D])
    w_out = di("w_out", [depth, D, D]); w_query = di("w_query", [depth, D, D]); skT = di("skT", [depth, 16, 128, 128])
    uT = di("uT", [depth, D, NEXP]); ev = di("ev", [depth, NEXP, D]); w_pg = di("w_pg", [depth, D, D])
    w_ple = di("w_ple", [depth, 256, D]); cst = di("cst", [128, 1792])
    outT = nc.dram_tensor("outT", [D, NT], F32, kind="ExternalOutput").ap()
    qk = sc("s_qk", [32, 128, SEQ], BF16); v = sc("s_v", [2, SEQ, 1024], BF16); logf = sc("s_logf", [8, SEQ], F32)
    sg = sc("s_sg", [2, D, SEQ], BF16); oT = sc("s_oT", [2, 1024, SEQ], BF16)
    hbuf = [sc("s_hA", [D, SEQ], F32), sc("s_hB", [D, SEQ], F32)]
    cst1 = cst[:, 0:768]
    with ExitStack() as stack:
        cx = Ctx(nc, stack)
        cur = xT
        for i in range(depth):
            nxt = hbuf[i % 2]
            last = i == depth - 1
            for hf in range(SEQ // NT):
                hs = slice(hf * NT, (hf + 1) * NT)
                phase_proj(cx, cur[:, hs], gmix[i], w_in[i], bfg[i], cst1, qk[:, :, hs], v[:, hs, :], logf[:, hs],
                           sg[:, :, hs], NT)
            phase_attn(cx, qk[0:8], qk[8:16], v[0], qk[16:24], qk[24:32], v[1], logf, cst, oT, 8, SEQ)
            if last:
                h0, h1 = slice(0, NT), slice(NT, 2 * NT)
                phase_post(cx, cur[:, h0], oT[:, :, h0], sg[:, :, h0], w_bsb[i], w_bfx[i], w_out[i], w_query[i], skT[i],
                           uT[i], ev[i], w_pg[i], w_ple[i], pTl, g3[i], cst1, nxt[:, h0], outT, NT, NEXP,
                           sel={'m': msel_in, 'hT1': cur[:, h1], 'oT1': oT[:, :, h1], 'sg1': sg[:, :, h1]})
            else:
                for hf in range(SEQ // NT):
                    hs = slice(hf * NT, (hf + 1) * NT)
                    phase_post(cx, cur[:, hs], oT[:, :, hs], sg[:, :, hs], w_bsb[i], w_bfx[i], w_out[i], w_query[i],
                               skT[i], uT[i], ev[i], w_pg[i], w_ple[i], pT[i][:, hs], g3[i], cst1, nxt[:, hs],
                               None, NT, NEXP)
            cur = nxt
        cx.S.finish()
    return nc


_PROGS = {}


def _pc(g):
    return np.ascontiguousarray(np.asarray(g, np.float32).reshape(NCH, 128).T)


def kernel(x, p, norm_mix_g, w_in, b_forget, w_branch_sb, w_branch_fox, w_out, norm_ffn_g, w_query, sub_keys,
           expert_u, expert_v, norm_ple_g, w_ple, w_ple_gate, final_norm_g):
    f32 = lambda a: np.ascontiguousarray(np.asarray(a, np.float32))
    x = f32(x); p = f32(p)
    depth = w_in.shape[0]
    if "fused" not in _PROGS:
        _PROGS["fused"] = build_fused(depth)
    nc = _PROGS["fused"]
    cores = list(range(NCORES))
    shared = {
        "gmix": np.stack([_pc(norm_mix_g[i]) for i in range(depth)]),
        "g3": np.stack([np.concatenate([_pc(norm_ffn_g[i]), _pc(norm_ple_g[i]), _pc(final_norm_g)], axis=1)
                        for i in range(depth)]),
        "bfg": f32(b_forget).reshape(depth, 8, 1),
        "w_in": f32(w_in), "w_bsb": f32(w_branch_sb), "w_bfx": f32(w_branch_fox), "w_out": f32(w_out),
        "w_query": f32(w_query),
        "skT": np.ascontiguousarray(f32(sub_keys).reshape(depth, 16, 128, 128).transpose(0, 1, 3, 2)),
        "uT": np.ascontiguousarray(f32(expert_u).transpose(0, 2, 1)), "ev": f32(expert_v),
        "w_pg": f32(w_ple_gate), "w_ple": f32(w_ple), "cst": make_consts2(),
    }
    maps = []
    for c in cores:
        b, g = c % BATCH, c // BATCH
        m = dict(shared)
        m["xT"] = np.ascontiguousarray(x[b].T)
        m["pT"] = np.ascontiguousarray(p[:depth - 1, b].transpose(0, 2, 1))
        m["pTl"] = np.ascontiguousarray(p[depth - 1, b, g * NT:(g + 1) * NT].T)
        ms = np.zeros((128, 2), np.float32)
        ms[:, g] = 1.0
        m["msel"] = ms
        maps.append(m)
    res = run_bass_kernel_spmd(nc, maps, core_ids=cores).results
    out = np.empty((BATCH, SEQ, D), np.float32)
    for c in cores:
        b, g = c % BATCH, c // BATCH
        out[b, g * NT:(g + 1) * NT] = res[c]["outT"].T
    return out
```

```python
from contextlib import ExitStack
import numpy as np
import ml_dtypes
import concourse.bass as bass
import concourse.mybir as mybir
from concourse.bass_utils import run_bass_kernel_spmd

F32 = mybir.dt.float32
BF16 = mybir.dt.bfloat16
AF = mybir.ActivationFunctionType
ALU = mybir.AluOpType

D = 2048
NCH = 16
SEQ = 4096
BATCH = 4
HD = 128
NCORES = 8
NT = 2048
IN_COLS = 10248
SCALE = HD ** -0.5
EPS = 1e-6
NEXP = 16384


class Sync:
    def __init__(self, nc, stack):
        self.nc = nc
        self.stack = stack
        self.eng = {'pe': nc.tensor, 'dve': nc.vector, 'act': nc.scalar, 'pool': nc.gpsimd, 'sp': nc.sync}
        self.prod = {}
        self.waited = {e: {} for e in self.eng}
        self.lastw = {}
        self.readers = {}
        self.nsem = 0
        self.sems = {}

    def _newsem(self):
        self.nsem += 1
        s = self.stack.enter_context(self.nc.semaphore(f"sy{self.nsem}"))
        self.sems[self.nsem] = s
        return self.nsem

    def _inc(self, pname, n):
        p = self.prod.get(pname)
        if p is None or p[1] + n > 30000:
            p = [self._newsem(), 0]
            self.prod[pname] = p
        p[1] += n
        return p[0], p[1]

    def _deps(self, r, w):
        deps = []
        for k in r:
            if k in self.lastw:
                deps.append(self.lastw[k])
        for k in w:
            if k in self.lastw:
                deps.append(self.lastw[k])
            rd = self.readers.get(k)
            if rd:
                deps.extend(rd.values())
        return deps

    def _wait(self, e, deps):
        best = {}
        for (sid, val, pn) in deps:
            if pn == 'pe' and e == 'pe':
                continue
            if self.waited[e].get(sid, 0) >= val:
                continue
            if sid not in best or best[sid] < val:
                best[sid] = val
        for sid, val in best.items():
            self.eng[e].wait_ge(self.sems[sid], val)
            self.waited[e][sid] = val

    def _commit(self, tok, r, w):
        for k in w:
            self.lastw[k] = tok
            self.readers[k] = {}
        for k in r:
            d = self.readers.setdefault(k, {})
            d[(tok[0], tok[2])] = tok

    def op(self, e, fn, r=(), w=()):
        self._wait(e, self._deps(r, w))
        ins = fn(self.eng[e])
        sid, val = self._inc(e, 1)
        ins.then_inc(self.sems[sid], 1)
        self._commit((sid, val, e), r, w)

    def dma(self, q, out, in_, r=(), w=(), stream=None, **kw):
        pname = 'dma:' + stream
        deps = self._deps(r, w)
        p = self.prod.get(pname)
        if p is not None:
            deps.append((p[0], p[1], pname))
        self._wait(q, deps)
        ins = self.eng[q].dma_start(out=out, in_=in_, **kw)
        sid, val = self._inc(pname, 16)
        ins.then_inc(self.sems[sid], 16)
        self._commit((sid, val, pname), r, w)

    def barrier(self):
        for e in ('sp', 'pool', 'act', 'dve', 'pe'):
            deps = [(p[0], p[1], pn) for pn, p in self.prod.items() if pn != e]
            self._wait(e, deps)

    def finish(self):
        for e in ('sp', 'pool', 'act', 'dve', 'pe'):
            deps = [(p[0], p[1], pn) for pn, p in self.prod.items() if pn != e]
            self._wait(e, deps)


class Ctx:
    def __init__(self, nc, stack):
        self.nc = nc
        self.S = Sync(nc, stack)
        self.stack = stack
        self.banks = [stack.enter_context(nc.psum_tensor(f"bank{i}", [128, 512], F32)) for i in range(8)]
        self.bi = 0
        self.uid = 0

    def bank(self):
        b = self.bi
        self.bi = (self.bi + 1) % 8
        return b

    def sb(self, stack, name, shape, dt):
        self.uid += 1
        return stack.enter_context(self.nc.sbuf_tensor(f"{name}_{self.uid}", shape, dt))


def load_consts(cx, stack, cst):
    S = cx.S
    c32 = cx.sb(stack, "c32", [128, 6 * 128], F32)
    c16 = cx.sb(stack, "c16", [128, 6 * 128], BF16)
    S.dma('sp', c32[:], cst, w=['c32'], stream='c32')
    S.op('dve', lambda e: e.tensor_copy(out=c16[:], in_=c32[:]), r=['c32'], w=['c16'])
    return c32, c16


def make_consts():
    p = np.arange(128)[:, None]
    i = np.arange(128)[None, :]
    c = np.concatenate([
        (p == i), (p < i), (p <= i), -1.0 * (p > i), -1.0 * (p <= i), np.ones((128, 128))
    ], axis=1).astype(np.float32)
    return np.ascontiguousarray(c)


def rmsnorm_tile(cx, hT_t, hkey, gcol, gi, out_ap_fn, okey, n, c32, tmp):
    S = cx.S
    sq, rstd, epsc = tmp
    b = cx.bank()
    bk = cx.banks[b]
    for c in range(NCH):
        s = sq[c % 2]
        S.op('act', lambda e, s=s, c=c: e.activation(out=s[:, :n], in_=hT_t[:, c, :n], func=AF.Square),
             r=[hkey], w=[f'sq{c % 2}'])
        S.op('pe', lambda e, s=s, c=c: e.matmul(bk[:, :n], lhsT=c32[:, 640:768], rhs=s[:, :n],
                                                 start=(c == 0), stop=(c == NCH - 1)),
             r=[f'sq{c % 2}', 'c32'], w=[f'bank{b}'])
    S.op('act', lambda e: e.activation(out=rstd[:, :n], in_=bk[:, :n], func=AF.Ln, bias=epsc[:, 0:1], scale=1.0 / D),
         r=[f'bank{b}', 'epsc'], w=['rstd'])
    S.op('act', lambda e: e.activation(out=rstd[:, :n], in_=rstd[:, :n], func=AF.Exp, scale=-0.5),
         r=['rstd'], w=['rstd'])
    for c in range(NCH):
        S.op('dve', lambda e, c=c: e.scalar_tensor_tensor(out=out_ap_fn(c), in0=hT_t[:, c, :n],
                                                          scalar=gcol[:, gi * NCH + c:gi * NCH + c + 1],
                                                          in1=rstd[:, :n], op0=ALU.mult, op1=ALU.mult),
             r=[hkey, 'rstd', 'gcol'], w=[okey])


def norm_tmp(cx, stack):
    sq = [cx.sb(stack, "sq", [128, 512], F32) for _ in range(2)]
    rstd = cx.sb(stack, "rstd", [128, 512], F32)
    epsc = cx.sb(stack, "epsc", [128, 1], F32)
    cx.S.op('dve', lambda e: e.memset(epsc[:], EPS), w=['epsc'])
    return sq, rstd, epsc


def phase_proj(cx, hT, gpc, w_in, bfg, cst, qk_out, v_out, logf_out, sg_out, nt=NT):
    S = cx.S
    nc = cx.nc
    ntile = nt // 512
    with ExitStack() as st:
        c32, c16 = load_consts(cx, st, cst)
        xnT = cx.sb(st, "xnT", [128, NCH, nt], BF16)
        gcol = cx.sb(st, "gcol", [128, NCH], F32)
        bcol = cx.sb(st, "bcol", [8, 1], F32)
        S.dma('sp', gcol[:], gpc, w=['gcol'], stream='gcol')
        S.dma('sp', bcol[:], bfg, w=['bcol'], stream='bcol')
        tmp = norm_tmp(cx, st)
        hview = hT.rearrange("(c p) t -> p c t", p=128)
        with ExitStack() as st2:
            hts = [cx.sb(st2, "ht", [128, NCH, 512], F32) for _ in range(2)]
            for tt in range(ntile):
                ht = hts[tt % 2]
                hk = f'ht{tt % 2}'
                S.dma('sp', ht[:], hview[:, :, tt * 512:(tt + 1) * 512], w=[hk], stream=hk)
                rmsnorm_tile(cx, ht, hk, gcol, 0, lambda c, tt=tt: xnT[:, c, tt * 512:(tt + 1) * 512],
                             'xnT', 512, c32, tmp)
        S.barrier()
        wview = w_in.rearrange("(c p) n -> p c n", p=128)
        wsl = [cx.sb(st, "wsl", [128, NCH, 512], BF16) for _ in range(2)]
        stg = [cx.sb(st, "stg", [128, 512], BF16) for _ in range(4)]
        stgf = cx.sb(st, "stgf", [8, 512], F32)
        wf = cx.sb(st, "wf", [128, NCH, 8], BF16)
        slabs = []
        for kind, base in enumerate([0, 1024, 3072, 4096]):
            for s2 in range(2):
                slabs.append((base + 512 * s2, 'qk', kind * 8 + 4 * s2))
        for kind, base in enumerate([2048, 5120]):
            for s2 in range(2):
                slabs.append((base + 512 * s2, 'v', (kind, s2)))
        for kind, base in enumerate([6152, 8200]):
            for s4 in range(4):
                slabs.append((base + 512 * s4, 'g', (kind, s4)))
        si = 0
        ev = 0

        def load_slab(i):
            col0 = slabs[i][0]
            S.dma('pool', wsl[i % 2][:], wview[:, :, col0:col0 + 512], w=[f'wsl{i % 2}'], stream=f'wsl{i % 2}')

        load_slab(0)
        S.dma('pool', wf[:], wview[:, :, 6144:6152], w=['wf'], stream='wf')
        for i, (col0, mode, meta) in enumerate(slabs):
            if i + 1 < len(slabs):
                load_slab(i + 1)
            w = wsl[i % 2]
            wk = f'wsl{i % 2}'
            if mode in ('qk', 'g'):
                for tt in range(ntile):
                    for cc in range(4):
                        b = cx.bank()
                        bk = cx.banks[b]
                        for dc in range(NCH):
                            S.op('pe', lambda e, dc=dc, cc=cc, tt=tt, bk=bk: e.matmul(
                                bk[:, :], lhsT=w[:, dc, cc * 128:(cc + 1) * 128],
                                rhs=xnT[:, dc, tt * 512:(tt + 1) * 512], start=(dc == 0), stop=(dc == NCH - 1)),
                                r=[wk, 'xnT'], w=[f'bank{b}'])
                        sg = stg[ev % 4]
                        sk = f'stg{ev % 4}'
                        if mode == 'qk':
                            if ev % 2 == 0:
                                S.op('act', lambda e, sg=sg, bk=bk: e.activation(out=sg[:], in_=bk[:], func=AF.Copy),
                                     r=[f'bank{b}'], w=[sk])
                            else:
                                S.op('dve', lambda e, sg=sg, bk=bk: e.tensor_copy(out=sg[:], in_=bk[:]),
                                     r=[f'bank{b}'], w=[sk])
                            dst = qk_out[meta + cc, :, tt * 512:(tt + 1) * 512]
                        else:
                            S.op('act', lambda e, sg=sg, bk=bk: e.activation(out=sg[:], in_=bk[:], func=AF.Sigmoid),
                                 r=[f'bank{b}'], w=[sk])
                            kind, s4 = meta
                            r0 = (s4 * 4 + cc) * 128
                            dst = sg_out[kind, r0:r0 + 128, tt * 512:(tt + 1) * 512]
                        S.dma('sp', dst, sg[:], r=[sk], w=[], stream=sk + 'o')
                        ev += 1
            else:
                kind, s2 = meta
                for stt in range(nt // 128):
                    b = cx.bank()
                    bk = cx.banks[b]
                    for dc in range(NCH):
                        S.op('pe', lambda e, dc=dc, stt=stt, bk=bk: e.matmul(
                            bk[:, :], lhsT=xnT[:, dc, stt * 128:(stt + 1) * 128], rhs=w[:, dc, :],
                            start=(dc == 0), stop=(dc == NCH - 1)), r=[wk, 'xnT'], w=[f'bank{b}'])
                    sg = stg[ev % 4]
                    sk = f'stg{ev % 4}'
                    if ev % 2 == 0:
                        S.op('act', lambda e, sg=sg, bk=bk: e.activation(out=sg[:], in_=bk[:], func=AF.Copy),
                             r=[f'bank{b}'], w=[sk])
                    else:
                        S.op('dve', lambda e, sg=sg, bk=bk: e.tensor_copy(out=sg[:], in_=bk[:]),
                             r=[f'bank{b}'], w=[sk])
                    S.dma('sp', v_out[kind, stt * 128:(stt + 1) * 128, s2 * 512:(s2 + 1) * 512], sg[:],
                          r=[sk], w=[], stream=sk + 'o')
                    ev += 1
        for tt in range(ntile):
            b = cx.bank()
            bk = cx.banks[b]
            for dc in range(NCH):
                S.op('pe', lambda e, dc=dc, tt=tt, bk=bk: e.matmul(
                    bk[0:8, :], lhsT=wf[:, dc, :], rhs=xnT[:, dc, tt * 512:(tt + 1) * 512],
                    start=(dc == 0), stop=(dc == NCH - 1)), r=['wf', 'xnT'], w=[f'bank{b}'])
            S.op('act', lambda e, bk=bk: e.activation(out=stgf[:], in_=bk[0:8, :], func=AF.Sigmoid,
                                                      bias=bcol[:, 0:1], scale=1.0),
                 r=[f'bank{b}', 'bcol'], w=['stgf'])
            S.op('act', lambda e: e.activation(out=stgf[:], in_=stgf[:], func=AF.Ln), r=['stgf'], w=['stgf'])
            S.dma('sp', logf_out[:, tt * 512:(tt + 1) * 512], stgf[:], r=['stgf'], w=[], stream='stgfo')
        S.barrier()


def build_proj(nt=NT):
    nc = bass.Bass("TRN2", target_bir_lowering=False)
    hT = nc.dram_tensor("hT", [D, nt], F32, kind="ExternalInput").ap()
    gpc = nc.dram_tensor("gpc", [128, NCH], F32, kind="ExternalInput").ap()
    w_in = nc.dram_tensor("w_in", [D, IN_COLS], F32, kind="ExternalInput").ap()
    bfg = nc.dram_tensor("bfg", [8, 1], F32, kind="ExternalInput").ap()
    cst = nc.dram_tensor("cst", [128, 768], F32, kind="ExternalInput").ap()
    qk = nc.dram_tensor("qk", [32, 128, nt], BF16, kind="ExternalOutput").ap()
    v = nc.dram_tensor("v", [2, nt, 1024], BF16, kind="ExternalOutput").ap()
    logf = nc.dram_tensor("logf", [8, nt], F32, kind="ExternalOutput").ap()
    sg = nc.dram_tensor("sg", [2, D, nt], BF16, kind="ExternalOutput").ap()
    with ExitStack() as stack:
        cx = Ctx(nc, stack)
        phase_proj(cx, hT, gpc, w_in, bfg, cst, qk, v, logf, sg, nt)
        cx.S.finish()
    return nc


def make_consts2():
    c = make_consts()
    sel = np.zeros((128, 8 * 128), np.float32)
    for h in range(8):
        sel[h, h * 128:(h + 1) * 128] = 1.0
    return np.ascontiguousarray(np.concatenate([c, sel], axis=1))


def load_consts2(cx, stack, cst):
    S = cx.S
    c32 = cx.sb(stack, "c32", [128, 1792], F32)
    c16 = cx.sb(stack, "c16", [128, 768], BF16)
    S.dma('sp', c32[:], cst, w=['c32'], stream='c32')
    S.op('dve', lambda e: e.tensor_copy(out=c16[:], in_=c32[:, 0:768]), r=['c32'], w=['c16'])
    return c32, c16


def attn_load_head(cx, st, ci, qT, kT, v, h, seq):
    S = cx.S
    nblk = seq // 128
    d = {}
    d['k'] = cx.sb(st, 'k', [128, seq], BF16)
    d['q'] = cx.sb(st, 'q', [128, seq], BF16)
    d['v'] = cx.sb(st, 'v', [128, nblk, 128], BF16)
    return d


def attn_issue_loads(cx, d, ci, qT, kT, v, h):
    S = cx.S
    S.dma('sp', d['k'][:], kT[h], w=[f'k{ci}'], stream=f'k{ci}')
    S.dma('sp', d['q'][:], qT[h], w=[f'q{ci}'], stream=f'q{ci}')
    S.dma('sp', d['v'][:], v[:, h * 128:(h + 1) * 128].rearrange("(b p) d -> p b d", p=128), w=[f'v{ci}'],
          stream=f'v{ci}')


def phase_attn(cx, qT_sb, kT_sb, v_sb, qT_fx, kT_fx, v_fx, logf, cst, oT, nh, seq):
    S = cx.S
    nblk = seq // 128
    nqc = seq // 512
    nchain = 2 if nh >= 2 else 1
    with ExitStack() as st:
        c32, c16 = load_consts2(cx, st, cst)
        ident32 = c32[:, 0:128]
        strict32 = c32[:, 128:256]
        strict16 = c16[:, 128:256]
        incl16 = c16[:, 256:384]
        negTri = c32[:, 384:512]
        negTriC = c32[:, 512:640]
        ones16 = c16[:, 640:768]
        onec = cx.sb(st, "onec", [128, 1], F32)
        S.op('dve', lambda e: e.memset(onec[:], 1.0), w=['onec'])
        chains = []
        for ci in range(nchain):
            d = attn_load_head(cx, st, ci, None, None, None, 0, seq)
            for nm in ('e', 'sp', 'tmp', 'ea'):
                d[nm] = [cx.sb(st, nm, [128, 512], F32) for _ in range(2)]
            d['P'] = [cx.sb(st, 'P', [128, 512], BF16) for _ in range(2)]
            d['bias'] = [cx.sb(st, 'bias', [128, 4], F32) for _ in range(2)]
            d['og'] = cx.sb(st, 'og', [128, 512], BF16)
            d['rd'] = cx.sb(st, 'rd', [128, 512], F32)
            d['Zs'], d['A'], d['O'] = (3 * ci, 6 + ci), 3 * ci + 1, 3 * ci + 2
            chains.append(d)

        Fsb = cx.sb(st, "Fsb", [nh, seq], F32)
        negF = cx.sb(st, "negF", [128, nblk * nh], F32)
        Fref = cx.sb(st, "Fref", [128, nh * nblk], F32)
        spl = [cx.sb(st, "spl", [nh, seq], BF16) for _ in range(3)]
        kaug = [cx.sb(st, "kaug", [6, seq], BF16) for _ in range(nchain)]
        qaug = [cx.sb(st, "qaug", [6, seq], BF16) for _ in range(nchain)]
        with ExitStack() as st2:
            lf = cx.sb(st2, "lf", [nh, seq], F32)
            onesr = cx.sb(st2, "onesr", [nh, seq], F32)
            S.dma('sp', lf[:], logf, w=['lf'], stream='lf')
            S.op('dve', lambda e: e.memset(onesr[:], 1.0), w=['onesr'])
            S.op('dve', lambda e: e.tensor_tensor_scan(out=Fsb[:], data0=onesr[:], data1=lf[:], initial=0.0,
                                                       op0=ALU.mult, op1=ALU.add), r=['lf', 'onesr'], w=['Fsb'])
            b6, b7 = cx.banks[6], cx.banks[7]
            for blk in range(nblk):
                S.op('pe', lambda e, blk=blk: e.matmul(b6[:, blk * nh:(blk + 1) * nh],
                                                       lhsT=Fsb[0:nh, blk * 128:(blk + 1) * 128],
                                                       rhs=c32[0:nh, 0:nh], start=True, stop=True),
                     r=['Fsb', 'c32'], w=['bank6'])
            S.op('dve', lambda e: e.tensor_scalar(out=negF[:], in0=b6[:, 0:nblk * nh], scalar1=-1.0, scalar2=None,
                                                  op0=ALU.mult), r=['bank6'], w=['negF'])
            for h in range(nh):
                S.op('pe', lambda e, h=h: e.matmul(b7[:, h * nblk:(h + 1) * nblk],
                                                   lhsT=c32[0:nh, 768 + h * 128:768 + (h + 1) * 128],
                                                   rhs=Fsb[0:nh, 64:seq:128], start=True, stop=True),
                     r=['Fsb', 'c32'], w=['bank7'])
            S.op('dve', lambda e: e.tensor_copy(out=Fref[:], in_=b7[:, 0:nh * nblk]), r=['bank7'], w=['Fref'])
            S.op('dve', lambda e: e.tensor_scalar(out=lf[:], in0=Fsb[:], scalar1=float(HD ** 0.5), scalar2=None,
                                                  op0=ALU.mult), r=['Fsb', 'lf'], w=['lf'])
            S.op('dve', lambda e: e.tensor_copy(out=spl[0][:], in_=lf[:]), r=['lf'], w=['spl0'])
            S.op('dve', lambda e: e.tensor_tensor(out=onesr[:], in0=lf[:], in1=spl[0][:], op=ALU.subtract),
                 r=['lf', 'spl0'], w=['onesr'])
            S.op('dve', lambda e: e.tensor_copy(out=spl[1][:], in_=onesr[:]), r=['onesr'], w=['spl1'])
            S.op('dve', lambda e: e.tensor_tensor(out=onesr[:], in0=onesr[:], in1=spl[1][:], op=ALU.subtract),
                 r=['onesr', 'spl1'], w=['onesr'])
            S.op('dve', lambda e: e.tensor_copy(out=spl[2][:], in_=onesr[:]), r=['onesr'], w=['spl2'])
            S.barrier()

        def run_group(kind, heads):
            qT, kT, v = (qT_sb, kT_sb, v_sb) if kind == 0 else (qT_fx, kT_fx, v_fx)
            act = list(enumerate(heads))
            for ci, h in act:
                attn_issue_loads(cx, chains[ci], ci, qT, kT, v, h)
            if kind == 1:
                for ci, h in act:
                    S.op('dve', lambda e, ci=ci: e.memset(kaug[ci][:], 1.0), w=[f'kaug{ci}'])
                    S.op('dve', lambda e, ci=ci: e.memset(qaug[ci][:], -1.0), w=[f'qaug{ci}'])
                    for r3 in range(3):
                        S.dma('sp', kaug[ci][3 + r3:4 + r3, :], spl[r3][h:h + 1, :], r=[f'spl{r3}'], w=[f'kaug{ci}'],
                              stream=f'ka{ci}_{r3}')
                        S.dma('sp', qaug[ci][r3:r3 + 1, :], spl[r3][h:h + 1, :], r=[f'spl{r3}'], w=[f'qaug{ci}'],
                              stream=f'qa{ci}_{r3}')
            tiles = [(qc, ti, kb) for qc in range(nqc) for ti, kb in enumerate(range(4 * qc + 3, -1, -1))]

            def emit_qk(idx):
                qc_, ti_, kb_ = tiles[idx]
                c0_ = 128 * (kb_ - 4 * qc_) if kb_ >= 4 * qc_ else 0
                t0_ = qc_ * 512
                for ci, h in act:
                    d = chains[ci]
                    zb = d['Zs'][idx % 2]
                    Z = cx.banks[zb]
                    S.op('pe', lambda e, d=d, Z=Z: e.matmul(Z[:, c0_:512], lhsT=d['k'][:, kb_ * 128:(kb_ + 1) * 128],
                                                             rhs=d['q'][:, t0_ + c0_:t0_ + 512], start=True,
                                                             stop=(kind == 0)),
                         r=[f'k{ci}', f'q{ci}'], w=[f"bank{zb}"])
                    if kind == 1:
                        S.op('pe', lambda e, ci=ci, Z=Z: e.matmul(Z[:, c0_:512],
                                                                   lhsT=kaug[ci][:, kb_ * 128:(kb_ + 1) * 128],
                                                                   rhs=qaug[ci][:, t0_ + c0_:t0_ + 512], start=False,
                                                                   stop=True),
                             r=[f'kaug{ci}', f'qaug{ci}'], w=[f"bank{zb}"])

            emit_qk(0)
            tix = 0
            for qc in range(nqc):
                blocks = list(range(4 * qc + 3, -1, -1))
                for ti, kb in enumerate(blocks):
                    par = tix % 2
                    tix += 1
                    for d_ in chains:
                        d_['Z'] = d_['Zs'][par]
                    diag = kb >= 4 * qc
                    j = kb - 4 * qc if diag else 0
                    c0 = 128 * j
                    first = ti == 0
                    last = kb == 0
                    t0 = qc * 512
                    if tix < len(tiles):
                        emit_qk(tix)
                    if kind == 0:
                        for ci, h in act:
                            d = chains[ci]
                            Z = cx.banks[d['Z']]
                            e_, sp_ = d['e'][par], d['sp'][par]
                            S.op('act', lambda e, Z=Z, e_=e_: e.activation(out=e_[:, c0:512], in_=Z[:, c0:512],
                                                                           func=AF.Exp, scale=SCALE),
                                 r=[f"bank{d['Z']}"], w=[f'e{ci}_{par}'])
                            S.op('act', lambda e, e_=e_, sp_=sp_: e.activation(out=sp_[:, c0:512], in_=e_[:, c0:512],
                                                                               func=AF.Ln, bias=onec[:, 0:1], scale=1.0),
                                 r=[f'e{ci}_{par}', 'onec'], w=[f'sp{ci}_{par}'])
                            if diag:
                                S.op('pool', lambda e, sp_=sp_: e.tensor_tensor(out=sp_[:, c0:c0 + 128],
                                                                                in0=sp_[:, c0:c0 + 128], in1=strict32,
                                                                                op=ALU.mult),
                                     r=[f'sp{ci}_{par}', 'c32'], w=[f'sp{ci}_{par}'])
                        for ci, h in act:
                            d = chains[ci]
                            A = cx.banks[d['A']]
                            sp_ = d['sp'][par]
                            S.op('pe', lambda e, A=A, sp_=sp_: e.matmul(A[:, c0:512], lhsT=negTri, rhs=sp_[:, c0:512],
                                                                         start=first, stop=True),
                                 r=[f'sp{ci}_{par}', 'c32'], w=[f"bank{d['A']}"])
                        for ci, h in act:
                            d = chains[ci]
                            A = cx.banks[d['A']]
                            sp_, tmp_ = d['sp'][par], d['tmp'][par]
                            S.op('dve', lambda e, A=A, sp_=sp_, tmp_=tmp_: e.tensor_tensor(
                                out=tmp_[:, c0:512], in0=A[:, c0:512], in1=sp_[:, c0:512], op=ALU.subtract),
                                r=[f"bank{d['A']}", f'sp{ci}_{par}'], w=[f'tmp{ci}_{par}'])
                        for ci, h in act:
                            d = chains[ci]
                            tmp_, ea_ = d['tmp'][par], d['ea'][par]
                            S.op('act', lambda e, tmp_=tmp_, ea_=ea_: e.activation(out=ea_[:, c0:512],
                                                                                   in_=tmp_[:, c0:512], func=AF.Exp),
                                 r=[f'tmp{ci}_{par}'], w=[f'ea{ci}_{par}'])
                        for ci, h in act:
                            d = chains[ci]
                            e_, ea_, P_ = d['e'][par], d['ea'][par], d['P'][par]
                            S.op('dve', lambda e, e_=e_, ea_=ea_, P_=P_: e.tensor_tensor(
                                out=P_[:, c0:512], in0=e_[:, c0:512], in1=ea_[:, c0:512], op=ALU.mult),
                                r=[f'e{ci}_{par}', f'ea{ci}_{par}'], w=[f'P{ci}_{par}'])
                            if diag:
                                S.op('pool', lambda e, P_=P_: e.tensor_tensor(out=P_[:, c0:c0 + 128],
                                                                              in0=P_[:, c0:c0 + 128], in1=strict16,
                                                                              op=ALU.mult),
                                     r=[f'P{ci}_{par}', 'c16'], w=[f'P{ci}_{par}'])
                        for ci, h in act:
                            d = chains[ci]
                            O = cx.banks[d['O']]
                            A = cx.banks[d['A']]
                            P_, sp_ = d['P'][par], d['sp'][par]
                            S.op('pe', lambda e, O=O, P_=P_, d=d: e.matmul(O[:, c0:512], lhsT=d['v'][:, kb, :],
                                                                            rhs=P_[:, c0:512], start=first, stop=last),
                                 r=[f'P{ci}_{par}', f'v{ci}'], w=[f"bank{d['O']}"])
                            if not last:
                                S.op('pe', lambda e, A=A, sp_=sp_: e.matmul(A[:, c0:512], lhsT=negTriC,
                                                                             rhs=sp_[:, c0:512], start=False, stop=True),
                                     r=[f'sp{ci}_{par}', 'c32', f'tmp{ci}_{par}'], w=[f"bank{d['A']}"])
                    else:
                        for ci, h in act:
                            d = chains[ci]
                            Z = cx.banks[d['Z']]
                            bs, P_ = d['bias'][par], d['P'][par]
                            S.op('act', lambda e, Z=Z, P_=P_: e.activation(
                                out=P_[:, c0:512], in_=Z[:, c0:512], func=AF.Exp, scale=SCALE),
                                r=[f"bank{d['Z']}"], w=[f'P{ci}_{par}'])
                            if diag:
                                S.op('pool', lambda e, P_=P_: e.tensor_tensor(out=P_[:, c0:c0 + 128],
                                                                              in0=P_[:, c0:c0 + 128], in1=incl16,
                                                                              op=ALU.mult),
                                     r=[f'P{ci}_{par}', 'c16'], w=[f'P{ci}_{par}'])
                        for ci, h in act:
                            d = chains[ci]
                            O = cx.banks[d['O']]
                            A = cx.banks[d['A']]
                            P_ = d['P'][par]
                            S.op('pe', lambda e, O=O, P_=P_, d=d: e.matmul(O[:, c0:512], lhsT=d['v'][:, kb, :],
                                                                            rhs=P_[:, c0:512], start=first, stop=last),
                                 r=[f'P{ci}_{par}', f'v{ci}'], w=[f"bank{d['O']}"])
                            S.op('pe', lambda e, A=A, P_=P_: e.matmul(A[:, c0:512], lhsT=ones16, rhs=P_[:, c0:512],
                                                                       start=first, stop=last),
                                 r=[f'P{ci}_{par}', 'c16'], w=[f"bank{d['A']}"])
                for ci, h in act:
                    d = chains[ci]
                    O = cx.banks[d['O']]
                    A = cx.banks[d['A']]
                    if kind == 0:
                        S.op('act', lambda e, O=O, d=d: e.activation(out=d['og'][:], in_=O[:], func=AF.Copy),
                             r=[f"bank{d['O']}"], w=[f'og{ci}'])
                    else:
                        S.op('dve', lambda e, A=A, d=d: e.reciprocal(out=d['rd'][:], in_=A[:]),
                             r=[f"bank{d['A']}"], w=[f'rd{ci}'])
                        S.op('dve', lambda e, O=O, d=d: e.tensor_tensor(out=d['og'][:], in0=O[:], in1=d['rd'][:],
                                                                        op=ALU.mult),
                             r=[f"bank{d['O']}", f'rd{ci}'], w=[f'og{ci}'])
                    S.dma('sp', oT[kind, h * 128:(h + 1) * 128, qc * 512:(qc + 1) * 512], d['og'][:],
                          r=[f'og{ci}'], stream=f'og{ci}o')

        for kind in (0, 1):
            for h0 in range(0, nh, nchain):
                run_group(kind, list(range(h0, min(nh, h0 + nchain))))
        S.barrier()


def build_attn(nh=4, seq=SEQ):
    nc = bass.Bass("TRN2", target_bir_lowering=False)
    aps = {}
    for nm in ("qT_sb", "kT_sb", "qT_fx", "kT_fx"):
        aps[nm] = nc.dram_tensor(nm, [nh, 128, seq], BF16, kind="ExternalInput").ap()
    for nm in ("v_sb", "v_fx"):
        aps[nm] = nc.dram_tensor(nm, [seq, nh * 128], BF16, kind="ExternalInput").ap()
    logf = nc.dram_tensor("logf", [nh, seq], F32, kind="ExternalInput").ap()
    cst = nc.dram_tensor("cst", [128, 1792], F32, kind="ExternalInput").ap()
    oT = nc.dram_tensor("oT", [2, nh * 128, seq], BF16, kind="ExternalOutput").ap()
    with ExitStack() as stack:
        cx = Ctx(nc, stack)
        phase_attn(cx, aps["qT_sb"], aps["kT_sb"], aps["v_sb"], aps["qT_fx"], aps["kT_fx"], aps["v_fx"], logf, cst,
                   oT, nh, seq)
        cx.S.finish()
    return nc


def linear_T(cx, wview, kc, ncols, rhs_fn, rkeys, wsl, evac, n=512, pre=None):
    S = cx.S
    nsl = ncols // 512

    def load(i):
        S.dma('pool', wsl[i % 2][:, 0:kc, :], wview[:, :, i * 512:(i + 1) * 512],
              w=[f'wsl{i % 2}'], stream=f'wsl{i % 2}')

    load(0)
    for i in range(nsl):
        if i + 1 < nsl:
            load(i + 1)
        if pre is not None:
            pre(i)
        w = wsl[i % 2]
        for cc in range(4):
            b = cx.bank2()
            bk = cx.banks[b]
            for dc in range(kc):
                S.op('pe', lambda e, dc=dc, cc=cc, bk=bk: e.matmul(bk[:, :n], lhsT=w[:, dc, cc * 128:(cc + 1) * 128],
                                                                    rhs=rhs_fn(dc), start=(dc == 0), stop=(dc == kc - 1)),
                     r=[f'wsl{i % 2}'] + rkeys, w=[f'bank{b}'])
            evac(i * 4 + cc, b, bk)


def phase_post(cx, hT, oT, sg, w_bsb, w_bfx, w_out, w_query, skT, uT, ev, w_pg, w_ple, pT, gpc, cst, hT_out,
               fin_out, nt=NT, nexp=NEXP, sel=None):
    S = cx.S
    ntile = nt // 512
    nich = nexp // 128
    assert nich == 128
    GRP = 4
    ngrp = nich // GRP
    cx.bank2 = lambda: 4 + (cx.bank() % 4)
    with ExitStack() as st:
        c32, c16 = load_consts(cx, st, cst)
        ident16 = c16[:, 0:128]
        gcol = cx.sb(st, "gcol", [128, 48], F32)
        S.dma('sp', gcol[:], gpc, w=['gcol'], stream='gcol')
        tmpn = norm_tmp(cx, st)
        msel = None
        if sel is not None:
            msel = cx.sb(st, "msel", [128, 2], F32)
            S.dma('sp', msel[:], sel['m'], w=['msel'], stream='msel')
        hacc = cx.sb(st, "hacc", [128, NCH, 512], F32)
        xnb = cx.sb(st, "xnb", [128, NCH, 512], BF16)
        E = [cx.sb(st, "E", [128, 16, 128], F32) for _ in range(4)]
        thr = [cx.sb(st, "thr", [128, 8], F32) for _ in range(4)]
        Dg = [cx.sb(st, "Dg", [128, 8, 128], BF16) for _ in range(4)]
        hview = hT.rearrange("(c p) t -> p c t", p=128)
        oview = hT_out.rearrange("(c p) t -> p c t", p=128)
        fview = fin_out.rearrange("(c p) t -> p c t", p=128) if fin_out is not None else None
        wv = lambda w: w.rearrange("(c p) n -> p c n", p=128)
        for tt in range(ntile):
            tsl = slice(tt * 512, (tt + 1) * 512)
            S.dma('sp', hacc[:], hview[:, :, tsl], w=['hacc'], stream='hacc')
            if sel is not None:
                h1view = sel['hT1'].rearrange("(c p) t -> p c t", p=128)
                for k4 in range(4):
                    S.dma('sp', E[k4][:].rearrange("p a k -> p (a k)").rearrange("p (c t) -> p c t", c=4),
                          h1view[:, 4 * k4:4 * k4 + 4, tsl], w=[f'E{k4}'], stream=f'E{k4}')
                S.op('dve', lambda e: e.tensor_scalar(out=hacc[:].rearrange("p c t -> p (c t)"),
                                                      in0=hacc[:].rearrange("p c t -> p (c t)"),
                                                      scalar1=msel[:, 0:1], scalar2=None, op0=ALU.mult),
                     r=['hacc', 'msel'], w=['hacc'])
                for k4 in range(4):
                    hv = hacc[:, 4 * k4:4 * k4 + 4, :].rearrange("p c t -> p (c t)")
                    S.op('dve', lambda e, k4=k4, hv=hv: e.scalar_tensor_tensor(
                        out=hv, in0=E[k4][:].rearrange("p a k -> p (a k)"), scalar=msel[:, 1:2], in1=hv,
                        op0=ALU.mult, op1=ALU.add), r=[f'E{k4}', 'hacc', 'msel'], w=['hacc'])
            with ExitStack() as s2:
                wsl = [cx.sb(s2, "wsl", [128, NCH, 512], BF16) for _ in range(2)]
                mrg = cx.sb(s2, "mrg", [128, NCH, 512], BF16)
                ot = cx.sb(s2, "ot", [128, 16, 512], BF16)
                sgs = cx.sb(s2, "sgs", [128, 2, 4, 512], BF16)
                t2 = cx.sb(s2, "t2", [128, 512], F32)
                skb = cx.sb(s2, "skb", [128, 16, 128], BF16)
                top = cx.sb(s2, "top", [128, 16, 16], F32)
                negm = cx.sb(s2, "negm", [128, 16], F32)
                d16 = cx.sb(s2, "d16", [128, 16], F32)
                work = cx.sb(s2, "work", [128, 256], F32)
                cand = cx.sb(s2, "cand", [128, 8, 256], F32)
                ctop = cx.sb(s2, "ctop", [128, 8, 24], F32)
                sm = cx.sb(s2, "sm", [128, 8, 4], F32)
                ez = cx.sb(s2, "ez", [128, 8, 16], F32)
                S.dma('pool', skb[:], skT.rearrange("a c k -> c a k"), w=['skb'], stream='skb')
                S.dma('sp', ot[:], oT.rearrange("k (c p) t -> p (k c) t", p=128)[:, :, tsl], w=['ot'], stream='ot')
                sgv = sg.rearrange("k (c p) t -> p k c t", p=128)
                if sel is not None:
                    sgs2 = cx.sb(s2, "sgs2", [128, 2, 4, 512], BF16)
                    sgv1 = sel['sg1'].rearrange("k (c p) t -> p k c t", p=128)
                    S.dma('sp', mrg[:], sel['oT1'].rearrange("k (c p) t -> p (k c) t", p=128)[:, :, tsl], w=['mrg'],
                          stream='mrgl')
                    of, mf = ot[:].rearrange("p c t -> p (c t)"), mrg[:].rearrange("p c t -> p (c t)")
                    S.op('dve', lambda e: e.tensor_scalar(out=of, in0=of, scalar1=msel[:, 0:1], scalar2=None,
                                                          op0=ALU.mult), r=['ot', 'msel'], w=['ot'])
                    S.op('dve', lambda e: e.scalar_tensor_tensor(out=of, in0=mf, scalar=msel[:, 1:2], in1=of,
                                                                 op0=ALU.mult, op1=ALU.add),
                         r=['mrg', 'ot', 'msel'], w=['ot'])
                bsv, bfv = wv(w_bsb), wv(w_bfx)

                def load_b(i):
                    S.dma('pool', wsl[i % 2][:, 0:8, :], bsv[:, :, i * 512:(i + 1) * 512], w=[f'wsl{i % 2}'],
                          stream=f'wsl{i % 2}')
                    S.dma('pool', wsl[i % 2][:, 8:16, :], bfv[:, :, i * 512:(i + 1) * 512], w=[f'wsl{i % 2}'],
                          stream=f'wsl{i % 2}b')

                load_b(0)
                for i in range(4):
                    if i + 1 < 4:
                        load_b(i + 1)
                    S.dma('sp', sgs[:, 0], sgv[:, 0, 4 * i:4 * i + 4, tsl], w=['sgs'], stream='sgs')
                    S.dma('sp', sgs[:, 1], sgv[:, 1, 4 * i:4 * i + 4, tsl], w=['sgs'], stream='sgsb')
                    if sel is not None:
                        S.dma('sp', sgs2[:, 0], sgv1[:, 0, 4 * i:4 * i + 4, tsl], w=['sgs2'], stream='sgs2')
                        S.dma('sp', sgs2[:, 1], sgv1[:, 1, 4 * i:4 * i + 4, tsl], w=['sgs2'], stream='sgs2b')
                        sf = sgs[:].rearrange("p k c t -> p (k c t)")
                        sf2 = sgs2[:].rearrange("p k c t -> p (k c t)")
                        S.op('dve', lambda e, sf=sf: e.tensor_scalar(out=sf, in0=sf, scalar1=msel[:, 0:1], scalar2=None,
                                                                     op0=ALU.mult), r=['sgs', 'msel'], w=['sgs'])
                        S.op('dve', lambda e, sf=sf, sf2=sf2: e.scalar_tensor_tensor(
                            out=sf, in0=sf2, scalar=msel[:, 1:2], in1=sf, op0=ALU.mult, op1=ALU.add),
                            r=['sgs2', 'sgs', 'msel'], w=['sgs'])
                    w = wsl[i % 2]
                    for cc in range(4):
                        c = 4 * i + cc
                        b1, b2 = cx.bank2(), cx.bank2()
                        k1, k2 = cx.banks[b1], cx.banks[b2]
                        for dc in range(8):
                            S.op('pe', lambda e, dc=dc, cc=cc, k1=k1: e.matmul(
                                k1[:], lhsT=w[:, dc, cc * 128:(cc + 1) * 128], rhs=ot[:, dc, :],
                                start=(dc == 0), stop=(dc == 7)), r=[f'wsl{i % 2}', 'ot'], w=[f'bank{b1}'])
                        for dc in range(8):
                            S.op('pe', lambda e, dc=dc, cc=cc, k2=k2: e.matmul(
                                k2[:], lhsT=w[:, 8 + dc, cc * 128:(cc + 1) * 128], rhs=ot[:, 8 + dc, :],
                                start=(dc == 0), stop=(dc == 7)), r=[f'wsl{i % 2}', 'ot'], w=[f'bank{b2}'])
                        S.op('dve', lambda e, k1=k1, cc=cc: e.tensor_tensor(out=t2[:], in0=k1[:], in1=sgs[:, 0, cc, :],
                                                                            op=ALU.mult),
                             r=[f'bank{b1}', 'sgs'], w=['t2'])
                        S.op('dve', lambda e, k2=k2, cc=cc, c=c: e.tensor_tensor(out=mrg[:, c, :], in0=k2[:],
                                                                                 in1=sgs[:, 1, cc, :], op=ALU.mult),
                             r=[f'bank{b2}', 'sgs'], w=['mrg'])
                        S.op('dve', lambda e, c=c: e.tensor_tensor(out=mrg[:, c, :], in0=mrg[:, c, :], in1=t2[:],
                                                                   op=ALU.add), r=['mrg', 't2'], w=['mrg'])

                def ev_out(c, b, bk):
                    S.op('dve', lambda e: e.tensor_tensor(out=hacc[:, c, :], in0=bk[:], in1=hacc[:, c, :], op=ALU.add),
                         r=[f'bank{b}', 'hacc'], w=['hacc'])

                linear_T(cx, wv(w_out), NCH, D, lambda dc: mrg[:, dc, :], ['mrg'], wsl, ev_out)
                rmsnorm_tile(cx, hacc, 'hacc', gcol, 0, lambda c: xnb[:, c, :], 'xnb', 512, c32, tmpn)
                qpT = mrg

                def ev_q(c, b, bk):
                    if c % 2 == 0:
                        S.op('act', lambda e: e.activation(out=qpT[:, c, :], in_=bk[:], func=AF.Copy),
                             r=[f'bank{b}'], w=['mrg'])
                    else:
                        S.op('dve', lambda e: e.tensor_copy(out=qpT[:, c, :], in_=bk[:]), r=[f'bank{b}'], w=['mrg'])

                linear_T(cx, wv(w_query), NCH, D, lambda dc: xnb[:, dc, :], ['xnb'], wsl, ev_q)
                for stt in range(4):
                    sbanks = []
                    for g4 in range(4):
                        b = cx.bank2()
                        bk = cx.banks[b]
                        sbanks.append((b, bk))
                        for q4 in range(4):
                            hp = g4 * 4 + q4
                            S.op('pe', lambda e, hp=hp, q4=q4, bk=bk: e.matmul(
                                bk[:, q4 * 128:(q4 + 1) * 128], lhsT=qpT[:, hp, stt * 128:(stt + 1) * 128],
                                rhs=skb[:, hp, :], start=True, stop=True), r=['mrg', 'skb'], w=[f'bank{b}'])
                    for hp in range(16):
                        b, bk = sbanks[hp // 4]
                        sc = bk[:, (hp % 4) * 128:(hp % 4 + 1) * 128]
                        S.op('dve', lambda e, sc=sc, hp=hp: e.max(out=top[:, hp, 0:8], in_=sc), r=[f'bank{b}'], w=['top'])
                        S.op('dve', lambda e, sc=sc, hp=hp: e.match_replace(out=work[:, 0:128], in_to_replace=top[:, hp, 0:8],
                                                                            in_values=sc, imm_value=-1e30),
                             r=[f'bank{b}', 'top'], w=['work'])
                        S.op('dve', lambda e, hp=hp: e.max(out=top[:, hp, 8:16], in_=work[:, 0:128]), r=['work'], w=['top'])
                    S.op('dve', lambda e: e.tensor_scalar(out=negm[:], in0=top[:, :, 0], scalar1=-1.0, scalar2=None,
                                                          op0=ALU.mult), r=['top'], w=['negm'])
                    S.op('dve', lambda e: e.tensor_tensor(out=d16[:], in0=top[:, :, 15], in1=negm[:], op=ALU.add),
                         r=['top', 'negm'], w=['d16'])
                    S.op('dve', lambda e: e.tensor_scalar(out=d16[:], in0=d16[:], scalar1=-2e-6, scalar2=None,
                                                          op0=ALU.add), r=['d16'], w=['d16'])
                    S.op('act', lambda e: e.activation(out=d16[:], in_=d16[:], func=AF.Exp), r=['d16'], w=['d16'])
                    for hp in range(16):
                        b, bk = sbanks[hp // 4]
                        sc = bk[:, (hp % 4) * 128:(hp % 4 + 1) * 128]
                        S.op('act', lambda e, sc=sc, hp=hp: e.activation(out=E[stt][:, hp, :], in_=sc, func=AF.Exp,
                                                                         bias=negm[:, hp:hp + 1], scale=1.0),
                             r=[f'bank{b}', 'negm'], w=[f'E{stt}'])
                        S.op('dve', lambda e, hp=hp: e.scalar_tensor_tensor(
                            out=E[stt][:, hp, :], in0=E[stt][:, hp, :], scalar=d16[:, hp:hp + 1], in1=E[stt][:, hp, :],
                            op0=ALU.is_ge, op1=ALU.mult), r=[f'E{stt}', 'd16'], w=[f'E{stt}'])
                    tv = top[:].rearrange("p (h two) a -> p h two a", two=2)
                    S.op('dve', lambda e: e.tensor_tensor(
                        out=cand[:].rearrange("p h (a b) -> p h a b", a=16),
                        in0=tv[:, :, 0, :].unsqueeze(3).to_broadcast([128, 8, 16, 16]),
                        in1=tv[:, :, 1, :].unsqueeze(2).to_broadcast([128, 8, 16, 16]), op=ALU.add),
                        r=['top'], w=['cand'])
                    for h in range(8):
                        S.op('dve', lambda e, h=h: e.max(out=ctop[:, h, 0:8], in_=cand[:, h, :]), r=['cand'], w=['ctop'])
                        S.op('dve', lambda e, h=h: e.match_replace(out=work[:], in_to_replace=ctop[:, h, 0:8],
                                                                   in_values=cand[:, h, :], imm_value=-1e30),
                             r=['cand', 'ctop'], w=['work'])
                        S.op('dve', lambda e, h=h: e.max(out=ctop[:, h, 8:16], in_=work[:]), r=['work'], w=['ctop'])
                        S.op('dve', lambda e, h=h: e.match_replace(out=work[:], in_to_replace=ctop[:, h, 8:16],
                                                                   in_values=work[:], imm_value=-1e30),
                             r=['work', 'ctop'], w=['work'])
                        S.op('dve', lambda e, h=h: e.max(out=ctop[:, h, 16:24], in_=work[:]), r=['work'], w=['ctop'])
                    S.op('dve', lambda e: e.tensor_scalar(out=sm[:, :, 0], in0=ctop[:, :, 0], scalar1=-1.0, scalar2=None,
                                                          op0=ALU.mult), r=['ctop'], w=['sm'])
                    S.op('dve', lambda e: e.tensor_tensor(out=sm[:, :, 3], in0=ctop[:, :, 15], in1=ctop[:, :, 16],
                                                          op=ALU.add), r=['ctop'], w=['sm'])
                    S.op('dve', lambda e: e.scalar_tensor_tensor(out=sm[:, :, 3], in0=sm[:, :, 3], scalar=0.5,
                                                                 in1=sm[:, :, 0], op0=ALU.mult, op1=ALU.add),
                         r=['sm'], w=['sm'])
                    for h in range(8):
                        S.op('act', lambda e, h=h: e.activation(out=ez[:, h, :], in_=ctop[:, h, 0:16], func=AF.Exp,
                                                                bias=sm[:, h, 0:1], scale=1.0),
                             r=['ctop', 'sm'], w=['ez'])
                    S.op('dve', lambda e: e.tensor_reduce(out=sm[:, :, 1], in_=ez[:], axis=mybir.AxisListType.X,
                                                          op=ALU.add), r=['ez'], w=['sm'])
                    S.op('dve', lambda e: e.reciprocal(out=sm[:, :, 2], in_=sm[:, :, 1]), r=['sm'], w=['sm'])
                    S.op('act', lambda e: e.activation(out=thr[stt][:], in_=sm[:, :, 3], func=AF.Exp), r=['sm'],
                         w=[f'thr{stt}'])
                    S.op('dve', lambda e: e.tensor_tensor(out=thr[stt][:], in0=thr[stt][:], in1=sm[:, :, 2], op=ALU.mult),
                         r=[f'thr{stt}', 'sm'], w=[f'thr{stt}'])
                    S.op('act', lambda e: e.activation(out=sm[:, :, 1], in_=sm[:, :, 3], func=AF.Exp, scale=-1.0),
                         r=['sm'], w=['sm'])
                    Ev = E[stt][:].rearrange("p (h two) k -> p h two k", two=2)
                    S.op('dve', lambda e, Ev=Ev: e.tensor_tensor(out=Ev[:, :, 0, :], in0=Ev[:, :, 0, :],
                                                                 in1=sm[:, :, 1].unsqueeze(2).to_broadcast([128, 8, 128]),
                                                                 op=ALU.mult), r=[f'E{stt}', 'sm'], w=[f'E{stt}'])
                    for h in range(8):
                        S.op('dve', lambda e, h=h: e.tensor_scalar(out=Dg[stt][:, h, :], in0=c32[:, 0:128],
                                                                   scalar1=thr[stt][:, h:h + 1], scalar2=None,
                                                                   op0=ALU.mult), r=['c32', f'thr{stt}'], w=[f'Dg{stt}'])
                S.barrier()
            with ExitStack() as s3:
                usl = [cx.sb(s3, "usl", [128, NCH, GRP * 128], BF16) for _ in range(2)]
                vsl = [cx.sb(s3, "vsl", [128, GRP, D], BF16) for _ in range(2)]
                Gm = [[cx.sb(s3, "Gm", [128, 4 * GRP * 128], BF16) for _ in range(2)] for _ in range(2)]
                Pt = [cx.sb(s3, "Pt", [128, 4 * GRP * 128], F32) for _ in range(1)]
                oev = [cx.sb(s3, "oev", [128, 512], F32) for _ in range(2)]
                gl = [cx.sb(s3, "gl", [128, 512], F32) for _ in range(GRP)]
                WT = [cx.sb(s3, "WT", [128, 512], BF16) for _ in range(GRP)]
                uview = uT.rearrange("(c p) e -> p c e", p=128)
                vview = ev.rearrange("(g c p) d -> g p c d", p=128, c=GRP)
                st8 = {'pc': 0, 'hb': 0, 'ob': 0}
                obank = {}

                def load_u(g):
                    S.dma('pool', usl[g % 2][:], uview[:, :, g * GRP * 128:(g + 1) * GRP * 128], w=[f'usl{g % 2}'],
                          stream=f'usl{g % 2}')

                def load_v(g):
                    S.dma('pool', vsl[g % 2][:], vview[g], w=[f'vsl{g % 2}'], stream=f'vsl{g % 2}')

                def emit_hidden(g):
                    u = usl[g % 2]
                    for ic in range(GRP):
                        hb = 4 + st8['hb'] % 2
                        st8['hb'] += 1
                        hbk = cx.banks[hb]
                        for dc in range(NCH):
                            S.op('pe', lambda e, dc=dc, ic=ic, hbk=hbk: e.matmul(
                                hbk[:], lhsT=u[:, dc, ic * 128:(ic + 1) * 128], rhs=xnb[:, dc, :],
                                start=(dc == 0), stop=(dc == NCH - 1)), r=[f'usl{g % 2}', 'xnb'], w=[f'bank{hb}'])
                        S.op('act', lambda e, ic=ic, hbk=hbk: e.activation(out=gl[ic][:], in_=hbk[:], func=AF.Gelu),
                             r=[f'bank{hb}'], w=[f'gl{ic}'])

                def emit_gm(g, stt, half):
                    i0 = g * GRP
                    par = stt % 2
                    P = Pt[0]
                    pk = "Pt0"
                    st8['pc'] += 1
                    Ev = E[stt][:].rearrange("p (h two) k -> p h two k", two=2)
                    gk = f'Gm{par}_{half}'
                    S.op('dve', lambda e, P=P: e.tensor_tensor(
                        out=P[:].rearrange("p (h a k) -> p h a k", h=4, a=GRP),
                        in0=Ev[:, 4 * half:4 * half + 4, 0, i0:i0 + GRP].unsqueeze(3).to_broadcast([128, 4, GRP, 128]),
                        in1=Ev[:, 4 * half:4 * half + 4, 1, :].unsqueeze(2).to_broadcast([128, 4, GRP, 128]),
                        op=ALU.mult), r=[f'E{stt}'], w=[pk])
                    S.op('dve', lambda e, P=P: e.scalar_tensor_tensor(
                        out=Gm[par][half][:], in0=P[:], scalar=1.0, in1=P[:], op0=ALU.is_ge, op1=ALU.mult),
                        r=[pk], w=[gk])
                    if half == 1:
                        for ic in range(GRP):
                            bk = cx.banks[ic]
                            for h in range(8):
                                S.op('pe', lambda e, bk=bk, h=h, ic=ic: e.matmul(
                                    bk[:, stt * 128:(stt + 1) * 128],
                                    lhsT=Gm[par][h // 4][:, ((h % 4) * GRP + ic) * 128:((h % 4) * GRP + ic + 1) * 128],
                                    rhs=Dg[stt][:, h, :], start=(h == 0), stop=(h == 7)),
                                    r=[f'Gm{par}_{h // 4}', f'Dg{stt}'], w=[f'bank{ic}'])

                def emit_wt(g):
                    for ic in range(GRP):
                        S.op('dve', lambda e, ic=ic: e.tensor_tensor(out=WT[ic][:], in0=cx.banks[ic][:], in1=gl[ic][:],
                                                                     op=ALU.mult),
                             r=[f'bank{ic}', f'gl{ic}'], w=[f'WT{ic}'])

                def emit_out(g, dcgs):
                    v = vsl[g % 2]
                    for dcg in dcgs:
                        ob = 6 + st8['ob'] % 2
                        st8['ob'] += 1
                        obank[dcg] = ob
                        obk = cx.banks[ob]
                        for ic in range(GRP):
                            S.op('pe', lambda e, ic=ic, dcg=dcg, obk=obk: e.matmul(
                                obk[:], lhsT=v[:, ic, dcg * 128:(dcg + 1) * 128], rhs=WT[ic][:],
                                start=(ic == 0), stop=(ic == GRP - 1)),
                                r=[f'vsl{g % 2}', f'WT{ic}'], w=[f'bank{ob}'])

                def emit_add(g, dcgs):
                    for dcg in dcgs:
                        ob = obank[dcg]
                        obk = cx.banks[ob]
                        oe = oev[dcg % 2]
                        ok = f'oev{dcg % 2}'
                        S.op('act', lambda e, obk=obk, oe=oe: e.activation(out=oe[:], in_=obk[:], func=AF.Copy),
                             r=[f'bank{ob}'], w=[ok])
                        S.op('pool', lambda e, dcg=dcg, oe=oe: e.tensor_tensor(out=hacc[:, dcg, :], in0=oe[:],
                                                                               in1=hacc[:, dcg, :], op=ALU.add),
                             r=[ok, f'hacc{dcg}'], w=[f'hacc{dcg}'])

                load_u(0)
                load_v(0)
                load_u(1)
                load_v(1)
                emit_hidden(0)
                for stt in range(4):
                    emit_gm(0, stt, 0)
                    emit_gm(0, stt, 1)
                for g in range(ngrp):
                    emit_wt(g)
                    if g + 1 < ngrp:
                        emit_hidden(g + 1)
                    if g + 2 < ngrp:
                        load_u(g + 2)
                    for k in range(8):
                        dcgs = [2 * k, 2 * k + 1]
                        emit_out(g, dcgs)
                        if g + 1 < ngrp:
                            emit_gm(g + 1, k // 2, k % 2)
                        emit_add(g, dcgs)
                    if g + 2 < ngrp:
                        load_v(g + 2)
                S.barrier()
            with ExitStack() as s4:
                wsl = [cx.sb(s4, "wsl", [128, NCH, 512], BF16) for _ in range(2)]
                sgate = cx.sb(s4, "sgate", [128, NCH, 512], F32)
                pt = cx.sb(s4, "pt", [128, 2, 512], BF16)
                t2 = [cx.sb(s4, "t2", [128, 512], F32) for _ in range(2)]
                S.dma('pool', pt[:], pT.rearrange("(c p) t -> p c t", p=128)[:, :, tsl], w=['pt'], stream='pt')
                rmsnorm_tile(cx, hacc, 'hacc', gcol, 1, lambda c: xnb[:, c, :], 'xnb', 512, c32, tmpn)

                def ev_g(c, b, bk):
                    S.op('act', lambda e: e.activation(out=sgate[:, c, :], in_=bk[:], func=AF.Sigmoid),
                         r=[f'bank{b}'], w=['sgate'])

                linear_T(cx, wv(w_pg), NCH, D, lambda dc: xnb[:, dc, :], ['xnb'], wsl, ev_g)

                def ev_p(c, b, bk):
                    t = t2[c % 2]
                    S.op('dve', lambda e: e.tensor_tensor(out=t[:], in0=bk[:], in1=sgate[:, c, :], op=ALU.mult),
                         r=[f'bank{b}', 'sgate'], w=[f't2{c % 2}'])
                    S.op('dve', lambda e: e.tensor_tensor(out=hacc[:, c, :], in0=t[:], in1=hacc[:, c, :], op=ALU.add),
                         r=[f't2{c % 2}', 'hacc'], w=['hacc'])

                linear_T(cx, wv(w_ple), 2, D, lambda dc: pt[:, dc, :], ['pt'], wsl, ev_p)
                S.dma('sp', oview[:, :, tsl], hacc[:], r=['hacc'], stream='hout')
                if fin_out is not None:
                    rmsnorm_tile(cx, hacc, 'hacc', gcol, 2, lambda c: sgate[:, c, :], 'sgate', 512, c32, tmpn)
                    S.dma('sp', fview[:, :, tsl], sgate[:], r=['sgate'], stream='fout')
                S.barrier()


def build_post(nt=NT, nexp=NEXP):
    nc = bass.Bass("TRN2", target_bir_lowering=False)
    di = lambda nm, shp, dt=F32: nc.dram_tensor(nm, shp, dt, kind="ExternalInput").ap()
    hT = di("hT", [D, nt]); oT = di("oT", [2, 1024, nt], BF16); sg = di("sg", [2, D, nt], BF16)
    w_bsb = di("w_bsb", [1024, D]); w_bfx = di("w_bfx", [1024, D]); w_out = di("w_out", [D, D])
    w_query = di("w_query", [D, D]); skT = di("skT", [16, 128, 128]); uT = di("uT", [D, nexp]); ev = di("ev", [nexp, D])
    w_pg = di("w_pg", [D, D]); w_ple = di("w_ple", [256, D]); pT = di("pT", [256, nt]); gpc = di("gpc", [128, 48])
    cst = di("cst", [128, 768])
    hT_out = nc.dram_tensor("hT_out", [D, nt], F32, kind="ExternalOutput").ap()
    fin_out = nc.dram_tensor("fin_out", [D, nt], F32, kind="ExternalOutput").ap()
    with ExitStack() as stack:
        cx = Ctx(nc, stack)
        phase_post(cx, hT, oT, sg, w_bsb, w_bfx, w_out, w_query, skT, uT, ev, w_pg, w_ple, pT, gpc, cst, hT_out,
                   fin_out, nt, nexp)
        cx.S.finish()
    return nc


DEPTH = 2


def build_fused(depth=DEPTH):
    nc = bass.Bass("TRN2", target_bir_lowering=False)
    di = lambda nm, shp, dt=F32: nc.dram_tensor(nm, shp, dt, kind="ExternalInput").ap()
    sc = lambda nm, shp, dt: nc.dram_tensor(nm, shp, dt, kind="Internal").ap()
    xT = di("xT", [D, SEQ]); pT = di("pT", [depth - 1, 256, SEQ]); pTl = di("pTl", [256, NT])
    msel_in = di("msel", [128, 2])
    gmix = di("gmix", [depth, 128, NCH]); g3 = di("g3", [depth, 128, 48]); bfg = di("bfg", [depth, 8, 1])
    w_in = di("w_in", [depth, D, IN_COLS]); w_bsb = di("w_bsb", [depth, 1024, D]); w_bfx = di("w_bfx", [depth, 1024, D])
    w_out = di("w_out", [depth, D, D]); w_query = di("w_query", [depth, D, D]); skT = di("skT", [depth, 16, 128, 128])
    uT = di("uT", [depth, D, NEXP]); ev = di("ev", [depth, NEXP, D]); w_pg = di("w_pg", [depth, D, D])
    w_ple = di("w_ple", [depth, 256, D]); cst = di("cst", [128, 1792])
    outT = nc.dram_tensor("outT", [D, NT], F32, kind="ExternalOutput").ap()
    qk = sc("s_qk", [32, 128, SEQ], BF16); v = sc("s_v", [2, SEQ, 1024], BF16); logf = sc("s_logf", [8, SEQ], F32)
    sg = sc("s_sg", [2, D, SEQ], BF16); oT = sc("s_oT", [2, 1024, SEQ], BF16)
    hbuf = [sc("s_hA", [D, SEQ], F32), sc("s_hB", [D, SEQ], F32)]
    cst1 = cst[:, 0:768]
    with ExitStack() as stack:
        cx = Ctx(nc, stack)
        cur = xT
        for i in range(depth):
            nxt = hbuf[i % 2]
            last = i == depth - 1
            for hf in range(SEQ // NT):
                hs = slice(hf * NT, (hf + 1) * NT)
                phase_proj(cx, cur[:, hs], gmix[i], w_in[i], bfg[i], cst1, qk[:, :, hs], v[:, hs, :], logf[:, hs],
                           sg[:, :, hs], NT)
            phase_attn(cx, qk[0:8], qk[8:16], v[0], qk[16:24], qk[24:32], v[1], logf, cst, oT, 8, SEQ)
            if last:
                h0, h1 = slice(0, NT), slice(NT, 2 * NT)
                phase_post(cx, cur[:, h0], oT[:, :, h0], sg[:, :, h0], w_bsb[i], w_bfx[i], w_out[i], w_query[i], skT[i],
                           uT[i], ev[i], w_pg[i], w_ple[i], pTl, g3[i], cst1, nxt[:, h0], outT, NT, NEXP,
                           sel={'m': msel_in, 'hT1': cur[:, h1], 'oT1': oT[:, :, h1], 'sg1': sg[:, :, h1]})
            else:
                for hf in range(SEQ // NT):
                    hs = slice(hf * NT, (hf + 1) * NT)
                    phase_post(cx, cur[:, hs], oT[:, :, hs], sg[:, :, hs], w_bsb[i], w_bfx[i], w_out[i], w_query[i],
                               skT[i], uT[i], ev[i], w_pg[i], w_ple[i], pT[i][:, hs], g3[i], cst1, nxt[:, hs],
                               None, NT, NEXP)
            cur = nxt
        cx.S.finish()
    return nc


_PROGS = {}


def _pc(g):
    return np.ascontiguousarray(np.asarray(g, np.float32).reshape(NCH, 128).T)


def kernel(x, p, norm_mix_g, w_in, b_forget, w_branch_sb, w_branch_fox, w_out, norm_ffn_g, w_query, sub_keys,
           expert_u, expert_v, norm_ple_g, w_ple, w_ple_gate, final_norm_g):
    f32 = lambda a: np.ascontiguousarray(np.asarray(a, np.float32))
    x = f32(x); p = f32(p)
    depth = w_in.shape[0]
    if "fused" not in _PROGS:
        _PROGS["fused"] = build_fused(depth)
    nc = _PROGS["fused"]
    cores = list(range(NCORES))
    shared = {
        "gmix": np.stack([_pc(norm_mix_g[i]) for i in range(depth)]),
        "g3": np.stack([np.concatenate([_pc(norm_ffn_g[i]), _pc(norm_ple_g[i]), _pc(final_norm_g)], axis=1)
                        for i in range(depth)]),
        "bfg": f32(b_forget).reshape(depth, 8, 1),
        "w_in": f32(w_in), "w_bsb": f32(w_branch_sb), "w_bfx": f32(w_branch_fox), "w_out": f32(w_out),
        "w_query": f32(w_query),
        "skT": np.ascontiguousarray(f32(sub_keys).reshape(depth, 16, 128, 128).transpose(0, 1, 3, 2)),
        "uT": np.ascontiguousarray(f32(expert_u).transpose(0, 2, 1)), "ev": f32(expert_v),
        "w_pg": f32(w_ple_gate), "w_ple": f32(w_ple), "cst": make_consts2(),
    }
    maps = []
    for c in cores:
        b, g = c % BATCH, c // BATCH
        m = dict(shared)
        m["xT"] = np.ascontiguousarray(x[b].T)
        m["pT"] = np.ascontiguousarray(p[:depth - 1, b].transpose(0, 2, 1))
        m["pTl"] = np.ascontiguousarray(p[depth - 1, b, g * NT:(g + 1) * NT].T)
        ms = np.zeros((128, 2), np.float32)
        ms[:, g] = 1.0
        m["msel"] = ms
        maps.append(m)
    res = run_bass_kernel_spmd(nc, maps, core_ids=cores).results
    out = np.empty((BATCH, SEQ, D), np.float32)
    for c in cores:
        b, g = c % BATCH, c // BATCH
        out[b, g * NT:(g + 1) * NT] = res[c]["outT"].T
    return out
```
